# Optimizing a Trainium2 kernel written in Bass

```python
import math
import jax, jax.numpy as jnp
from jax import lax
import numpy as np

D_MODEL = 2048
BATCH = 4
SEQ = 2048
DEPTH = 4
DEC_BATCH = 128
DEC_SEQ = 4
PAST_LEN = 16384
PAGE_SIZE = 128

N_MIXERS = 2
CONV_WIDTH = 31
CONV_STATE = CONV_WIDTH - 1
GROUP_SIZE = 16
N_GROUPS = D_MODEL // GROUP_SIZE
STATE_DIM = 64
D_FF = ((8 * D_MODEL // 3 + 255) // 256) * 256
FFN_CONV_WIDTH = 3
FFN_STATE = FFN_CONV_WIDTH - 1
N_CONV_LAYERS = (DEPTH + 1) // 2
N_SSM_LAYERS = DEPTH // 2
EPS = 1e-6

kernel_name = "hybrid_conformer_conv_s5_convffn_step"


def rms_norm(x, g):
    xf = x.astype(jnp.float32)
    y = xf * lax.rsqrt(jnp.mean(xf * xf, axis=-1, keepdims=True) + EPS)
    return (y * g.astype(jnp.float32)).astype(x.dtype)


def layer_norm(x, g, b):
    xf = x.astype(jnp.float32)
    mu = jnp.mean(xf, axis=-1, keepdims=True)
    var = jnp.mean(jnp.square(xf - mu), axis=-1, keepdims=True)
    y = (xf - mu) * lax.rsqrt(var + EPS)
    return (y * g.astype(jnp.float32) + b.astype(jnp.float32)).astype(x.dtype)


def causal_depthwise(ext, w):
    c = ext.shape[-1]
    return lax.conv_general_dilated(ext, w[:, None, :].astype(ext.dtype), window_strides=(1,), padding='VALID',
                                    dimension_numbers=('NWC', 'WIO', 'NWC'), feature_group_count=c)


def conformer_conv(h, cache, w_in, dw, dw_b, ln_g, ln_b, w_out):
    z = h @ w_in
    u = z[..., :D_MODEL] * jax.nn.sigmoid(z[..., D_MODEL:])
    ext = jnp.concatenate([cache.astype(u.dtype), u], axis=1)
    c = causal_depthwise(ext, dw) + dw_b
    c = jax.nn.silu(layer_norm(c, ln_g, ln_b))
    return c @ w_out, ext[:, -CONV_STATE:]


def _ssm_combine(e1, e2):
    ar1, ai1, br1, bi1 = e1
    ar2, ai2, br2, bi2 = e2
    return (ar2 * ar1 - ai2 * ai1,
            ar2 * ai1 + ai2 * ar1,
            ar2 * br1 - ai2 * bi1 + br2,
            ar2 * bi1 + ai2 * br1 + bi2)


def s5_layer(h, h0_re, h0_im, a_re, a_im, log_dt, b_re, b_im, c_re, c_im, d_skip, w_glu):
    f32 = jnp.float32
    bsz, seqlen, _ = h.shape
    u = h.astype(f32)
    ug = u.reshape(bsz, seqlen, N_GROUPS, GROUP_SIZE)
    lam_re, lam_im = a_re.astype(f32), a_im.astype(f32)
    dt = jnp.exp(log_dt.astype(f32))[:, None]
    mag = jnp.exp(lam_re * dt)
    ang = lam_im * dt
    abar_re, abar_im = mag * jnp.cos(ang), mag * jnp.sin(ang)
    nr, ni = abar_re - 1.0, abar_im
    den = lam_re * lam_re + lam_im * lam_im
    q_re = (nr * lam_re + ni * lam_im) / den
    q_im = (ni * lam_re - nr * lam_im) / den
    br, bi = b_re.astype(f32), b_im.astype(f32)
    bb_re = q_re[..., None] * br - q_im[..., None] * bi
    bb_im = q_re[..., None] * bi + q_im[..., None] * br
    bu_re = jnp.einsum('blgi,gpi->blgp', ug, bb_re)
    bu_im = jnp.einsum('blgi,gpi->blgp', ug, bb_im)
    s_re, s_im = h0_re.astype(f32), h0_im.astype(f32)
    bu_re = bu_re.at[:, 0].add(abar_re * s_re - abar_im * s_im)
    bu_im = bu_im.at[:, 0].add(abar_re * s_im + abar_im * s_re)
    a_full_re = jnp.broadcast_to(abar_re, bu_re.shape)
    a_full_im = jnp.broadcast_to(abar_im, bu_im.shape)
    _, _, st_re, st_im = lax.associative_scan(_ssm_combine, (a_full_re, a_full_im, bu_re, bu_im), axis=1)
    y = (jnp.einsum('blgp,gip->blgi', st_re, c_re.astype(f32))
         - jnp.einsum('blgp,gip->blgi', st_im, c_im.astype(f32)))
    y = y.reshape(bsz, seqlen, D_MODEL) + d_skip.astype(f32) * u
    v = jax.nn.gelu(y).astype(h.dtype)
    z = v @ w_glu
    out = z[..., :D_MODEL] * jax.nn.sigmoid(z[..., D_MODEL:])
    return out, st_re[:, -1].astype(h0_re.dtype), st_im[:, -1].astype(h0_im.dtype)


def conv_ffn(h, cache, w_gate, w_up, conv_w, w_down):
    g = h @ w_gate
    ext = jnp.concatenate([cache.astype(g.dtype), g], axis=1)
    gc = causal_depthwise(ext, conv_w)
    out = (jax.nn.silu(gc) * (h @ w_up)) @ w_down
    return out, ext[:, -FFN_STATE:]


def setup_inputs(seed: int = 0) -> dict:
    key = jax.random.key(seed)
    ks = jax.random.split(key, 32)
    f32 = jnp.float32
    nrm = lambda k, shape, s: jax.random.normal(k, shape, f32) * s
    D, F, G, P, NC, NS = D_MODEL, D_FF, N_GROUPS, STATE_DIM, N_CONV_LAYERS, N_SSM_LAYERS
    n_idx = jnp.arange(P, dtype=f32)
    return {
        "x_prompt": nrm(ks[0], (BATCH, SEQ, D), 1.0),
        "x_sample": nrm(ks[1], (DEC_BATCH, DEC_SEQ, D), 1.0),
        "state_conv": nrm(ks[2], (NC, DEC_BATCH, CONV_STATE, D), 0.5),
        "state_ssm_re": nrm(ks[3], (NS, DEC_BATCH, G, P), 0.1),
        "state_ssm_im": nrm(ks[4], (NS, DEC_BATCH, G, P), 0.1),
        "state_ffn": nrm(ks[5], (DEPTH, DEC_BATCH, FFN_STATE, F), 1.0),
        "norm_mix": 1.0 + nrm(ks[6], (DEPTH, D), 0.02),
        "norm_ffn": 1.0 + nrm(ks[7], (DEPTH, D), 0.02),
        "norm_final": 1.0 + nrm(ks[8], (D,), 0.02),
        "conv_w_in": nrm(ks[9], (NC, D, 2 * D), D ** -0.5),
        "conv_dw": nrm(ks[10], (NC, CONV_WIDTH, D), CONV_WIDTH ** -0.5),
        "conv_dw_b": nrm(ks[11], (NC, D), 0.02),
        "conv_ln_g": 1.0 + nrm(ks[12], (NC, D), 0.02),
        "conv_ln_b": nrm(ks[13], (NC, D), 0.02),
        "conv_w_out": nrm(ks[14], (NC, D, D), D ** -0.5),
        "ssm_A_re": -0.5 + nrm(ks[15], (NS, G, P), 0.01),
        "ssm_A_im": math.pi * n_idx + nrm(ks[16], (NS, G, P), 0.01),
        "ssm_log_dt": jax.random.uniform(ks[17], (NS, G), f32, math.log(1e-3), math.log(1e-1)),
        "ssm_B_re": nrm(ks[18], (NS, G, P, GROUP_SIZE), GROUP_SIZE ** -0.5),
        "ssm_B_im": nrm(ks[19], (NS, G, P, GROUP_SIZE), GROUP_SIZE ** -0.5),
        "ssm_C_re": nrm(ks[20], (NS, G, GROUP_SIZE, P), P ** -0.5),
        "ssm_C_im": nrm(ks[21], (NS, G, GROUP_SIZE, P), P ** -0.5),
        "ssm_D": nrm(ks[22], (NS, D), 1.0),
        "ssm_w_glu": nrm(ks[23], (NS, D, 2 * D), D ** -0.5),
        "ffn_w_gate": nrm(ks[24], (DEPTH, D, F), D ** -0.5),
        "ffn_w_up": nrm(ks[25], (DEPTH, D, F), D ** -0.5),
        "ffn_conv": nrm(ks[26], (DEPTH, FFN_CONV_WIDTH, F), FFN_CONV_WIDTH ** -0.5),
        "ffn_w_down": nrm(ks[27], (DEPTH, F, D), F ** -0.5),
    }


def reference(x_prompt, x_sample, state_conv, state_ssm_re, state_ssm_im, state_ffn,
              norm_mix, norm_ffn, norm_final,
              conv_w_in, conv_dw, conv_dw_b, conv_ln_g, conv_ln_b, conv_w_out,
              ssm_A_re, ssm_A_im, ssm_log_dt, ssm_B_re, ssm_B_im, ssm_C_re, ssm_C_im, ssm_D, ssm_w_glu,
              ffn_w_gate, ffn_w_up, ffn_conv, ffn_w_down):

    def run(x, conv_st, ssm_re_st, ssm_im_st, ffn_st):
        new_conv, new_re, new_im, new_ffn = [], [], [], []
        for i in range(DEPTH):
            j = i // N_MIXERS
            hn = rms_norm(x, norm_mix[i])
            if i % N_MIXERS == 0:
                mix, c_new = conformer_conv(hn, conv_st[j], conv_w_in[j], conv_dw[j], conv_dw_b[j],
                                            conv_ln_g[j], conv_ln_b[j], conv_w_out[j])
                new_conv.append(c_new)
            else:
                mix, s_re, s_im = s5_layer(hn, ssm_re_st[j], ssm_im_st[j], ssm_A_re[j], ssm_A_im[j],
                                           ssm_log_dt[j], ssm_B_re[j], ssm_B_im[j], ssm_C_re[j],
                                           ssm_C_im[j], ssm_D[j], ssm_w_glu[j])
                new_re.append(s_re)
                new_im.append(s_im)
            x = x + mix
            f_out, f_new = conv_ffn(rms_norm(x, norm_ffn[i]), ffn_st[i], ffn_w_gate[i], ffn_w_up[i],
                                    ffn_conv[i], ffn_w_down[i])
            new_ffn.append(f_new)
            x = x + f_out
        y = rms_norm(x, norm_final)
        return (y, jnp.stack(new_conv), jnp.stack(new_re), jnp.stack(new_im), jnp.stack(new_ffn))

    bp = x_prompt.shape[0]
    zc = jnp.zeros((N_CONV_LAYERS, bp, CONV_STATE, D_MODEL), x_prompt.dtype)
    zs_re = jnp.zeros((N_SSM_LAYERS, bp, N_GROUPS, STATE_DIM), state_ssm_re.dtype)
    zs_im = jnp.zeros((N_SSM_LAYERS, bp, N_GROUPS, STATE_DIM), state_ssm_im.dtype)
    zf = jnp.zeros((DEPTH, bp, FFN_STATE, D_FF), x_prompt.dtype)
    y_prompt, conv_p, re_p, im_p, ffn_p = run(x_prompt, zc, zs_re, zs_im, zf)
    y_sample, conv_s, re_s, im_s, ffn_s = run(x_sample, state_conv, state_ssm_re, state_ssm_im, state_ffn)
    return (y_prompt, y_sample, conv_p, conv_s, re_p, im_p, re_s, im_s, ffn_p, ffn_s)
```

```python
import os
import numpy as np
from contextlib import ExitStack
import concourse.bass as bass
import concourse.mybir as mybir
from concourse.bass_utils import run_bass_kernel_spmd

F32 = mybir.dt.float32
BF16 = mybir.dt.bfloat16
AF = mybir.ActivationFunctionType
ALU = mybir.AluOpType
AX = mybir.AxisListType

D = 2048
DC = 16
FF = 5632
FCH = 44
NPASS = 4
NTA = 512
NSEQ = 16
NTB = 64
NT = NTA + NTB
DEPTH = 4
EPS = 1e-6
NCORES = 8


class Sched:
    def __init__(self, nc, stack, n_dma_sems=24):
        self.nc = nc
        self.eng = {"pe": nc.tensor, "act": nc.scalar, "dve": nc.vector,
                    "pool": nc.gpsimd, "sp": nc.sync}
        self.sem = {}
        self.cnt = {}
        for e in self.eng:
            self.sem[e] = stack.enter_context(nc.semaphore("s_" + e))
            self.cnt[e] = 0
        self.dma_free = []
        for i in range(n_dma_sems):
            k = "dma%d" % i
            self.sem[k] = stack.enter_context(nc.semaphore("s_" + k))
            self.cnt[k] = 0
            self.dma_free.append(k)
        self.slot_sem = {}
        self.waited = {e: {} for e in self.eng}
        self.last_w = {}
        self.reads = {}

    def _deps(self, reads, writes):
        deps = {}

        def add(d):
            if d is None:
                return
            k, n = d
            if deps.get(k, 0) < n:
                deps[k] = n
        for b in reads:
            add(self.last_w.get(b))
        for b in writes:
            add(self.last_w.get(b))
            for d in self.reads.get(b, ()):
                add(d)
        return deps

    def _emit_waits(self, e, deps):
        for k, n in deps.items():
            if k == e and e == "pe":
                continue
            if self.waited[e].get(k, 0) >= n:
                continue
            self.eng[e].wait_ge(self.sem[k], n)
            self.waited[e][k] = n

    def _record(self, key, n, reads, writes):
        for b in writes:
            self.last_w[b] = (key, n)
            self.reads[b] = []
        for b in reads:
            lst = self.reads.setdefault(b, [])
            lst.append((key, n))
            if len(lst) > 8:
                m = {}
                for k2, n2 in lst:
                    m[k2] = max(m.get(k2, 0), n2)
                self.reads[b] = list(m.items())

    def op(self, e, fns, reads=(), writes=()):
        if callable(fns):
            fns = [fns]
        deps = self._deps(reads, writes)
        self._emit_waits(e, deps)
        eng = self.eng[e]
        ins = None
        for f in fns:
            ins = f(eng)
        self.cnt[e] += 1
        ins.then_inc(self.sem[e], 1)
        self._record(e, self.cnt[e], reads, writes)

    def dma(self, e, slot, fns, reads=(), writes=()):
        if callable(fns):
            fns = [fns]
        if slot not in self.slot_sem:
            self.slot_sem[slot] = self.dma_free.pop(0)
        k = self.slot_sem[slot]
        deps = self._deps(reads, writes)
        self._emit_waits(e, deps)
        eng = self.eng[e]
        for f in fns:
            ins = f(eng)
            self.cnt[k] += 16
            ins.then_inc(self.sem[k], 16)
        self._record(k, self.cnt[k], reads, writes)

    def barrier(self):
        for e in self.eng:
            deps = {k: self.cnt[k] for k in self.sem if self.cnt[k] > 0}
            self._emit_waits(e, deps)

    def final_wait(self, e="sp"):
        deps = {k: self.cnt[k] for k in self.sem if self.cnt[k] > 0}
        self._emit_waits(e, deps)


def build_nc(layers=(0, 1, 2, 3), npass=NPASS, ssm_stage=4):
    nc = bass.Bass("TRN2", target_bir_lowering=False)

    def din(name, shape):
        return nc.dram_tensor(name, list(shape), F32, kind="ExternalInput").ap()

    def dout(name, shape):
        return nc.dram_tensor(name, list(shape), F32, kind="ExternalOutput").ap()

    xp = din("xp", [2048, D])
    xs = din("xs", [NTB, D])
    sconv = din("sconv", [2, NSEQ, 30, D])
    sre = din("sre", [2, NSEQ, 8192])
    sim = din("sim", [2, NSEQ, 8192])
    sffn = din("sffn", [4, NSEQ * 2, FF])
    norm_mix = din("norm_mix", [4, D])
    norm_ffn = din("norm_ffn", [4, D])
    norm_final = din("norm_final", [1, D])
    conv_w_in = din("conv_w_in", [2, D, 2 * D])
    conv_dw = din("conv_dw", [2, 31, D])
    conv_dw_b = din("conv_dw_b", [2, D])
    conv_ln_g = din("conv_ln_g", [2, D])
    conv_ln_b = din("conv_ln_b", [2, D])
    conv_w_out = din("conv_w_out", [2, D, D])
    ssm_A_re = din("ssm_A_re", [2, 128, 64])
    ssm_A_im = din("ssm_A_im", [2, 128, 64])
    ssm_log_dt = din("ssm_log_dt", [2, 128])
    ssm_B_re = din("ssm_B_re", [2, 128, 1024])
    ssm_B_im = din("ssm_B_im", [2, 128, 1024])
    ssm_C_re = din("ssm_C_re", [2, 128, 1024])
    ssm_C_im = din("ssm_C_im", [2, 128, 1024])
    ssm_D = din("ssm_D", [2, D])
    ssm_w_glu = din("ssm_w_glu", [2, D, 2 * D])
    ffn_w_gate = din("ffn_w_gate", [4, D, FF])
    ffn_w_up = din("ffn_w_up", [4, D, FF])
    ffn_conv = din("ffn_conv", [4, 3, FF])
    ffn_w_down = din("ffn_w_down", [4, FF, D])

    yp = dout("yp", [2048, D])
    ys = dout("ys", [NTB, D])
    convp = dout("convp", [2, 30, D])
    convs = dout("convs", [2, NSEQ, 30, D])
    rep = dout("rep", [2, 128, 64])
    imp = dout("imp", [2, 128, 64])
    res_ = dout("res", [2, NSEQ, 8192])
    ims_ = dout("ims", [2, NSEQ, 8192])
    ffnp = dout("ffnp", [4, 2, FF])
    ffns = dout("ffns", [4, NSEQ * 2, FF])

    scrA = nc.dram_tensor("scrA", [2, 2, 128, 64], F32, kind="Internal").ap()
    scrB = nc.dram_tensor("scrB", [2, 2, 128, 1024], F32, kind="Internal").ap()
    scrBb = nc.dram_tensor("scrBb", [2, 2, 128, DC * 128], BF16, kind="Internal").ap()
    scrCb = nc.dram_tensor("scrCb", [2, 2, 128, 64 * 32], BF16, kind="Internal").ap()

    with ExitStack() as st:
        S = Sched(nc, st, n_dma_sems=40)

        def T(name, shape, dt=F32):
            return st.enter_context(nc.sbuf_tensor(name, list(shape), dt))

        xT = T("xT", [128, DC, NT])
        hn = T("hn", [128, DC, NT], BF16)
        cb = T("cb", [128, DC, NT], BF16)
        rs = T("rs", [128, NT])
        rs2 = T("rs2", [128, NT])
        tmpA = [T("tmpA%d" % i, [128, NT]) for i in range(3)]
        xin = [T("xin%d" % i, [128, D]) for i in range(2)]
        NWS = 4
        wsl = [T("wsl%d" % i, [128, 16, 256], BF16) for i in range(NWS)]
        ident = T("ident", [128, 128])
        identb = T("identb", [128, 128], BF16)
        ones = T("ones", [128, 128])
        onesb = T("onesb", [128, 128], BF16)
        epst = T("epst", [128, 1])
        g_mix = T("g_mix", [128, 64])
        g_ffn = T("g_ffn", [128, 64])
        g_fin = T("g_fin", [128, 16])
        cdwb = T("cdwb", [128, 32])
        clng = T("clng", [128, 32])
        clnb = T("clnb", [128, 32])
        sDv = T("sDv", [128, 32])
        fcw = T("fcw", [128, 4 * 3 * FCH])
        dwT = T("dwT", [128, 2 * 31 * DC])
        halo_c = [T("halo_c%d" % j, [128, DC, 30], BF16) for j in range(2)]
        halo_f = [T("halo_f%d" % i, [128, FCH, 2]) for i in range(4)]
        stg = T("stg", [128, 128])
        apair = [[T("apair%d%d" % (j, r), [128, 64]) for r in range(2)] for j in range(2)]
        carry = [[T("carry%d%d" % (j, r), [128, 64]) for r in range(2)] for j in range(2)]
        ps = [st.enter_context(nc.psum_tensor("ps%d" % i, [128, 512], F32)) for i in range(8)]
        psn = [0]
        uid = [0]

        def nextps():
            i = psn[0] % 8
            psn[0] += 1
            return ("ps", i), ps[i]

        wsn = [0]

        def nextws():
            i = wsn[0] % NWS
            wsn[0] += 1
            return ("ws", i), wsl[i]

        tan = [0]

        def nexttmp():
            i = tan[0] % 3
            tan[0] += 1
            return ("tmpA", i), tmpA[i]

        S.op("pool", lambda e: e.memset(ones[:], 1.0), writes=["ones"])
        S.op("pool", lambda e: e.memset(ident[:], 1.0), writes=["ident"])
        S.op("pool", lambda e: e.affine_select(out=ident[:], in_=ident[:], pattern=[[1, 128]],
                                               compare_op=ALU.is_equal, fill=0.0, base=0,
                                               channel_multiplier=-1),
             reads=["ident"], writes=["ident"])
        S.op("dve", lambda e: e.tensor_copy(out=identb[:], in_=ident[:]), reads=["ident"], writes=["identb"])
        S.op("dve", lambda e: e.memset(epst[:], EPS), writes=["epst"])
        S.op("dve", lambda e: e.memset(onesb[:], 1.0), writes=["onesb"])
        for j in range(2):
            S.op("dve", lambda e, j=j: e.memset(halo_c[j][:], 0.0), writes=[("halo_c", j)])
        for i in range(4):
            S.op("dve", lambda e, i=i: e.memset(halo_f[i][:], 0.0), writes=[("halo_f", i)])

        def load_vecT(src_rows_ap, nrows, dst, dst_col0):
            r0 = 0
            while r0 < nrows:
                n = min(128, nrows - r0)
                S.dma("sp", "stg", lambda e, r0=r0, n=n: e.dma_start(out=stg[0:n, :], in_=src_rows_ap[r0:r0 + n, :]),
                      writes=["stg"])
                pk, pt = nextps()
                S.op("pe", lambda e, n=n, pt=pt: e.transpose(pt[:, 0:n], stg[0:n, :], ident[0:n, 0:n]),
                     reads=["stg", "ident"], writes=[pk])
                S.op("act", lambda e, n=n, pt=pt, r0=r0: e.activation(out=dst[:, dst_col0 + r0:dst_col0 + r0 + n],
                                                                   in_=pt[:, 0:n], func=AF.Copy),
                     reads=[pk], writes=[("vec", id(dst))])
                r0 += n

        load_vecT(norm_mix.rearrange("l (c p) -> (l c) p", p=128), 64, g_mix, 0)
        load_vecT(norm_ffn.rearrange("l (c p) -> (l c) p", p=128), 64, g_ffn, 0)
        load_vecT(norm_final.rearrange("l (c p) -> (l c) p", p=128), 16, g_fin, 0)
        load_vecT(conv_dw_b.rearrange("l (c p) -> (l c) p", p=128), 32, cdwb, 0)
        load_vecT(conv_ln_g.rearrange("l (c p) -> (l c) p", p=128), 32, clng, 0)
        load_vecT(conv_ln_b.rearrange("l (c p) -> (l c) p", p=128), 32, clnb, 0)
        load_vecT(ssm_D.rearrange("l (c p) -> (l c) p", p=128), 32, sDv, 0)
        load_vecT(ffn_conv.rearrange("l k (c p) -> (l k c) p", p=128), 4 * 3 * FCH, fcw, 0)
        load_vecT(conv_dw.rearrange("l k (c p) -> (l k c) p", p=128), 2 * 31 * DC, dwT, 0)
        VEC = [("vec", id(t)) for t in (g_mix, g_ffn, g_fin, cdwb, clng, clnb, sDv, fcw, dwT)]

        def blocks_of(p):
            return [(0, NTA)] + ([(NTA, NTB)] if p == npass - 1 else [])

        def load_x(p):
            srcs = [(xp, p * NTA + r * 128, 128, r * 128) for r in range(4)]
            if p == npass - 1:
                srcs.append((xs, 0, NTB, NTA))
            for si, (src, row0, n, col0) in enumerate(srcs):
                xi = si % 2
                S.dma("sp", ("xin", xi), lambda e, src=src, row0=row0, n=n, xi=xi:
                      e.dma_start(out=xin[xi][0:n, :], in_=src[row0:row0 + n, :]), writes=[("xin", xi)])
                for c4 in range(4):
                    pk, pt = nextps()
                    S.op("pe", [lambda e, c=c, n=n, xi=xi, pt=pt, c4=c4: e.transpose(
                        pt[:, (c - 4 * c4) * 128:(c - 4 * c4) * 128 + n], xin[xi][0:n, c * 128:(c + 1) * 128],
                        ident[0:n, 0:n]) for c in range(4 * c4, 4 * c4 + 4)],
                        reads=[("xin", xi), "ident"], writes=[pk])
                    S.op("act", lambda e, c4=c4, n=n, col0=col0, pt=pt: e.activation(
                        out=xT[:, 4 * c4:4 * c4 + 4, col0:col0 + n],
                        in_=pt[:, :].rearrange("p (c t) -> p c t", t=128)[:, :, 0:n], func=AF.Copy),
                        reads=[pk], writes=[("xT", c) for c in range(4 * c4, 4 * c4 + 4)])

        def rmsnorm(p, gain, gcol0, dst_fn=None, dst_key="hn"):
            blks = blocks_of(p)
            ncol = blks[-1][0] + blks[-1][1]
            pks = [nextps() for _ in blks]
            for c in range(DC):
                tk, tt = nexttmp()
                S.op("act", lambda e, c=c, tt=tt: e.activation(out=tt[:, 0:ncol], in_=xT[:, c, 0:ncol], func=AF.Square),
                     reads=[("xT", c)], writes=[tk])
                for (pk, pt), (b0, bn) in zip(pks, blks):
                    S.op("pe", lambda e, c=c, pt=pt, b0=b0, bn=bn, tt=tt: e.matmul(
                        pt[:, 0:bn], ones[:], tt[:, b0:b0 + bn], start=(c == 0), stop=(c == DC - 1)),
                        reads=[tk, "ones"], writes=[pk])
            for (pk, pt), (b0, bn) in zip(pks, blks):
                S.op("act", lambda e, pt=pt, b0=b0, bn=bn: e.activation(
                    out=rs[:, b0:b0 + bn], in_=pt[:, 0:bn], func=AF.Sqrt, scale=1.0 / D, bias=epst[:, 0:1]),
                    reads=[pk, "epst"], writes=["rs"])
            S.op("dve", lambda e: e.reciprocal(out=rs[:, 0:ncol], in_=rs[:, 0:ncol]), reads=["rs"], writes=["rs"])
            if dst_fn is None:
                for c in range(DC):
                    S.op("dve", lambda e, c=c: e.scalar_tensor_tensor(
                        out=hn[:, c, 0:ncol], in0=xT[:, c, 0:ncol], scalar=gain[:, gcol0 + c:gcol0 + c + 1],
                        in1=rs[:, 0:ncol], op0=ALU.mult, op1=ALU.mult),
                        reads=[("xT", c), "rs"] + VEC, writes=[(dst_key, c)])

        def gemm(p, srcs, kc_n, ngroups, rhs_fn, rhs_keys, epi):
            blks = blocks_of(p)
            loaded = {}

            def load(g):
                lst = []
                for (W, row0, col0) in srcs:
                    wk, wt = nextws()
                    S.dma("pool", wk, lambda e, W=W, row0=row0, col0=col0, wt=wt, g=g: e.dma_start(
                        out=wt[:, 0:kc_n, :],
                        in_=W[row0:row0 + kc_n * 128, col0 + 256 * g:col0 + 256 * g + 256].rearrange(
                            "(k p) m -> p k m", p=128)), writes=[wk])
                    lst.append((wk, wt))
                loaded[g] = lst
            load(0)
            for g in range(ngroups):
                if g + 1 < ngroups:
                    load(g + 1)
                for ci in range(2):
                    for bi, (b0, bn) in enumerate(blks):
                        pls = []
                        for (wk, wt) in loaded[g]:
                            pk, pt = nextps()
                            S.op("pe", [lambda e, kc=kc, wt=wt, pt=pt, b0=b0, bn=bn, ci=ci: e.matmul(
                                pt[:, 0:bn], wt[:, kc, ci * 128:(ci + 1) * 128], rhs_fn(kc, b0, bn),
                                start=(kc == 0), stop=(kc == kc_n - 1)) for kc in range(kc_n)],
                                reads=[wk] + rhs_keys, writes=[pk])
                            pls.append((pk, pt))
                        epi(g, ci, bi, b0, bn, pls)
                del loaded[g]

        def glu_residual_epi(g, ci, bi, b0, bn, pls):
            mc = 2 * g + ci
            (pkv, ptv), (pkg, ptg) = pls
            tk, tt = nexttmp()
            S.op("act", lambda e: e.activation(out=tt[:, 0:bn], in_=ptg[:, 0:bn], func=AF.Sigmoid),
                 reads=[pkg], writes=[tk])
            S.op("dve", lambda e: e.tensor_tensor(out=tt[:, 0:bn], in0=ptv[:, 0:bn], in1=tt[:, 0:bn], op=ALU.mult),
                 reads=[pkv, tk], writes=[tk])
            S.op("dve", lambda e: e.tensor_tensor(out=xT[:, mc, b0:b0 + bn], in0=xT[:, mc, b0:b0 + bn],
                                                  in1=tt[:, 0:bn], op=ALU.add),
                 reads=[tk, ("xT", mc)], writes=[("xT", mc)])

        def residual_epi(g, ci, bi, b0, bn, pls):
            mc = 2 * g + ci
            (pk, pt), = pls
            S.op("dve", lambda e: e.tensor_tensor(out=xT[:, mc, b0:b0 + bn], in0=xT[:, mc, b0:b0 + bn],
                                                  in1=pt[:, 0:bn], op=ALU.add),
                 reads=[pk, ("xT", mc)], writes=[("xT", mc)])

        HN_KEYS = [("hn", c) for c in range(DC)]

        def conv_layer(p, li, j):
            blks = blocks_of(p)
            last = (p == npass - 1)
            ncol = blks[-1][0] + blks[-1][1]
            with ExitStack() as ls:
                def LT(name, shape, dt=F32):
                    uid[0] += 1
                    return ls.enter_context(nc.sbuf_tensor("%s_%d" % (name, uid[0]), list(shape), dt))
                uxp = LT("uxp", [128, DC, 30 + NTA], BF16)
                uxs = LT("uxs", [128, DC, NSEQ, 34], BF16)
                dg = [LT("dg%d" % i, [128, 31, 128], BF16) for i in range(1)]
                ulast = LT("ulast", [128, DC, 94])
                rmsnorm(p, g_mix, li * DC)
                for c in range(DC):
                    S.op("act", lambda e, c=c: e.activation(out=uxp[:, c, 0:30], in_=halo_c[j][:, c, :], func=AF.Copy),
                         reads=[("halo_c", j)], writes=[("uxp", c)])
                if last:
                    for r4 in range(4):
                        xi = r4 % 2
                        S.dma("sp", ("xin", xi), lambda e, r4=r4, xi=xi: e.dma_start(
                            out=xin[xi][0:120, :],
                            in_=sconv[j, 4 * r4:4 * r4 + 4].rearrange("s r d -> (s r) d")), writes=[("xin", xi)])
                        for c4 in range(4):
                            pk, pt = nextps()
                            S.op("pe", [lambda e, c=c, xi=xi, pt=pt, c4=c4: e.transpose(
                                pt[:, (c - 4 * c4) * 128:(c - 4 * c4) * 128 + 120], xin[xi][0:120, c * 128:(c + 1) * 128],
                                ident[0:120, 0:120]) for c in range(4 * c4, 4 * c4 + 4)],
                                reads=[("xin", xi), "ident"], writes=[pk])
                            for cc in range(4):
                                c = 4 * c4 + cc
                                S.op("act", lambda e, c=c, cc=cc, r4=r4, pt=pt: e.activation(
                                    out=uxs[:, c, 4 * r4:4 * r4 + 4, 0:30],
                                    in_=pt[:, cc * 128:cc * 128 + 120].rearrange("p (s r) -> p s r", r=30),
                                    func=AF.Copy), reads=[pk], writes=[("uxs", c)])
                    S.dma("sp", "cs_copy", lambda e: e.dma_start(out=convs[j, :, 0:26, :], in_=sconv[j, :, 4:30, :]))

                def glu_u_epi(g, ci, bi, b0, bn, pls):
                    mc = 2 * g + ci
                    (pkv, ptv), (pkg, ptg) = pls
                    tk, tt = nexttmp()
                    S.op("act", lambda e: e.activation(out=tt[:, 0:bn], in_=ptg[:, 0:bn], func=AF.Sigmoid),
                         reads=[pkg], writes=[tk])
                    if bi == 0:
                        S.op("dve", lambda e: e.tensor_tensor(out=uxp[:, mc, 30:30 + NTA], in0=ptv[:, 0:NTA],
                                                              in1=tt[:, 0:NTA], op=ALU.mult),
                             reads=[pkv, tk], writes=[("uxp", mc)])
                        S.op("act", lambda e: e.activation(out=halo_c[j][:, mc, :], in_=uxp[:, mc, NTA:NTA + 30],
                                                           func=AF.Copy),
                             reads=[("uxp", mc)], writes=[("halo_c", j)])
                        if last:
                            S.op("dve", lambda e: e.tensor_tensor(out=ulast[:, mc, 0:30], in0=ptv[:, NTA - 30:NTA],
                                                                  in1=tt[:, NTA - 30:NTA], op=ALU.mult),
                                 reads=[pkv, tk], writes=[("ulast", mc)])
                    else:
                        S.op("dve", lambda e: e.tensor_tensor(
                            out=uxs[:, mc, :, 30:34], in0=ptv[:, 0:NTB].rearrange("p (s t) -> p s t", t=4),
                            in1=tt[:, 0:NTB].rearrange("p (s t) -> p s t", t=4), op=ALU.mult),
                            reads=[pkv, tk], writes=[("uxs", mc)])
                        S.op("dve", lambda e: e.tensor_tensor(out=ulast[:, mc, 30:94], in0=ptv[:, 0:NTB],
                                                              in1=tt[:, 0:NTB], op=ALU.mult),
                             reads=[pkv, tk], writes=[("ulast", mc)])

                W = conv_w_in[j]
                gemm(p, [(W, 0, 0), (W, 0, D)], DC, 8, lambda kc, b0, bn: hn[:, kc, b0:b0 + bn], HN_KEYS, glu_u_epi)

                if last:
                    for c4 in range(4):
                        pk, pt = nextps()
                        S.op("pe", [lambda e, c=c, pt=pt, c4=c4: e.transpose(
                            pt[0:94, (c - 4 * c4) * 128:(c - 4 * c4 + 1) * 128], ulast[:, c, :], ident[:, :])
                            for c in range(4 * c4, 4 * c4 + 4)],
                            reads=[("ulast", c) for c in range(4 * c4, 4 * c4 + 4)] + ["ident"], writes=[pk])
                        S.op("act", lambda e, c4=c4, pt=pt: e.activation(out=xin[0][0:94, c4 * 512:(c4 + 1) * 512],
                                                                      in_=pt[0:94, :], func=AF.Copy),
                             reads=[pk], writes=[("xin", 0)])
                    S.dma("sp", ("xin", 0), [lambda e: e.dma_start(out=convp[j], in_=xin[0][0:30, :])] +
                          [lambda e, s=s: e.dma_start(out=convs[j, s, 26:30, :], in_=xin[0][30 + 4 * s:34 + 4 * s, :])
                           for s in range(NSEQ)], reads=[("xin", 0)])

                for c in range(DC):
                    di = 0
                    S.op("dve", lambda e, c=c, di=di: e.tensor_tensor(
                        out=dg[di][:, :, :], in0=identb[:, :].unsqueeze(1).broadcast_to([128, 31, 128]),
                        in1=dwT[:, (j * 31) * DC + c:(j * 31 + 31) * DC:DC].unsqueeze(2).broadcast_to([128, 31, 128]),
                        op=ALU.mult), reads=["identb"] + VEC, writes=[("dg", di)])
                    for bi, (b0, bn) in enumerate(blks):
                        pk, pt = nextps()
                        if bi == 0:
                            S.op("pe", [lambda e, k=k, c=c, di=di, pt=pt: e.matmul(
                                pt[:, 0:NTA], dg[di][:, k, :], uxp[:, c, k:k + NTA], start=(k == 0), stop=(k == 30))
                                for k in range(31)], reads=[("dg", di), ("uxp", c)], writes=[pk])
                        else:
                            S.op("pe", [lambda e, k=k, c=c, di=di, pt=pt: e.matmul(
                                pt[:, 0:NTB].rearrange("p (s t) -> p s t", t=4), dg[di][:, k, :], uxs[:, c, :, k:k + 4],
                                start=(k == 0), stop=(k == 30)) for k in range(31)],
                                reads=[("dg", di), ("uxs", c)], writes=[pk])
                        S.op("act", lambda e, c=c, pt=pt, b0=b0, bn=bn: e.activation(
                            out=cb[:, c, b0:b0 + bn], in_=pt[:, 0:bn], func=AF.Identity,
                            bias=cdwb[:, j * DC + c:j * DC + c + 1]), reads=[pk] + VEC, writes=[("cb", c)])
                pk1 = [nextps() for _ in blks]
                pk2 = [nextps() for _ in blks]
                for c in range(DC):
                    tk, tt = nexttmp()
                    S.op("act", lambda e, c=c, tt=tt: e.activation(out=tt[:, 0:ncol], in_=cb[:, c, 0:ncol], func=AF.Square),
                         reads=[("cb", c)], writes=[tk])
                    for (pka, pta), (pkb, ptb), (b0, bn) in zip(pk1, pk2, blks):
                        S.op("pe", lambda e, c=c, pta=pta, b0=b0, bn=bn: e.matmul(
                            pta[:, 0:bn], onesb[:], cb[:, c, b0:b0 + bn], start=(c == 0), stop=(c == DC - 1)),
                            reads=[("cb", c), "onesb"], writes=[pka])
                        S.op("pe", lambda e, c=c, ptb=ptb, b0=b0, bn=bn, tt=tt: e.matmul(
                            ptb[:, 0:bn], ones[:], tt[:, b0:b0 + bn], start=(c == 0), stop=(c == DC - 1)),
                            reads=[tk, "ones"], writes=[pkb])
                for (pka, pta), (pkb, ptb), (b0, bn) in zip(pk1, pk2, blks):
                    S.op("act", lambda e, pta=pta, b0=b0, bn=bn: e.activation(
                        out=rs2[:, b0:b0 + bn], in_=pta[:, 0:bn], func=AF.Copy, scale=1.0 / D), reads=[pka], writes=["rs2"])
                    tk, tt = nexttmp()
                    S.op("dve", lambda e, tt=tt, b0=b0, bn=bn: e.tensor_tensor(
                        out=tt[:, 0:bn], in0=rs2[:, b0:b0 + bn], in1=rs2[:, b0:b0 + bn], op=ALU.mult),
                        reads=["rs2"], writes=[tk])
                    S.op("dve", lambda e, tt=tt, ptb=ptb, b0=b0, bn=bn: e.scalar_tensor_tensor(
                        out=tt[:, 0:bn], in0=ptb[:, 0:bn], scalar=1.0 / D, in1=tt[:, 0:bn], op0=ALU.mult,
                        op1=ALU.subtract), reads=[pkb, tk], writes=[tk])
                    S.op("act", lambda e, tt=tt, b0=b0, bn=bn: e.activation(
                        out=rs[:, b0:b0 + bn], in_=tt[:, 0:bn], func=AF.Sqrt, bias=epst[:, 0:1]),
                        reads=[tk, "epst"], writes=["rs"])
                S.op("dve", lambda e: e.reciprocal(out=rs[:, 0:ncol], in_=rs[:, 0:ncol]), reads=["rs"], writes=["rs"])
                for c in range(DC):
                    tk, tt = nexttmp()
                    S.op("dve", lambda e, c=c, tt=tt: e.tensor_tensor(out=tt[:, 0:ncol], in0=cb[:, c, 0:ncol],
                                                               in1=rs2[:, 0:ncol], op=ALU.subtract),
                         reads=[("cb", c), "rs2"], writes=[tk])
                    S.op("dve", lambda e, c=c, tt=tt: e.tensor_tensor(out=tt[:, 0:ncol], in0=tt[:, 0:ncol],
                                                               in1=rs[:, 0:ncol], op=ALU.mult),
                         reads=[tk, "rs"], writes=[tk])
                    S.op("act", lambda e, c=c, tt=tt: e.activation(
                        out=hn[:, c, 0:ncol], in_=tt[:, 0:ncol], func=AF.Silu,
                        scale=clng[:, j * DC + c:j * DC + c + 1], bias=clnb[:, j * DC + c:j * DC + c + 1]),
                        reads=[tk] + VEC, writes=[("hn", c)])
                gemm(p, [(conv_w_out[j], 0, 0)], DC, 8, lambda kc, b0, bn: hn[:, kc, b0:b0 + bn], HN_KEYS, residual_epi)
                S.barrier()

        def ffn_layer(p, li):
            blks = blocks_of(p)
            last = (p == npass - 1)
            ncol = blks[-1][0] + blks[-1][1]
            QS = [(0, 12), (12, 12), (24, 12), (36, 8)]
            with ExitStack() as ls:
                def LT(name, shape, dt=F32):
                    uid[0] += 1
                    return ls.enter_context(nc.sbuf_tensor("%s_%d" % (name, uid[0]), list(shape), dt))
                aT = LT("aT", [128, 12, NT], BF16)
                gxp = [LT("gxp%d" % i, [128, 2 + NTA]) for i in range(3)]
                gxs = [LT("gxs%d" % i, [128, NSEQ, 6]) for i in range(3)]
                gcs = LT("gcs", [128, FCH, NSEQ, 2]) if last else None
                glast = LT("glast", [128, FCH, 34]) if last else None
                rmsnorm(p, g_ffn, li * DC)
                if last:
                    for h in range(3):
                        c0 = h * 2048
                        cn_ = min(2048, FF - c0)
                        xi = h % 2
                        S.dma("sp", ("xin", xi), lambda e, c0=c0, cn_=cn_, xi=xi: e.dma_start(
                            out=xin[xi][0:32, 0:cn_], in_=sffn[li, :, c0:c0 + cn_]), writes=[("xin", xi)])
                        for c4 in range(cn_ // 512):
                            pk, pt = nextps()
                            S.op("pe", [lambda e, cc=cc, xi=xi, pt=pt, c4=c4: e.transpose(
                                pt[:, cc * 128:cc * 128 + 32], xin[xi][0:32, (4 * c4 + cc) * 128:(4 * c4 + cc + 1) * 128],
                                ident[0:32, 0:32]) for cc in range(4)],
                                reads=[("xin", xi), "ident"], writes=[pk])
                            ch0 = h * 16 + 4 * c4
                            S.op("act", lambda e, ch0=ch0, pt=pt: e.activation(
                                out=gcs[:, ch0:ch0 + 4, :, :].rearrange("p c s r -> p c (s r)"),
                                in_=pt[:, :].rearrange("p (c t) -> p c t", t=128)[:, :, 0:32], func=AF.Copy),
                                reads=[pk], writes=["gcs"])
                gi = [0]

                def ffn_epi_factory(q0):
                    def epi(g, ci, bi, b0, bn, pls):
                        fl = 2 * g + ci
                        f = q0 + fl
                        (pkg, ptg), (pku, ptu) = pls
                        w0 = fcw[:, (li * 3 + 0) * FCH + f:(li * 3 + 0) * FCH + f + 1]
                        w1 = fcw[:, (li * 3 + 1) * FCH + f:(li * 3 + 1) * FCH + f + 1]
                        w2 = fcw[:, (li * 3 + 2) * FCH + f:(li * 3 + 2) * FCH + f + 1]
                        i3 = gi[0] % 3
                        gi[0] += 1
                        tk, tt = nexttmp()
                        if bi == 0:
                            gx = gxp[i3]
                            gk = ("gxp", i3)
                            S.op("act", lambda e: e.activation(out=gx[:, 0:2], in_=halo_f[li][:, f, :], func=AF.Copy),
                                 reads=[("halo_f", li)], writes=[gk])
                            S.op("act", lambda e: e.activation(out=gx[:, 2:2 + NTA], in_=ptg[:, 0:NTA], func=AF.Copy),
                                 reads=[pkg, gk], writes=[gk])
                            S.op("act", lambda e: e.activation(out=halo_f[li][:, f, :], in_=gx[:, NTA:NTA + 2], func=AF.Copy),
                                 reads=[gk], writes=[("halo_f", li)])
                            if last:
                                S.op("act", lambda e: e.activation(out=glast[:, f, 0:2], in_=gx[:, NTA:NTA + 2],
                                                                   func=AF.Copy), reads=[gk], writes=["glast"])
                            a0, a1, a2 = gx[:, 0:NTA], gx[:, 1:1 + NTA], gx[:, 2:2 + NTA]
                            to = tt[:, 0:NTA]
                            pu = ptu[:, 0:NTA]
                            ao = aT[:, fl, 0:NTA]
                        else:
                            gx = gxs[i3]
                            gk = ("gxs", i3)
                            S.op("act", lambda e: e.activation(out=gx[:, :, 0:2], in_=gcs[:, f, :, :], func=AF.Copy),
                                 reads=["gcs"], writes=[gk])
                            S.op("act", lambda e: e.activation(
                                out=gx[:, :, 2:6], in_=ptg[:, 0:NTB].rearrange("p (s t) -> p s t", t=4), func=AF.Copy),
                                reads=[pkg, gk], writes=[gk])
                            S.op("act", lambda e: e.activation(
                                out=glast[:, f, 2:34].rearrange("p (s r) -> p s r", r=2), in_=gx[:, :, 4:6], func=AF.Copy),
                                reads=[gk], writes=["glast"])
                            a0, a1, a2 = gx[:, :, 0:4], gx[:, :, 1:5], gx[:, :, 2:6]
                            to = tt[:, 0:NTB].rearrange("p (s t) -> p s t", t=4)
                            pu = ptu[:, 0:NTB].rearrange("p (s t) -> p s t", t=4)
                            ao = aT[:, fl, NTA:NT].rearrange("p (s t) -> p s t", t=4)
                        S.op("dve", lambda e: e.tensor_scalar(out=to, in0=a0, scalar1=w0, scalar2=None, op0=ALU.mult),
                             reads=[gk] + VEC, writes=[tk])
                        S.op("dve", lambda e: e.scalar_tensor_tensor(out=to, in0=a1, scalar=w1, in1=to, op0=ALU.mult,
                                                                     op1=ALU.add), reads=[gk, tk], writes=[tk])
                        S.op("dve", lambda e: e.scalar_tensor_tensor(out=to, in0=a2, scalar=w2, in1=to, op0=ALU.mult,
                                                                     op1=ALU.add), reads=[gk, tk], writes=[tk])
                        S.op("act", lambda e: e.activation(out=to, in_=to, func=AF.Silu), reads=[tk], writes=[tk])
                        S.op("dve", lambda e: e.tensor_tensor(out=ao, in0=pu, in1=to, op=ALU.mult),
                             reads=[tk, pku], writes=[("aT", fl)])
                    return epi

                AT_KEYS = [("aT", f) for f in range(12)]
                for (q0, qn) in QS:
                    gemm(p, [(ffn_w_gate[li], 0, q0 * 128), (ffn_w_up[li], 0, q0 * 128)], DC, qn // 2,
                         lambda kc, b0, bn: hn[:, kc, b0:b0 + bn], HN_KEYS, ffn_epi_factory(q0))
                    gemm(p, [(ffn_w_down[li], q0 * 128, 0)], qn, 8,
                         lambda kc, b0, bn: aT[:, kc, b0:b0 + bn], AT_KEYS, residual_epi)
                if last:
                    for h in range(3):
                        c0 = h * 2048
                        cn_ = min(2048, FF - c0)
                        xi = h % 2
                        for c4 in range(cn_ // 512):
                            pk, pt = nextps()
                            S.op("pe", [lambda e, cc=cc, pt=pt, c4=c4, h=h: e.transpose(
                                pt[0:34, cc * 128:(cc + 1) * 128], glast[:, h * 16 + 4 * c4 + cc, :], ident[:, :])
                                for cc in range(4)], reads=["glast", "ident"], writes=[pk])
                            S.op("act", lambda e, c4=c4, pt=pt, xi=xi: e.activation(
                                out=xin[xi][0:34, c4 * 512:(c4 + 1) * 512], in_=pt[0:34, :], func=AF.Copy),
                                reads=[pk], writes=[("xin", xi)])
                        S.dma("sp", ("xin", xi), [
                            lambda e, c0=c0, cn_=cn_, xi=xi: e.dma_start(out=ffnp[li][:, c0:c0 + cn_], in_=xin[xi][0:2, 0:cn_]),
                            lambda e, c0=c0, cn_=cn_, xi=xi: e.dma_start(out=ffns[li][:, c0:c0 + cn_], in_=xin[xi][2:34, 0:cn_])],
                            reads=[("xin", xi)])
                S.barrier()


        TWO_PI = 2.0 * np.pi

        def ssm_prep(j):
            with ExitStack() as ls:
                def LT(name, shape, dt=F32):
                    uid[0] += 1
                    return ls.enter_context(nc.sbuf_tensor("%s_%d" % (name, uid[0]), list(shape), dt))
                k = [0]

                def key(n):
                    return ("pp", n)
                Are = LT("Are", [128, 64]); Aim = LT("Aim", [128, 64]); ldt = LT("ldt", [128, 1])
                Bre = LT("Bre", [128, 1024]); Bim = LT("Bim", [128, 1024])
                S.dma("sp", "prep_in", [
                    lambda e: e.dma_start(out=Are[:], in_=ssm_A_re[j]),
                    lambda e: e.dma_start(out=Aim[:], in_=ssm_A_im[j]),
                    lambda e: e.dma_start(out=ldt[:], in_=ssm_log_dt[j].rearrange("(g o) -> g o", o=1)),
                    lambda e: e.dma_start(out=Bre[:], in_=ssm_B_re[j]),
                    lambda e: e.dma_start(out=Bim[:], in_=ssm_B_im[j])], writes=["pin"])
                dtt = LT("dtt", [128, 1]); lr = LT("lr", [128, 64]); li = LT("li", [128, 64])
                mag = LT("mag", [128, 64]); t1 = LT("t1", [128, 64]); t2 = LT("t2", [128, 64])
                ki = LT("ki", [128, 64], mybir.dt.int32)
                cs = [LT("cs0", [128, 64]), LT("cs1", [128, 64])]
                ab = [LT("ab0", [128, 64]), LT("ab1", [128, 64])]
                qq = [LT("q0", [128, 64]), LT("q1", [128, 64])]
                Bb = [LT("Bb0", [128, 1024]), LT("Bb1", [128, 1024])]
                tb = LT("tb", [128, 1024])
                PK = ["pin", "pw"]

                def dv(fn):
                    S.op("dve", fn, reads=PK, writes=["pw"])

                def ac(fn):
                    S.op("act", fn, reads=PK, writes=["pw"])
                ac(lambda e: e.activation(out=dtt[:], in_=ldt[:], func=AF.Exp))
                dv(lambda e: e.tensor_scalar(out=lr[:], in0=Are[:], scalar1=dtt[:, 0:1], scalar2=None, op0=ALU.mult))
                dv(lambda e: e.tensor_scalar(out=li[:], in0=Aim[:], scalar1=dtt[:, 0:1], scalar2=None, op0=ALU.mult))
                ac(lambda e: e.activation(out=mag[:], in_=lr[:], func=AF.Exp))
                for which, shift in ((0, np.pi / 2.0), (1, 0.0)):
                    dv(lambda e, shift=shift: e.tensor_scalar(out=t1[:], in0=li[:], scalar1=shift, scalar2=None, op0=ALU.add))
                    dv(lambda e: e.tensor_scalar(out=t2[:], in0=t1[:], scalar1=1.0 / TWO_PI, scalar2=None, op0=ALU.mult))
                    dv(lambda e: e.tensor_copy(out=ki[:], in_=t2[:]))
                    dv(lambda e: e.tensor_copy(out=t2[:], in_=ki[:]))
                    dv(lambda e: e.scalar_tensor_tensor(out=t1[:], in0=t2[:], scalar=-TWO_PI, in1=t1[:], op0=ALU.mult,
                                                        op1=ALU.add))
                    dv(lambda e: e.tensor_scalar(out=t2[:], in0=t1[:], scalar1=float(np.pi), scalar2=-TWO_PI,
                                                 op0=ALU.is_gt, op1=ALU.mult))
                    dv(lambda e: e.tensor_tensor(out=t1[:], in0=t1[:], in1=t2[:], op=ALU.add))
                    dv(lambda e: e.tensor_scalar(out=t1[:], in0=t1[:], scalar1=float(np.pi), scalar2=-float(np.pi),
                                                 op0=ALU.min, op1=ALU.max))
                    ac(lambda e, which=which: e.activation(out=cs[which][:], in_=t1[:], func=AF.Sin))
                dv(lambda e: e.tensor_tensor(out=ab[0][:], in0=mag[:], in1=cs[0][:], op=ALU.mult))
                dv(lambda e: e.tensor_tensor(out=ab[1][:], in0=mag[:], in1=cs[1][:], op=ALU.mult))
                dv(lambda e: e.tensor_tensor(out=t1[:], in0=Are[:], in1=Are[:], op=ALU.mult))
                dv(lambda e: e.tensor_tensor(out=t2[:], in0=Aim[:], in1=Aim[:], op=ALU.mult))
                dv(lambda e: e.tensor_tensor(out=t1[:], in0=t1[:], in1=t2[:], op=ALU.add))
                dv(lambda e: e.reciprocal(out=t1[:], in_=t1[:]))
                dv(lambda e: e.tensor_scalar(out=mag[:], in0=ab[0][:], scalar1=-1.0, scalar2=None, op0=ALU.add))
                dv(lambda e: e.tensor_tensor(out=qq[0][:], in0=mag[:], in1=Are[:], op=ALU.mult))
                dv(lambda e: e.tensor_tensor(out=t2[:], in0=ab[1][:], in1=Aim[:], op=ALU.mult))
                dv(lambda e: e.tensor_tensor(out=qq[0][:], in0=qq[0][:], in1=t2[:], op=ALU.add))
                dv(lambda e: e.tensor_tensor(out=qq[0][:], in0=qq[0][:], in1=t1[:], op=ALU.mult))
                dv(lambda e: e.tensor_tensor(out=qq[1][:], in0=ab[1][:], in1=Are[:], op=ALU.mult))
                dv(lambda e: e.tensor_tensor(out=t2[:], in0=mag[:], in1=Aim[:], op=ALU.mult))
                dv(lambda e: e.tensor_tensor(out=qq[1][:], in0=qq[1][:], in1=t2[:], op=ALU.subtract))
                dv(lambda e: e.tensor_tensor(out=qq[1][:], in0=qq[1][:], in1=t1[:], op=ALU.mult))

                def bq(t):
                    return t[:, :].unsqueeze(2).broadcast_to([128, 64, 16])

                def v3(t):
                    return t[:, :].rearrange("g (p i) -> g p i", i=16)
                dv(lambda e: e.tensor_tensor(out=v3(Bb[0]), in0=v3(Bre), in1=bq(qq[0]), op=ALU.mult))
                dv(lambda e: e.tensor_tensor(out=v3(tb), in0=v3(Bim), in1=bq(qq[1]), op=ALU.mult))
                dv(lambda e: e.tensor_tensor(out=Bb[0][:], in0=Bb[0][:], in1=tb[:], op=ALU.subtract))
                dv(lambda e: e.tensor_tensor(out=v3(Bb[1]), in0=v3(Bim), in1=bq(qq[0]), op=ALU.mult))
                dv(lambda e: e.tensor_tensor(out=v3(tb), in0=v3(Bre), in1=bq(qq[1]), op=ALU.mult))
                dv(lambda e: e.tensor_tensor(out=Bb[1][:], in0=Bb[1][:], in1=tb[:], op=ALU.add))
                S.dma("sp", "prep_o", [lambda e, r=r: e.dma_start(out=scrA[j, r], in_=ab[r][:]) for r in range(2)] +
                      [lambda e, r=r: e.dma_start(out=scrB[j, r], in_=Bb[r][:]) for r in range(2)],
                      reads=["pw"], writes=["scr1"])
                pidx = LT("pidx", [128, 1], mybir.dt.int32); pi2 = LT("pi2", [128, 1], mybir.dt.int32)
                mpar = [LT("mpar0", [128, 1]), LT("mpar1", [128, 1])]
                mhalf = [LT("mh0", [128, 1]), LT("mh1", [128, 1])]
                nmhalf = [LT("nmh0", [128, 1]), LT("nmh1", [128, 1])]
                S.op("pool", lambda e: e.iota(pidx[:], pattern=[[0, 1]], base=0, channel_multiplier=1), writes=["pidx"])
                pf = LT("pf", [128, 1]); ptmp = LT("ptmp", [128, 1])
                S.op("dve", lambda e: e.tensor_copy(out=pf[:], in_=pidx[:]), reads=["pidx", "pw"], writes=["pw"])
                dv(lambda e: e.tensor_scalar(out=mhalf[1][:], in0=pf[:], scalar1=64.0, scalar2=None, op0=ALU.is_ge))
                dv(lambda e: e.memset(mpar[1][:], 0.0))
                for m_, (thr, sg) in enumerate(((16, 1.0), (32, -1.0), (48, 1.0), (64, -1.0), (80, 1.0), (96, -1.0), (112, 1.0))):
                    dv(lambda e, thr=thr, sg=sg: e.tensor_scalar(out=ptmp[:], in0=pf[:], scalar1=float(thr), scalar2=sg,
                                                                 op0=ALU.is_ge, op1=ALU.mult))
                    dv(lambda e: e.tensor_tensor(out=mpar[1][:], in0=mpar[1][:], in1=ptmp[:], op=ALU.add))
                for mm in (mpar, mhalf):
                    dv(lambda e, mm=mm: e.tensor_scalar(out=mm[0][:], in0=mm[1][:], scalar1=-1.0, scalar2=1.0,
                                                        op0=ALU.mult, op1=ALU.add))
                for a in range(2):
                    dv(lambda e, a=a: e.tensor_scalar(out=nmhalf[a][:], in0=mhalf[a][:], scalar1=-1.0, scalar2=None,
                                                      op0=ALU.mult))
                BbT = [LT("BbT0", [128, DC, 64]), LT("BbT1", [128, DC, 64])]
                Cp = [LT("Cp0", [128, 64, 16]), LT("Cp1", [128, 64, 16])]
                with nc.allow_non_contiguous_dma(reason="ssm weight relayout"):
                    S.dma("sp", "prep_in", [lambda e, r=r: e.dma_start(
                        out=apair[j][r][:], in_=scrA[j, r].rearrange("(q a) p -> (a p) q", a=2)) for r in range(2)],
                        reads=["scr1"], writes=[("apair", j)])
                    fl = []
                    for r in range(2):
                        vB = scrB[j, r].rearrange("(f a) (p i) -> a i f p", a=8, i=16)
                        for a in range(8):
                            for f in range(DC):
                                fl.append(lambda e, r=r, a=a, vB=vB, f=f: e.dma_start(
                                    out=BbT[r][16 * a:16 * a + 16, f, :], in_=vB[a][:, f, :]))
                        vC = (ssm_C_re, ssm_C_im)[r][j].rearrange("(q a) (j p) -> a p q j", a=2, p=64)
                        for a in range(2):
                            for q in range(64):
                                fl.append(lambda e, r=r, a=a, vC=vC, q=q: e.dma_start(
                                    out=Cp[r][64 * a:64 * a + 64, q, :], in_=vC[a][:, q, :]))
                    S.dma("sp", "prep_in", fl, reads=["scr1"], writes=["pin2"])
                Bblk = [LT("Bblk0", [128, DC, 2, 64], BF16), LT("Bblk1", [128, DC, 2, 64], BF16)]
                Cblk = [LT("Cblk0", [128, 64, 2, 16], BF16), LT("Cblk1", [128, 64, 2, 16], BF16)]
                for r in range(2):
                    for a in range(2):
                        S.op("dve", lambda e, r=r, a=a: e.tensor_scalar(
                            out=Bblk[r][:, :, a, :], in0=BbT[r][:, :, :], scalar1=mpar[a][:, 0:1], scalar2=None, op0=ALU.mult),
                            reads=["pin2", "pw"], writes=["pblk"])
                        mm = mhalf if r == 0 else nmhalf
                        S.op("dve", lambda e, r=r, a=a, mm=mm: e.tensor_scalar(
                            out=Cblk[r][:, :, a, :], in0=Cp[r][:, :, :], scalar1=mm[a][:, 0:1], scalar2=None, op0=ALU.mult),
                            reads=["pin2", "pw"], writes=["pblk"])
                S.dma("sp", "prep_o", [lambda e, r=r: e.dma_start(
                    out=scrBb[j, r], in_=Bblk[r][:, :, :, :].rearrange("p f a q -> p (f a q)")) for r in range(2)] +
                    [lambda e, r=r: e.dma_start(
                        out=scrCb[j, r], in_=Cblk[r][:, :, :, :].rearrange("p f a q -> p (f a q)")) for r in range(2)],
                    reads=["pblk"], writes=[("scrblk", j)])
                for r in range(2):
                    S.op("dve", lambda e, r=r: e.memset(carry[j][r][:], 0.0), writes=[("carry", j)])
                S.barrier()

        TS = 32
        GC1 = 2.0 * float(np.sqrt(2.0 / np.pi))

        HALL = ["H", "Hc0", "Hc1"]
        TALL = ["tmS", "tmc0", "tmc1"]
        QSPLIT = 32

        def ssm_layer(p, li, j):
            blks = blocks_of(p)
            last = (p == npass - 1)
            with ExitStack() as ls:
                def LT(name, shape, dt=F32):
                    uid[0] += 1
                    return ls.enter_context(nc.sbuf_tensor("%s_%d" % (name, uid[0]), list(shape), dt))
                Bblk = [LT("Bblk0", [128, DC, 2, 64], BF16), LT("Bblk1", [128, DC, 2, 64], BF16)]
                Cblk = [LT("Cblk0", [128, 64, 2, 16], BF16), LT("Cblk1", [128, 64, 2, 16], BF16)]
                H = [LT("H0", [128, 64, 40]), LT("H1", [128, 64, 40])]
                Hb = [LT("Hb0", [128, 64, 32], BF16), LT("Hb1", [128, 64, 32], BF16)]
                tm = [LT("tm%d" % i, [128, 512]) for i in range(4)]
                h0s = [LT("h0s%d" % r, [128, 64, NSEQ]) for r in range(2)] if (last and ssm_stage >= 3) else None
                S.dma("sp", "ssm_w", [lambda e, r=r: e.dma_start(
                    out=Bblk[r][:, :, :, :].rearrange("p f a q -> p (f a q)"), in_=scrBb[j, r]) for r in range(2)] +
                    [lambda e, r=r: e.dma_start(
                        out=Cblk[r][:, :, :, :].rearrange("p f a q -> p (f a q)"), in_=scrCb[j, r]) for r in range(2)],
                    reads=[("scrblk", j)], writes=["ssmw"])
                if h0s is not None:
                    for r in range(2):
                        for pc in range(4):
                            xi = pc % 2
                            S.dma("sp", ("xin", xi), lambda e, r=r, pc=pc, xi=xi: e.dma_start(
                                out=xin[xi][0:NSEQ, :], in_=(sre, sim)[r][j][:, pc * 2048:(pc + 1) * 2048]),
                                writes=[("xin", xi)])
                            pk, pt = nextps()
                            S.op("pe", [lambda e, k=k, xi=xi, pt=pt: e.transpose(
                                pt[:, k * NSEQ:(k + 1) * NSEQ], xin[xi][0:NSEQ, k * 128:(k + 1) * 128],
                                ident[0:NSEQ, 0:NSEQ]) for k in range(16)],
                                reads=[("xin", xi), "ident"], writes=[pk])
                            S.op("act", lambda e, r=r, pc=pc, pt=pt: e.activation(
                                out=h0s[r][:, 16 * pc:16 * pc + 16, :],
                                in_=pt[:, 0:16 * NSEQ].rearrange("p (q s) -> p q s", s=NSEQ), func=AF.Copy),
                                reads=[pk], writes=["h0s"])
                rmsnorm(p, g_mix, li * DC)
                are, aim = apair[j]

                def run_chunk(col0, ncols, nseq, tlen, init_fn, final_fn):
                    HW = nseq * (1 + tlen)
                    Hv = [H[r][:, :, 0:HW].rearrange("p q (s t) -> p q s t", t=1 + tlen) for r in range(2)]
                    init_fn(Hv)
                    for r in range(2):
                        for q in range(4):
                            pk, pt = nextps()
                            fns = []
                            for fc in range(DC):
                                fns.append(lambda e, r=r, fc=fc, q=q, pt=pt: e.matmul(
                                    pt[:, fc * ncols:(fc + 1) * ncols],
                                    Bblk[r][32 * q:32 * q + 32, fc, :, :].rearrange("p a q -> p (a q)"),
                                    hn[32 * q:32 * q + 32, fc, col0:col0 + ncols], start=True, stop=True,
                                    tile_position=(32 * q, 0)))
                            S.op("pe", fns, reads=["ssmw"] + HN_KEYS, writes=[pk])
                            S.op("act", lambda e, r=r, q=q, pt=pt: e.activation(
                                out=Hv[r][:, q::4, :, 1:1 + tlen],
                                in_=pt[:, 0:DC * ncols].rearrange("p (f s t) -> p f s t", s=nseq, t=tlen), func=AF.Copy),
                                reads=[pk], writes=HALL)
                    n = 64 * nseq

                    def tv(i):
                        return tm[i][:, 0:n].rearrange("p (q s) -> p q s", s=nseq)

                    def bc(a):
                        return a[:, :].unsqueeze(2).broadcast_to([128, 64, nseq])
                    splits = [("dve", 0, QSPLIT, 0), ("pool", QSPLIT, 64, 1)] if nseq == 1 else [("dve", 0, 64, 0)]
                    for (eng_, qa, qb, ci_) in splits:
                        def tvs(i, qa=qa, qb=qb):
                            return tm[i][:, qa * nseq:qb * nseq].rearrange("p (q s) -> p q s", s=nseq)

                        def bcs(a, qa=qa, qb=qb):
                            return a[:, qa:qb].unsqueeze(2).broadcast_to([128, qb - qa, nseq])
                        chain = []
                        for t in range(tlen):
                            hr0, hi0 = Hv[0][:, qa:qb, :, t], Hv[1][:, qa:qb, :, t]
                            hr1, hi1 = Hv[0][:, qa:qb, :, t + 1], Hv[1][:, qa:qb, :, t + 1]
                            chain += [
                                lambda e, hr0=hr0, tvs=tvs, bcs=bcs: e.tensor_tensor(out=tvs(0), in0=hr0, in1=bcs(are), op=ALU.mult),
                                lambda e, hi0=hi0, tvs=tvs, bcs=bcs: e.tensor_tensor(out=tvs(1), in0=hi0, in1=bcs(aim), op=ALU.mult),
                                lambda e, hi0=hi0, tvs=tvs, bcs=bcs: e.tensor_tensor(out=tvs(2), in0=hi0, in1=bcs(are), op=ALU.mult),
                                lambda e, hr0=hr0, tvs=tvs, bcs=bcs: e.tensor_tensor(out=tvs(3), in0=hr0, in1=bcs(aim), op=ALU.mult),
                                lambda e, tvs=tvs: e.tensor_tensor(out=tvs(0), in0=tvs(0), in1=tvs(1), op=ALU.subtract),
                                lambda e, tvs=tvs: e.tensor_tensor(out=tvs(2), in0=tvs(2), in1=tvs(3), op=ALU.add),
                                lambda e, hr1=hr1, tvs=tvs: e.tensor_tensor(out=hr1, in0=hr1, in1=tvs(0), op=ALU.add),
                                lambda e, hi1=hi1, tvs=tvs: e.tensor_tensor(out=hi1, in0=hi1, in1=tvs(2), op=ALU.add),
                            ]
                        if len(splits) == 1:
                            S.op(eng_, chain, reads=HALL + TALL + [("apair", j)], writes=HALL + TALL)
                        else:
                            S.op(eng_, chain, reads=["H", "Hc%d" % ci_, "tmS", "tmc%d" % ci_, ("apair", j)],
                                 writes=["Hc%d" % ci_, "tmc%d" % ci_])
                    final_fn(Hv)
                    if ssm_stage < 2:
                        return
                    for r in range(2):
                        S.op("act", lambda e, r=r: e.activation(
                            out=Hb[r][:, :, 0:ncols].rearrange("p q (s t) -> p q s t", t=tlen),
                            in_=Hv[r][:, :, :, 1:1 + tlen], func=AF.Copy), reads=HALL, writes=["Hb"])
                    pk, pt = nextps()
                    fns = []
                    for fc in range(DC):
                        for q in range(4):
                            pr = fc * 4 + q
                            for r in range(2):
                                fns.append(lambda e, q=q, pr=pr, r=r, fc=fc, pt=pt: e.matmul(
                                    pt[32 * q:32 * q + 32, fc * ncols:(fc + 1) * ncols],
                                    Cblk[r][:, pr, :, :].rearrange("p a q -> p (a q)"), Hb[r][:, pr, 0:ncols],
                                    start=(r == 0), stop=(r == 1), tile_position=(0, 32 * q)))
                    S.op("pe", fns, reads=["ssmw", "Hb"], writes=[pk])
                    W_ = DC * ncols

                    def wv(i):
                        return tm[i][:, 0:W_].rearrange("p (f t) -> p f t", t=ncols)
                    y, z = wv(0), wv(1)
                    Dv = sDv[:, j * DC:(j + 1) * DC].unsqueeze(2).broadcast_to([128, DC, ncols])
                    S.op("dve", [
                        lambda e: e.tensor_tensor(out=y, in0=hn[:, :, col0:col0 + ncols], in1=Dv, op=ALU.mult),
                        lambda e, pt=pt: e.tensor_tensor(out=y, in0=y, in1=pt[:, 0:W_].rearrange("p (f t) -> p f t", t=ncols),
                                                         op=ALU.add),
                        lambda e: e.tensor_tensor(out=z, in0=y, in1=y, op=ALU.mult),
                        lambda e: e.tensor_scalar(out=z, in0=z, scalar1=0.044715, scalar2=1.0, op0=ALU.mult, op1=ALU.add),
                        lambda e: e.tensor_tensor(out=z, in0=z, in1=y, op=ALU.mult),
                    ], reads=[pk] + TALL + HN_KEYS + VEC, writes=TALL)
                    S.op("act", lambda e: e.activation(out=z, in_=z, func=AF.Sigmoid, scale=GC1), reads=TALL, writes=TALL)
                    S.op("dve", lambda e: e.tensor_tensor(out=cb[:, :, col0:col0 + ncols], in0=z, in1=y, op=ALU.mult),
                         reads=TALL, writes=[("cb", c) for c in range(DC)])

                nsub = NTA // TS
                for sc_ in range(nsub):
                    def init_p(Hv):
                        for r in range(2):
                            S.op("act", lambda e, r=r: e.activation(out=Hv[r][:, :, 0, 0], in_=carry[j][r][:, :], func=AF.Copy),
                                 reads=[("carry", j), "Hb"] + HALL, writes=HALL)

                    def fin_p(Hv, sc_=sc_):
                        for r in range(2):
                            S.op("act", lambda e, r=r: e.activation(out=carry[j][r][:, :], in_=Hv[r][:, :, 0, TS], func=AF.Copy),
                                 reads=HALL, writes=[("carry", j)])
                        if last and sc_ == nsub - 1 and ssm_stage >= 3:
                            with nc.allow_non_contiguous_dma(reason="ssm state out"):
                                S.dma("sp", "st_o", [lambda e, r=r: e.dma_start(
                                    out=(rep, imp)[r][j].rearrange("(q a) p -> (a p) q", a=2), in_=carry[j][r][:, :])
                                    for r in range(2)], reads=[("carry", j)])
                    run_chunk(sc_ * TS, TS, 1, TS, init_p, fin_p)
                if last and ssm_stage >= 3:
                    for hb in range(2):
                        def init_s(Hv, hb=hb):
                            for r in range(2):
                                S.op("act", lambda e, r=r: e.activation(out=Hv[r][:, :, :, 0], in_=h0s[r][:, :, 8 * hb:8 * hb + 8],
                                                                        func=AF.Copy), reads=["h0s", "Hb"] + HALL, writes=HALL)

                        def fin_s(Hv, hb=hb):
                            for r in range(2):
                                S.op("act", lambda e, r=r: e.activation(out=h0s[r][:, :, 8 * hb:8 * hb + 8], in_=Hv[r][:, :, :, 4],
                                                                        func=AF.Copy), reads=["h0s"] + HALL, writes=[("h0o", hb), "h0s"])
                        run_chunk(NTA + 32 * hb, 32, 8, 4, init_s, fin_s)
                    for r in range(2):
                        for pc in range(4):
                            xi = pc % 2
                            for g4 in range(4):
                                pk, pt = nextps()
                                S.op("pe", [lambda e, k=k, r=r, pc=pc, g4=g4, pt=pt: e.transpose(
                                    pt[0:NSEQ, k * 128:(k + 1) * 128], h0s[r][:, 16 * pc + 4 * g4 + k, :], ident[:, :])
                                    for k in range(4)], reads=["h0s", ("h0o", 0), ("h0o", 1), "ident"], writes=[pk])
                                S.op("act", lambda e, g4=g4, xi=xi, pt=pt: e.activation(
                                    out=xin[xi][0:NSEQ, g4 * 512:(g4 + 1) * 512], in_=pt[0:NSEQ, :], func=AF.Copy),
                                    reads=[pk], writes=[("xin", xi)])
                            S.dma("sp", ("xin", xi), lambda e, r=r, pc=pc, xi=xi: e.dma_start(
                                out=(res_, ims_)[r][j][:, pc * 2048:(pc + 1) * 2048], in_=xin[xi][0:NSEQ, :]),
                                reads=[("xin", xi)])
                if ssm_stage >= 4:
                    W = ssm_w_glu[j]
                    gemm(p, [(W, 0, 0), (W, 0, D)], DC, 8, lambda kc, b0, bn: cb[:, kc, b0:b0 + bn],
                         [("cb", c) for c in range(DC)], glu_residual_epi)
                S.barrier()

        def final_out(p):
            blks = blocks_of(p)
            rmsnorm(p, g_fin, 0, dst_fn=True)
            dsts = [(yp, p * NTA + r * 128, 128, r * 128) for r in range(4)]
            if p == npass - 1:
                dsts.append((ys, 0, NTB, NTA))
            for si, (dst, row0, n, col0) in enumerate(dsts):
                xi = si % 2
                for c4 in range(4):
                    tk, tt = nexttmp()
                    for cc in range(4):
                        c = 4 * c4 + cc
                        S.op("dve", lambda e, c=c, cc=cc, tt=tt, n=n, col0=col0: e.scalar_tensor_tensor(
                            out=tt[:, cc * 128:cc * 128 + n], in0=xT[:, c, col0:col0 + n], scalar=g_fin[:, c:c + 1],
                            in1=rs[:, col0:col0 + n], op0=ALU.mult, op1=ALU.mult),
                            reads=[("xT", c), "rs"] + VEC, writes=[tk])
                    pk, pt = nextps()
                    S.op("pe", [lambda e, cc=cc, n=n, pt=pt, tt=tt: e.transpose(
                        pt[0:n, cc * 128:(cc + 1) * 128], tt[:, cc * 128:cc * 128 + n], ident[:, :])
                        for cc in range(4)], reads=[tk, "ident"], writes=[pk])
                    S.op("act", lambda e, c4=c4, n=n, xi=xi, pt=pt: e.activation(
                        out=xin[xi][0:n, c4 * 512:(c4 + 1) * 512], in_=pt[0:n, :], func=AF.Copy),
                        reads=[pk], writes=[("xin", xi)])
                S.dma("sp", ("xin", xi), lambda e, dst=dst, row0=row0, n=n, xi=xi: e.dma_start(
                    out=dst[row0:row0 + n, :], in_=xin[xi][0:n, :]), reads=[("xin", xi)])

        for li in layers:
            if li % 2 == 1:
                ssm_prep(li // 2)
        for p in range(npass):
            load_x(p)
            for li in layers:
                if li % 2 == 0:
                    conv_layer(p, li, li // 2)
                elif ssm_stage >= 1:
                    ssm_layer(p, li, li // 2)
                ffn_layer(p, li)
            final_out(p)
        S.final_wait("sp")
    return nc


_NC_CACHE = {}


def kernel(**inputs):
    layers = tuple(int(c) for c in os.environ.get("K_LAYERS", "0123"))
    stage = int(os.environ.get("K_SSM_STAGE", "4"))
    key = (layers, stage)
    if key not in _NC_CACHE:
        _NC_CACHE[key] = build_nc(layers=layers, ssm_stage=stage)
    nc = _NC_CACHE[key]
    f = lambda a: np.ascontiguousarray(np.asarray(a, dtype=np.float32))
    inp = {k: f(v) for k, v in inputs.items()}
    in_maps = []
    for c in range(NCORES):
        b = c % 4
        sl = slice(NSEQ * c, NSEQ * c + NSEQ)
        m = {
            "xp": inp["x_prompt"][b],
            "xs": f(inp["x_sample"][sl].reshape(NTB, D)),
            "sconv": f(inp["state_conv"][:, sl]),
            "sre": f(inp["state_ssm_re"][:, sl].reshape(2, NSEQ, 8192)),
            "sim": f(inp["state_ssm_im"][:, sl].reshape(2, NSEQ, 8192)),
            "sffn": f(inp["state_ffn"][:, sl].reshape(4, NSEQ * 2, FF)),
            "norm_final": inp["norm_final"].reshape(1, D),
            "ssm_B_re": inp["ssm_B_re"].reshape(2, 128, 1024),
            "ssm_B_im": inp["ssm_B_im"].reshape(2, 128, 1024),
            "ssm_C_re": inp["ssm_C_re"].reshape(2, 128, 1024),
            "ssm_C_im": inp["ssm_C_im"].reshape(2, 128, 1024),
        }
        for k in ("norm_mix", "norm_ffn", "conv_w_in", "conv_dw", "conv_dw_b", "conv_ln_g", "conv_ln_b",
                  "conv_w_out", "ssm_A_re", "ssm_A_im", "ssm_log_dt", "ssm_D", "ssm_w_glu", "ffn_w_gate",
                  "ffn_w_up", "ffn_conv", "ffn_w_down"):
            m[k] = inp[k]
        in_maps.append(m)
    res = run_bass_kernel_spmd(nc, in_maps, core_ids=list(range(NCORES)))
    R = res.results
    y_prompt = np.stack([R[b]["yp"] for b in range(4)])
    y_sample = np.concatenate([R[c]["ys"].reshape(NSEQ, 4, D) for c in range(NCORES)], axis=0)
    conv_p = np.stack([R[b]["convp"] for b in range(4)], axis=1)
    conv_s = np.concatenate([R[c]["convs"] for c in range(NCORES)], axis=1)
    re_p = np.stack([R[b]["rep"] for b in range(4)], axis=1)
    im_p = np.stack([R[b]["imp"] for b in range(4)], axis=1)
    re_s = np.concatenate([R[c]["res"].reshape(2, NSEQ, 128, 64) for c in range(NCORES)], axis=1)
    im_s = np.concatenate([R[c]["ims"].reshape(2, NSEQ, 128, 64) for c in range(NCORES)], axis=1)
    ffn_p = np.stack([R[b]["ffnp"] for b in range(4)], axis=1)
    ffn_s = np.concatenate([R[c]["ffns"].reshape(4, NSEQ, 2, FF) for c in range(NCORES)], axis=1)
    return (y_prompt, y_sample, conv_p, conv_s, re_p, im_p, re_s, im_s, ffn_p, ffn_s)
```

```python
import os
import numpy as np
from contextlib import ExitStack
import concourse.bass as bass
import concourse.mybir as mybir
from concourse.bass_utils import run_bass_kernel_spmd

F32 = mybir.dt.float32
BF16 = mybir.dt.bfloat16
AF = mybir.ActivationFunctionType
ALU = mybir.AluOpType
AX = mybir.AxisListType

D = 2048
DC = 16
FF = 5632
FCH = 44
NPASS = 4
NTA = 512
NSEQ = 16
NTB = 64
NT = NTA + NTB
DEPTH = 4
EPS = 1e-6
NCORES = 8


class Sched:
    def __init__(self, nc, stack, n_dma_sems=24):
        self.nc = nc
        self.eng = {"pe": nc.tensor, "act": nc.scalar, "dve": nc.vector,
                    "pool": nc.gpsimd, "sp": nc.sync}
        self.sem = {}
        self.cnt = {}
        for e in self.eng:
            self.sem[e] = stack.enter_context(nc.semaphore("s_" + e))
            self.cnt[e] = 0
        self.dma_free = []
        for i in range(n_dma_sems):
            k = "dma%d" % i
            self.sem[k] = stack.enter_context(nc.semaphore("s_" + k))
            self.cnt[k] = 0
            self.dma_free.append(k)
        self.slot_sem = {}
        self.waited = {e: {} for e in self.eng}
        self.last_w = {}
        self.reads = {}

    def _deps(self, reads, writes):
        deps = {}

        def add(d):
            if d is None:
                return
            k, n = d
            if deps.get(k, 0) < n:
                deps[k] = n
        for b in reads:
            add(self.last_w.get(b))
        for b in writes:
            add(self.last_w.get(b))
            for d in self.reads.get(b, ()):
                add(d)
        return deps

    def _emit_waits(self, e, deps):
        for k, n in deps.items():
            if k == e and e == "pe":
                continue
            if self.waited[e].get(k, 0) >= n:
                continue
            self.eng[e].wait_ge(self.sem[k], n)
            self.waited[e][k] = n

    def _record(self, key, n, reads, writes):
        for b in writes:
            self.last_w[b] = (key, n)
            self.reads[b] = []
        for b in reads:
            lst = self.reads.setdefault(b, [])
            lst.append((key, n))
            if len(lst) > 8:
                m = {}
                for k2, n2 in lst:
                    m[k2] = max(m.get(k2, 0), n2)
                self.reads[b] = list(m.items())

    def op(self, e, fns, reads=(), writes=()):
        if callable(fns):
            fns = [fns]
        deps = self._deps(reads, writes)
        self._emit_waits(e, deps)
        eng = self.eng[e]
        ins = None
        for f in fns:
            ins = f(eng)
        self.cnt[e] += 1
        ins.then_inc(self.sem[e], 1)
        self._record(e, self.cnt[e], reads, writes)

    def dma(self, e, slot, fns, reads=(), writes=()):
        if callable(fns):
            fns = [fns]
        if slot not in self.slot_sem:
            self.slot_sem[slot] = self.dma_free.pop(0)
        k = self.slot_sem[slot]
        deps = self._deps(reads, writes)
        self._emit_waits(e, deps)
        eng = self.eng[e]
        for f in fns:
            ins = f(eng)
            self.cnt[k] += 16
            ins.then_inc(self.sem[k], 16)
        self._record(k, self.cnt[k], reads, writes)

    def barrier(self):
        for e in self.eng:
            deps = {k: self.cnt[k] for k in self.sem if self.cnt[k] > 0}
            self._emit_waits(e, deps)

    def final_wait(self, e="sp"):
        deps = {k: self.cnt[k] for k in self.sem if self.cnt[k] > 0}
        self._emit_waits(e, deps)


def build_nc(layers=(0, 1, 2, 3), npass=NPASS, ssm_stage=4):
    nc = bass.Bass("TRN2", target_bir_lowering=False)

    def din(name, shape):
        return nc.dram_tensor(name, list(shape), F32, kind="ExternalInput").ap()

    def dout(name, shape):
        return nc.dram_tensor(name, list(shape), F32, kind="ExternalOutput").ap()

    xp = din("xp", [2048, D])
    xs = din("xs", [NTB, D])
    sconv = din("sconv", [2, NSEQ, 30, D])
    sre = din("sre", [2, NSEQ, 8192])
    sim = din("sim", [2, NSEQ, 8192])
    sffn = din("sffn", [4, NSEQ * 2, FF])
    norm_mix = din("norm_mix", [4, D])
    norm_ffn = din("norm_ffn", [4, D])
    norm_final = din("norm_final", [1, D])
    conv_w_in = din("conv_w_in", [2, D, 2 * D])
    conv_dw = din("conv_dw", [2, 31, D])
    conv_dw_b = din("conv_dw_b", [2, D])
    conv_ln_g = din("conv_ln_g", [2, D])
    conv_ln_b = din("conv_ln_b", [2, D])
    conv_w_out = din("conv_w_out", [2, D, D])
    ssm_A_re = din("ssm_A_re", [2, 128, 64])
    ssm_A_im = din("ssm_A_im", [2, 128, 64])
    ssm_log_dt = din("ssm_log_dt", [2, 128])
    ssm_B_re = din("ssm_B_re", [2, 128, 1024])
    ssm_B_im = din("ssm_B_im", [2, 128, 1024])
    ssm_C_re = din("ssm_C_re", [2, 128, 1024])
    ssm_C_im = din("ssm_C_im", [2, 128, 1024])
    ssm_D = din("ssm_D", [2, D])
    ssm_w_glu = din("ssm_w_glu", [2, D, 2 * D])
    ffn_w_gate = din("ffn_w_gate", [4, D, FF])
    ffn_w_up = din("ffn_w_up", [4, D, FF])
    ffn_conv = din("ffn_conv", [4, 3, FF])
    ffn_w_down = din("ffn_w_down", [4, FF, D])

    yp = dout("yp", [2048, D])
    ys = dout("ys", [NTB, D])
    convp = dout("convp", [2, 30, D])
    convs = dout("convs", [2, NSEQ, 30, D])
    rep = dout("rep", [2, 128, 64])
    imp = dout("imp", [2, 128, 64])
    res_ = dout("res", [2, NSEQ, 8192])
    ims_ = dout("ims", [2, NSEQ, 8192])
    ffnp = dout("ffnp", [4, 2, FF])
    ffns = dout("ffns", [4, NSEQ * 2, FF])

    scrA = nc.dram_tensor("scrA", [2, 2, 128, 64], F32, kind="Internal").ap()
    scrB = nc.dram_tensor("scrB", [2, 2, 128, 1024], F32, kind="Internal").ap()
    scrBb = nc.dram_tensor("scrBb", [2, 2, 128, DC * 128], BF16, kind="Internal").ap()
    scrCb = nc.dram_tensor("scrCb", [2, 2, 128, 64 * 32], BF16, kind="Internal").ap()

    with ExitStack() as st:
        S = Sched(nc, st, n_dma_sems=40)

        def T(name, shape, dt=F32):
            return st.enter_context(nc.sbuf_tensor(name, list(shape), dt))

        xT = T("xT", [128, DC, NT])
        hn = T("hn", [128, DC, NT], BF16)
        cb = T("cb", [128, DC, NT], BF16)
        rs = T("rs", [128, NT])
        rs2 = T("rs2", [128, NT])
        tmpA = [T("tmpA%d" % i, [128, NT]) for i in range(3)]
        xin = [T("xin%d" % i, [128, D]) for i in range(2)]
        NWS = 4
        wsl = [T("wsl%d" % i, [128, 16, 256], BF16) for i in range(NWS)]
        ident = T("ident", [128, 128])
        identb = T("identb", [128, 128], BF16)
        ones = T("ones", [128, 128])
        onesb = T("onesb", [128, 128], BF16)
        epst = T("epst", [128, 1])
        g_mix = T("g_mix", [128, 64])
        g_ffn = T("g_ffn", [128, 64])
        g_fin = T("g_fin", [128, 16])
        cdwb = T("cdwb", [128, 32])
        clng = T("clng", [128, 32])
        clnb = T("clnb", [128, 32])
        sDv = T("sDv", [128, 32])
        fcw = T("fcw", [128, 4 * 3 * FCH])
        dwT = T("dwT", [128, 2 * 31 * DC])
        halo_c = [T("halo_c%d" % j, [128, DC, 30], BF16) for j in range(2)]
        halo_f = [T("halo_f%d" % i, [128, FCH, 2]) for i in range(4)]
        stg = T("stg", [128, 128])
        apair = [[T("apair%d%d" % (j, r), [128, 64]) for r in range(2)] for j in range(2)]
        carry = [[T("carry%d%d" % (j, r), [128, 64]) for r in range(2)] for j in range(2)]
        ps = [st.enter_context(nc.psum_tensor("ps%d" % i, [128, 512], F32)) for i in range(8)]
        psn = [0]
        uid = [0]

        def nextps():
            i = psn[0] % 8
            psn[0] += 1
            return ("ps", i), ps[i]

        wsn = [0]

        def nextws():
            i = wsn[0] % NWS
            wsn[0] += 1
            return ("ws", i), wsl[i]

        tan = [0]

        def nexttmp():
            i = tan[0] % 3
            tan[0] += 1
            return ("tmpA", i), tmpA[i]

        S.op("pool", lambda e: e.memset(ones[:], 1.0), writes=["ones"])
        S.op("pool", lambda e: e.memset(ident[:], 1.0), writes=["ident"])
        S.op("pool", lambda e: e.affine_select(out=ident[:], in_=ident[:], pattern=[[1, 128]],
                                               compare_op=ALU.is_equal, fill=0.0, base=0,
                                               channel_multiplier=-1),
             reads=["ident"], writes=["ident"])
        S.op("dve", lambda e: e.tensor_copy(out=identb[:], in_=ident[:]), reads=["ident"], writes=["identb"])
        S.op("dve", lambda e: e.memset(epst[:], EPS), writes=["epst"])
        S.op("dve", lambda e: e.memset(onesb[:], 1.0), writes=["onesb"])
        for j in range(2):
            S.op("dve", lambda e, j=j: e.memset(halo_c[j][:], 0.0), writes=[("halo_c", j)])
        for i in range(4):
            S.op("dve", lambda e, i=i: e.memset(halo_f[i][:], 0.0), writes=[("halo_f", i)])

        def load_vecT(src_rows_ap, nrows, dst, dst_col0):
            r0 = 0
            while r0 < nrows:
                n = min(128, nrows - r0)
                S.dma("sp", "stg", lambda e, r0=r0, n=n: e.dma_start(out=stg[0:n, :], in_=src_rows_ap[r0:r0 + n, :]),
                      writes=["stg"])
                pk, pt = nextps()
                S.op("pe", lambda e, n=n, pt=pt: e.transpose(pt[:, 0:n], stg[0:n, :], ident[0:n, 0:n]),
                     reads=["stg", "ident"], writes=[pk])
                S.op("act", lambda e, n=n, pt=pt, r0=r0: e.activation(out=dst[:, dst_col0 + r0:dst_col0 + r0 + n],
                                                                   in_=pt[:, 0:n], func=AF.Copy),
                     reads=[pk], writes=[("vec", id(dst))])
                r0 += n

        load_vecT(norm_mix.rearrange("l (c p) -> (l c) p", p=128), 64, g_mix, 0)
        load_vecT(norm_ffn.rearrange("l (c p) -> (l c) p", p=128), 64, g_ffn, 0)
        load_vecT(norm_final.rearrange("l (c p) -> (l c) p", p=128), 16, g_fin, 0)
        load_vecT(conv_dw_b.rearrange("l (c p) -> (l c) p", p=128), 32, cdwb, 0)
        load_vecT(conv_ln_g.rearrange("l (c p) -> (l c) p", p=128), 32, clng, 0)
        load_vecT(conv_ln_b.rearrange("l (c p) -> (l c) p", p=128), 32, clnb, 0)
        load_vecT(ssm_D.rearrange("l (c p) -> (l c) p", p=128), 32, sDv, 0)
        load_vecT(ffn_conv.rearrange("l k (c p) -> (l k c) p", p=128), 4 * 3 * FCH, fcw, 0)
        load_vecT(conv_dw.rearrange("l k (c p) -> (l k c) p", p=128), 2 * 31 * DC, dwT, 0)
        VEC = [("vec", id(t)) for t in (g_mix, g_ffn, g_fin, cdwb, clng, clnb, sDv, fcw, dwT)]

        def blocks_of(p):
            return [(0, NTA)] + ([(NTA, NTB)] if p == npass - 1 else [])

        def load_x(p):
            srcs = [(xp, p * NTA + r * 128, 128, r * 128) for r in range(4)]
            if p == npass - 1:
                srcs.append((xs, 0, NTB, NTA))
            for si, (src, row0, n, col0) in enumerate(srcs):
                xi = si % 2
                S.dma("sp", ("xin", xi), lambda e, src=src, row0=row0, n=n, xi=xi:
                      e.dma_start(out=xin[xi][0:n, :], in_=src[row0:row0 + n, :]), writes=[("xin", xi)])
                for c4 in range(4):
                    pk, pt = nextps()
                    S.op("pe", [lambda e, c=c, n=n, xi=xi, pt=pt, c4=c4: e.transpose(
                        pt[:, (c - 4 * c4) * 128:(c - 4 * c4) * 128 + n], xin[xi][0:n, c * 128:(c + 1) * 128],
                        ident[0:n, 0:n]) for c in range(4 * c4, 4 * c4 + 4)],
                        reads=[("xin", xi), "ident"], writes=[pk])
                    S.op("act", lambda e, c4=c4, n=n, col0=col0, pt=pt: e.activation(
                        out=xT[:, 4 * c4:4 * c4 + 4, col0:col0 + n],
                        in_=pt[:, :].rearrange("p (c t) -> p c t", t=128)[:, :, 0:n], func=AF.Copy),
                        reads=[pk], writes=[("xT", c) for c in range(4 * c4, 4 * c4 + 4)])

        def rmsnorm(p, gain, gcol0, dst_fn=None, dst_key="hn"):
            blks = blocks_of(p)
            ncol = blks[-1][0] + blks[-1][1]
            pks = [nextps() for _ in blks]
            for c in range(DC):
                tk, tt = nexttmp()
                S.op("act", lambda e, c=c, tt=tt: e.activation(out=tt[:, 0:ncol], in_=xT[:, c, 0:ncol], func=AF.Square),
                     reads=[("xT", c)], writes=[tk])
                for (pk, pt), (b0, bn) in zip(pks, blks):
                    S.op("pe", lambda e, c=c, pt=pt, b0=b0, bn=bn, tt=tt: e.matmul(
                        pt[:, 0:bn], ones[:], tt[:, b0:b0 + bn], start=(c == 0), stop=(c == DC - 1)),
                        reads=[tk, "ones"], writes=[pk])
            for (pk, pt), (b0, bn) in zip(pks, blks):
                S.op("act", lambda e, pt=pt, b0=b0, bn=bn: e.activation(
                    out=rs[:, b0:b0 + bn], in_=pt[:, 0:bn], func=AF.Sqrt, scale=1.0 / D, bias=epst[:, 0:1]),
                    reads=[pk, "epst"], writes=["rs"])
            S.op("dve", lambda e: e.reciprocal(out=rs[:, 0:ncol], in_=rs[:, 0:ncol]), reads=["rs"], writes=["rs"])
            if dst_fn is None:
                for c in range(DC):
                    S.op("dve", lambda e, c=c: e.scalar_tensor_tensor(
                        out=hn[:, c, 0:ncol], in0=xT[:, c, 0:ncol], scalar=gain[:, gcol0 + c:gcol0 + c + 1],
                        in1=rs[:, 0:ncol], op0=ALU.mult, op1=ALU.mult),
                        reads=[("xT", c), "rs"] + VEC, writes=[(dst_key, c)])

        pre_loaded = {}

        def _load_group(srcs, kc_n, g):
            lst = []
            for (W, row0, col0) in srcs:
                wk, wt = nextws()
                S.dma("pool", wk, lambda e, W=W, row0=row0, col0=col0, wt=wt, g=g: e.dma_start(
                    out=wt[:, 0:kc_n, :],
                    in_=W[row0:row0 + kc_n * 128, col0 + 256 * g:col0 + 256 * g + 256].rearrange(
                        "(k p) m -> p k m", p=128)), writes=[wk])
                lst.append((wk, wt))
            return lst

        def gemm(p, srcs, kc_n, ngroups, rhs_fn, rhs_keys, epi, tag=None, nxt=None):
            blks = blocks_of(p)
            loaded = {}
            if tag is not None and tag in pre_loaded:
                loaded[0] = pre_loaded.pop(tag)
            else:
                loaded[0] = _load_group(srcs, kc_n, 0)
            for g in range(ngroups):
                if g + 1 < ngroups:
                    loaded[g + 1] = _load_group(srcs, kc_n, g + 1)
                elif nxt is not None and len(srcs) + len(nxt[1]) <= NWS:
                    pre_loaded[nxt[0]] = _load_group(nxt[1], nxt[2], 0)
                for ci in range(2):
                    for bi, (b0, bn) in enumerate(blks):
                        pls = []
                        for (wk, wt) in loaded[g]:
                            pk, pt = nextps()
                            S.op("pe", [lambda e, kc=kc, wt=wt, pt=pt, b0=b0, bn=bn, ci=ci: e.matmul(
                                pt[:, 0:bn], wt[:, kc, ci * 128:(ci + 1) * 128], rhs_fn(kc, b0, bn),
                                start=(kc == 0), stop=(kc == kc_n - 1)) for kc in range(kc_n)],
                                reads=[wk] + rhs_keys, writes=[pk])
                            pls.append((pk, pt))
                        epi(g, ci, bi, b0, bn, pls)
                del loaded[g]

        def glu_residual_epi(g, ci, bi, b0, bn, pls):
            mc = 2 * g + ci
            (pkv, ptv), (pkg, ptg) = pls
            tk, tt = nexttmp()
            S.op("act", lambda e: e.activation(out=tt[:, 0:bn], in_=ptg[:, 0:bn], func=AF.Sigmoid),
                 reads=[pkg], writes=[tk])
            S.op("dve", lambda e: e.tensor_tensor(out=tt[:, 0:bn], in0=ptv[:, 0:bn], in1=tt[:, 0:bn], op=ALU.mult),
                 reads=[pkv, tk], writes=[tk])
            S.op("dve", lambda e: e.tensor_tensor(out=xT[:, mc, b0:b0 + bn], in0=xT[:, mc, b0:b0 + bn],
                                                  in1=tt[:, 0:bn], op=ALU.add),
                 reads=[tk, ("xT", mc)], writes=[("xT", mc)])

        def residual_epi(g, ci, bi, b0, bn, pls):
            mc = 2 * g + ci
            (pk, pt), = pls
            S.op("dve", lambda e: e.tensor_tensor(out=xT[:, mc, b0:b0 + bn], in0=xT[:, mc, b0:b0 + bn],
                                                  in1=pt[:, 0:bn], op=ALU.add),
                 reads=[pk, ("xT", mc)], writes=[("xT", mc)])

        HN_KEYS = [("hn", c) for c in range(DC)]

        def conv_layer(p, li, j):
            blks = blocks_of(p)
            last = (p == npass - 1)
            ncol = blks[-1][0] + blks[-1][1]
            with ExitStack() as ls:
                def LT(name, shape, dt=F32):
                    uid[0] += 1
                    return ls.enter_context(nc.sbuf_tensor("%s_%d" % (name, uid[0]), list(shape), dt))
                uxp = LT("uxp", [128, DC, 30 + NTA], BF16)
                uxs = LT("uxs", [128, DC, NSEQ, 34], BF16)
                dg = [LT("dg%d" % i, [128, 31, 128], BF16) for i in range(1)]
                ulast = LT("ulast", [128, DC, 94])
                rmsnorm(p, g_mix, li * DC)
                for c in range(DC):
                    S.op("act", lambda e, c=c: e.activation(out=uxp[:, c, 0:30], in_=halo_c[j][:, c, :], func=AF.Copy),
                         reads=[("halo_c", j)], writes=[("uxp", c)])
                if last:
                    for r4 in range(4):
                        xi = r4 % 2
                        S.dma("sp", ("xin", xi), lambda e, r4=r4, xi=xi: e.dma_start(
                            out=xin[xi][0:120, :],
                            in_=sconv[j, 4 * r4:4 * r4 + 4].rearrange("s r d -> (s r) d")), writes=[("xin", xi)])
                        for c4 in range(4):
                            pk, pt = nextps()
                            S.op("pe", [lambda e, c=c, xi=xi, pt=pt, c4=c4: e.transpose(
                                pt[:, (c - 4 * c4) * 128:(c - 4 * c4) * 128 + 120], xin[xi][0:120, c * 128:(c + 1) * 128],
                                ident[0:120, 0:120]) for c in range(4 * c4, 4 * c4 + 4)],
                                reads=[("xin", xi), "ident"], writes=[pk])
                            for cc in range(4):
                                c = 4 * c4 + cc
                                S.op("act", lambda e, c=c, cc=cc, r4=r4, pt=pt: e.activation(
                                    out=uxs[:, c, 4 * r4:4 * r4 + 4, 0:30],
                                    in_=pt[:, cc * 128:cc * 128 + 120].rearrange("p (s r) -> p s r", r=30),
                                    func=AF.Copy), reads=[pk], writes=[("uxs", c)])
                    S.dma("sp", "cs_copy", lambda e: e.dma_start(out=convs[j, :, 0:26, :], in_=sconv[j, :, 4:30, :]))

                def glu_u_epi(g, ci, bi, b0, bn, pls):
                    mc = 2 * g + ci
                    (pkv, ptv), (pkg, ptg) = pls
                    tk, tt = nexttmp()
                    S.op("act", lambda e: e.activation(out=tt[:, 0:bn], in_=ptg[:, 0:bn], func=AF.Sigmoid),
                         reads=[pkg], writes=[tk])
                    if bi == 0:
                        S.op("dve", lambda e: e.tensor_tensor(out=uxp[:, mc, 30:30 + NTA], in0=ptv[:, 0:NTA],
                                                              in1=tt[:, 0:NTA], op=ALU.mult),
                             reads=[pkv, tk], writes=[("uxp", mc)])
                        S.op("act", lambda e: e.activation(out=halo_c[j][:, mc, :], in_=uxp[:, mc, NTA:NTA + 30],
                                                           func=AF.Copy),
                             reads=[("uxp", mc)], writes=[("halo_c", j)])
                        if last:
                            S.op("dve", lambda e: e.tensor_tensor(out=ulast[:, mc, 0:30], in0=ptv[:, NTA - 30:NTA],
                                                                  in1=tt[:, NTA - 30:NTA], op=ALU.mult),
                                 reads=[pkv, tk], writes=[("ulast", mc)])
                    else:
                        S.op("dve", lambda e: e.tensor_tensor(
                            out=uxs[:, mc, :, 30:34], in0=ptv[:, 0:NTB].rearrange("p (s t) -> p s t", t=4),
                            in1=tt[:, 0:NTB].rearrange("p (s t) -> p s t", t=4), op=ALU.mult),
                            reads=[pkv, tk], writes=[("uxs", mc)])
                        S.op("dve", lambda e: e.tensor_tensor(out=ulast[:, mc, 30:94], in0=ptv[:, 0:NTB],
                                                              in1=tt[:, 0:NTB], op=ALU.mult),
                             reads=[pkv, tk], writes=[("ulast", mc)])

                W = conv_w_in[j]
                gemm(p, [(W, 0, 0), (W, 0, D)], DC, 8, lambda kc, b0, bn: hn[:, kc, b0:b0 + bn], HN_KEYS, glu_u_epi)

                if last:
                    for c4 in range(4):
                        pk, pt = nextps()
                        S.op("pe", [lambda e, c=c, pt=pt, c4=c4: e.transpose(
                            pt[0:94, (c - 4 * c4) * 128:(c - 4 * c4 + 1) * 128], ulast[:, c, :], ident[:, :])
                            for c in range(4 * c4, 4 * c4 + 4)],
                            reads=[("ulast", c) for c in range(4 * c4, 4 * c4 + 4)] + ["ident"], writes=[pk])
                        S.op("act", lambda e, c4=c4, pt=pt: e.activation(out=xin[0][0:94, c4 * 512:(c4 + 1) * 512],
                                                                      in_=pt[0:94, :], func=AF.Copy),
                             reads=[pk], writes=[("xin", 0)])
                    S.dma("sp", ("xin", 0), [lambda e: e.dma_start(out=convp[j], in_=xin[0][0:30, :])] +
                          [lambda e, s=s: e.dma_start(out=convs[j, s, 26:30, :], in_=xin[0][30 + 4 * s:34 + 4 * s, :])
                           for s in range(NSEQ)], reads=[("xin", 0)])

                for c in range(DC):
                    di = 0
                    S.op("dve", lambda e, c=c, di=di: e.tensor_tensor(
                        out=dg[di][:, :, :], in0=identb[:, :].unsqueeze(1).broadcast_to([128, 31, 128]),
                        in1=dwT[:, (j * 31) * DC + c:(j * 31 + 31) * DC:DC].unsqueeze(2).broadcast_to([128, 31, 128]),
                        op=ALU.mult), reads=["identb"] + VEC, writes=[("dg", di)])
                    for bi, (b0, bn) in enumerate(blks):
                        pk, pt = nextps()
                        if bi == 0:
                            S.op("pe", [lambda e, k=k, c=c, di=di, pt=pt: e.matmul(
                                pt[:, 0:NTA], dg[di][:, k, :], uxp[:, c, k:k + NTA], start=(k == 0), stop=(k == 30))
                                for k in range(31)], reads=[("dg", di), ("uxp", c)], writes=[pk])
                        else:
                            S.op("pe", [lambda e, k=k, c=c, di=di, pt=pt: e.matmul(
                                pt[:, 0:NTB].rearrange("p (s t) -> p s t", t=4), dg[di][:, k, :], uxs[:, c, :, k:k + 4],
                                start=(k == 0), stop=(k == 30)) for k in range(31)],
                                reads=[("dg", di), ("uxs", c)], writes=[pk])
                        S.op("act", lambda e, c=c, pt=pt, b0=b0, bn=bn: e.activation(
                            out=cb[:, c, b0:b0 + bn], in_=pt[:, 0:bn], func=AF.Identity,
                            bias=cdwb[:, j * DC + c:j * DC + c + 1]), reads=[pk] + VEC, writes=[("cb", c)])
                pk1 = [nextps() for _ in blks]
                pk2 = [nextps() for _ in blks]
                for c in range(DC):
                    tk, tt = nexttmp()
                    S.op("act", lambda e, c=c, tt=tt: e.activation(out=tt[:, 0:ncol], in_=cb[:, c, 0:ncol], func=AF.Square),
                         reads=[("cb", c)], writes=[tk])
                    for (pka, pta), (pkb, ptb), (b0, bn) in zip(pk1, pk2, blks):
                        S.op("pe", lambda e, c=c, pta=pta, b0=b0, bn=bn: e.matmul(
                            pta[:, 0:bn], onesb[:], cb[:, c, b0:b0 + bn], start=(c == 0), stop=(c == DC - 1)),
                            reads=[("cb", c), "onesb"], writes=[pka])
                        S.op("pe", lambda e, c=c, ptb=ptb, b0=b0, bn=bn, tt=tt: e.matmul(
                            ptb[:, 0:bn], ones[:], tt[:, b0:b0 + bn], start=(c == 0), stop=(c == DC - 1)),
                            reads=[tk, "ones"], writes=[pkb])
                for (pka, pta), (pkb, ptb), (b0, bn) in zip(pk1, pk2, blks):
                    S.op("act", lambda e, pta=pta, b0=b0, bn=bn: e.activation(
                        out=rs2[:, b0:b0 + bn], in_=pta[:, 0:bn], func=AF.Copy, scale=1.0 / D), reads=[pka], writes=["rs2"])
                    tk, tt = nexttmp()
                    S.op("dve", lambda e, tt=tt, b0=b0, bn=bn: e.tensor_tensor(
                        out=tt[:, 0:bn], in0=rs2[:, b0:b0 + bn], in1=rs2[:, b0:b0 + bn], op=ALU.mult),
                        reads=["rs2"], writes=[tk])
                    S.op("dve", lambda e, tt=tt, ptb=ptb, b0=b0, bn=bn: e.scalar_tensor_tensor(
                        out=tt[:, 0:bn], in0=ptb[:, 0:bn], scalar=1.0 / D, in1=tt[:, 0:bn], op0=ALU.mult,
                        op1=ALU.subtract), reads=[pkb, tk], writes=[tk])
                    S.op("act", lambda e, tt=tt, b0=b0, bn=bn: e.activation(
                        out=rs[:, b0:b0 + bn], in_=tt[:, 0:bn], func=AF.Sqrt, bias=epst[:, 0:1]),
                        reads=[tk, "epst"], writes=["rs"])
                S.op("dve", lambda e: e.reciprocal(out=rs[:, 0:ncol], in_=rs[:, 0:ncol]), reads=["rs"], writes=["rs"])
                for c in range(DC):
                    tk, tt = nexttmp()
                    S.op("dve", lambda e, c=c, tt=tt: e.tensor_tensor(out=tt[:, 0:ncol], in0=cb[:, c, 0:ncol],
                                                               in1=rs2[:, 0:ncol], op=ALU.subtract),
                         reads=[("cb", c), "rs2"], writes=[tk])
                    S.op("dve", lambda e, c=c, tt=tt: e.tensor_tensor(out=tt[:, 0:ncol], in0=tt[:, 0:ncol],
                                                               in1=rs[:, 0:ncol], op=ALU.mult),
                         reads=[tk, "rs"], writes=[tk])
                    S.op("act", lambda e, c=c, tt=tt: e.activation(
                        out=hn[:, c, 0:ncol], in_=tt[:, 0:ncol], func=AF.Silu,
                        scale=clng[:, j * DC + c:j * DC + c + 1], bias=clnb[:, j * DC + c:j * DC + c + 1]),
                        reads=[tk] + VEC, writes=[("hn", c)])
                gemm(p, [(conv_w_out[j], 0, 0)], DC, 8, lambda kc, b0, bn: hn[:, kc, b0:b0 + bn], HN_KEYS, residual_epi)
                S.barrier()

        def ffn_layer(p, li):
            blks = blocks_of(p)
            last = (p == npass - 1)
            ncol = blks[-1][0] + blks[-1][1]
            QS = [(0, 12), (12, 12), (24, 12), (36, 8)]
            with ExitStack() as ls:
                def LT(name, shape, dt=F32):
                    uid[0] += 1
                    return ls.enter_context(nc.sbuf_tensor("%s_%d" % (name, uid[0]), list(shape), dt))
                aT = LT("aT", [128, 12, NT], BF16)
                gxp = [LT("gxp%d" % i, [128, 2 + NTA]) for i in range(3)]
                gxs = [LT("gxs%d" % i, [128, NSEQ, 6]) for i in range(3)]
                gcs = LT("gcs", [128, FCH, NSEQ, 2]) if last else None
                glast = LT("glast", [128, FCH, 34]) if last else None
                rmsnorm(p, g_ffn, li * DC)
                if last:
                    for h in range(3):
                        c0 = h * 2048
                        cn_ = min(2048, FF - c0)
                        xi = h % 2
                        S.dma("sp", ("xin", xi), lambda e, c0=c0, cn_=cn_, xi=xi: e.dma_start(
                            out=xin[xi][0:32, 0:cn_], in_=sffn[li, :, c0:c0 + cn_]), writes=[("xin", xi)])
                        for c4 in range(cn_ // 512):
                            pk, pt = nextps()
                            S.op("pe", [lambda e, cc=cc, xi=xi, pt=pt, c4=c4: e.transpose(
                                pt[:, cc * 128:cc * 128 + 32], xin[xi][0:32, (4 * c4 + cc) * 128:(4 * c4 + cc + 1) * 128],
                                ident[0:32, 0:32]) for cc in range(4)],
                                reads=[("xin", xi), "ident"], writes=[pk])
                            ch0 = h * 16 + 4 * c4
                            S.op("act", lambda e, ch0=ch0, pt=pt: e.activation(
                                out=gcs[:, ch0:ch0 + 4, :, :].rearrange("p c s r -> p c (s r)"),
                                in_=pt[:, :].rearrange("p (c t) -> p c t", t=128)[:, :, 0:32], func=AF.Copy),
                                reads=[pk], writes=["gcs"])
                gi = [0]

                def ffn_epi_factory(q0):
                    def epi(g, ci, bi, b0, bn, pls):
                        fl = 2 * g + ci
                        f = q0 + fl
                        (pkg, ptg), (pku, ptu) = pls
                        w0 = fcw[:, (li * 3 + 0) * FCH + f:(li * 3 + 0) * FCH + f + 1]
                        w1 = fcw[:, (li * 3 + 1) * FCH + f:(li * 3 + 1) * FCH + f + 1]
                        w2 = fcw[:, (li * 3 + 2) * FCH + f:(li * 3 + 2) * FCH + f + 1]
                        i3 = gi[0] % 3
                        gi[0] += 1
                        tk, tt = nexttmp()
                        if bi == 0:
                            gx = gxp[i3]
                            gk = ("gxp", i3)
                            S.op("act", lambda e: e.activation(out=gx[:, 0:2], in_=halo_f[li][:, f, :], func=AF.Copy),
                                 reads=[("halo_f", li)], writes=[gk])
                            S.op("act", lambda e: e.activation(out=gx[:, 2:2 + NTA], in_=ptg[:, 0:NTA], func=AF.Copy),
                                 reads=[pkg, gk], writes=[gk])
                            S.op("act", lambda e: e.activation(out=halo_f[li][:, f, :], in_=gx[:, NTA:NTA + 2], func=AF.Copy),
                                 reads=[gk], writes=[("halo_f", li)])
                            if last:
                                S.op("act", lambda e: e.activation(out=glast[:, f, 0:2], in_=gx[:, NTA:NTA + 2],
                                                                   func=AF.Copy), reads=[gk], writes=["glast"])
                            a0, a1, a2 = gx[:, 0:NTA], gx[:, 1:1 + NTA], gx[:, 2:2 + NTA]
                            to = tt[:, 0:NTA]
                            pu = ptu[:, 0:NTA]
                            ao = aT[:, fl, 0:NTA]
                        else:
                            gx = gxs[i3]
                            gk = ("gxs", i3)
                            S.op("act", lambda e: e.activation(out=gx[:, :, 0:2], in_=gcs[:, f, :, :], func=AF.Copy),
                                 reads=["gcs"], writes=[gk])
                            S.op("act", lambda e: e.activation(
                                out=gx[:, :, 2:6], in_=ptg[:, 0:NTB].rearrange("p (s t) -> p s t", t=4), func=AF.Copy),
                                reads=[pkg, gk], writes=[gk])
                            S.op("act", lambda e: e.activation(
                                out=glast[:, f, 2:34].rearrange("p (s r) -> p s r", r=2), in_=gx[:, :, 4:6], func=AF.Copy),
                                reads=[gk], writes=["glast"])
                            a0, a1, a2 = gx[:, :, 0:4], gx[:, :, 1:5], gx[:, :, 2:6]
                            to = tt[:, 0:NTB].rearrange("p (s t) -> p s t", t=4)
                            pu = ptu[:, 0:NTB].rearrange("p (s t) -> p s t", t=4)
                            ao = aT[:, fl, NTA:NT].rearrange("p (s t) -> p s t", t=4)
                        S.op("dve", lambda e: e.tensor_scalar(out=to, in0=a0, scalar1=w0, scalar2=None, op0=ALU.mult),
                             reads=[gk] + VEC, writes=[tk])
                        S.op("dve", lambda e: e.scalar_tensor_tensor(out=to, in0=a1, scalar=w1, in1=to, op0=ALU.mult,
                                                                     op1=ALU.add), reads=[gk, tk], writes=[tk])
                        S.op("dve", lambda e: e.scalar_tensor_tensor(out=to, in0=a2, scalar=w2, in1=to, op0=ALU.mult,
                                                                     op1=ALU.add), reads=[gk, tk], writes=[tk])
                        S.op("act", lambda e: e.activation(out=to, in_=to, func=AF.Silu), reads=[tk], writes=[tk])
                        S.op("dve", lambda e: e.tensor_tensor(out=ao, in0=pu, in1=to, op=ALU.mult),
                             reads=[tk, pku], writes=[("aT", fl)])
                    return epi

                AT_KEYS = [("aT", f) for f in range(12)]
                def gu_srcs(q0):
                    return [(ffn_w_gate[li], 0, q0 * 128), (ffn_w_up[li], 0, q0 * 128)]

                def dn_srcs(q0):
                    return [(ffn_w_down[li], q0 * 128, 0)]
                for qi, (q0, qn) in enumerate(QS):
                    gemm(p, gu_srcs(q0), DC, qn // 2, lambda kc, b0, bn: hn[:, kc, b0:b0 + bn], HN_KEYS, ffn_epi_factory(q0),
                         tag=("gu", p, li, qi), nxt=(("dn", p, li, qi), dn_srcs(q0), qn))
                    nq = QS[qi + 1] if qi + 1 < len(QS) else None
                    gemm(p, dn_srcs(q0), qn, 8, lambda kc, b0, bn: aT[:, kc, b0:b0 + bn], AT_KEYS, residual_epi,
                         tag=("dn", p, li, qi),
                         nxt=((("gu", p, li, qi + 1), gu_srcs(nq[0]), DC) if nq is not None else None))
                if last:
                    for h in range(3):
                        c0 = h * 2048
                        cn_ = min(2048, FF - c0)
                        xi = h % 2
                        for c4 in range(cn_ // 512):
                            pk, pt = nextps()
                            S.op("pe", [lambda e, cc=cc, pt=pt, c4=c4, h=h: e.transpose(
                                pt[0:34, cc * 128:(cc + 1) * 128], glast[:, h * 16 + 4 * c4 + cc, :], ident[:, :])
                                for cc in range(4)], reads=["glast", "ident"], writes=[pk])
                            S.op("act", lambda e, c4=c4, pt=pt, xi=xi: e.activation(
                                out=xin[xi][0:34, c4 * 512:(c4 + 1) * 512], in_=pt[0:34, :], func=AF.Copy),
                                reads=[pk], writes=[("xin", xi)])
                        S.dma("sp", ("xin", xi), [
                            lambda e, c0=c0, cn_=cn_, xi=xi: e.dma_start(out=ffnp[li][:, c0:c0 + cn_], in_=xin[xi][0:2, 0:cn_]),
                            lambda e, c0=c0, cn_=cn_, xi=xi: e.dma_start(out=ffns[li][:, c0:c0 + cn_], in_=xin[xi][2:34, 0:cn_])],
                            reads=[("xin", xi)])
                S.barrier()


        TWO_PI = 2.0 * np.pi

        def ssm_prep(j):
            with ExitStack() as ls:
                def LT(name, shape, dt=F32):
                    uid[0] += 1
                    return ls.enter_context(nc.sbuf_tensor("%s_%d" % (name, uid[0]), list(shape), dt))
                k = [0]

                def key(n):
                    return ("pp", n)
                Are = LT("Are", [128, 64]); Aim = LT("Aim", [128, 64]); ldt = LT("ldt", [128, 1])
                Bre = LT("Bre", [128, 1024]); Bim = LT("Bim", [128, 1024])
                S.dma("sp", "prep_in", [
                    lambda e: e.dma_start(out=Are[:], in_=ssm_A_re[j]),
                    lambda e: e.dma_start(out=Aim[:], in_=ssm_A_im[j]),
                    lambda e: e.dma_start(out=ldt[:], in_=ssm_log_dt[j].rearrange("(g o) -> g o", o=1)),
                    lambda e: e.dma_start(out=Bre[:], in_=ssm_B_re[j]),
                    lambda e: e.dma_start(out=Bim[:], in_=ssm_B_im[j])], writes=["pin"])
                dtt = LT("dtt", [128, 1]); lr = LT("lr", [128, 64]); li = LT("li", [128, 64])
                mag = LT("mag", [128, 64]); t1 = LT("t1", [128, 64]); t2 = LT("t2", [128, 64])
                ki = LT("ki", [128, 64], mybir.dt.int32)
                cs = [LT("cs0", [128, 64]), LT("cs1", [128, 64])]
                ab = [LT("ab0", [128, 64]), LT("ab1", [128, 64])]
                qq = [LT("q0", [128, 64]), LT("q1", [128, 64])]
                Bb = [LT("Bb0", [128, 1024]), LT("Bb1", [128, 1024])]
                tb = LT("tb", [128, 1024])
                PK = ["pin", "pw"]

                def dv(fn):
                    S.op("dve", fn, reads=PK, writes=["pw"])

                def ac(fn):
                    S.op("act", fn, reads=PK, writes=["pw"])
                ac(lambda e: e.activation(out=dtt[:], in_=ldt[:], func=AF.Exp))
                dv(lambda e: e.tensor_scalar(out=lr[:], in0=Are[:], scalar1=dtt[:, 0:1], scalar2=None, op0=ALU.mult))
                dv(lambda e: e.tensor_scalar(out=li[:], in0=Aim[:], scalar1=dtt[:, 0:1], scalar2=None, op0=ALU.mult))
                ac(lambda e: e.activation(out=mag[:], in_=lr[:], func=AF.Exp))
                for which, shift in ((0, np.pi / 2.0), (1, 0.0)):
                    dv(lambda e, shift=shift: e.tensor_scalar(out=t1[:], in0=li[:], scalar1=shift, scalar2=None, op0=ALU.add))
                    dv(lambda e: e.tensor_scalar(out=t2[:], in0=t1[:], scalar1=1.0 / TWO_PI, scalar2=None, op0=ALU.mult))
                    dv(lambda e: e.tensor_copy(out=ki[:], in_=t2[:]))
                    dv(lambda e: e.tensor_copy(out=t2[:], in_=ki[:]))
                    dv(lambda e: e.scalar_tensor_tensor(out=t1[:], in0=t2[:], scalar=-TWO_PI, in1=t1[:], op0=ALU.mult,
                                                        op1=ALU.add))
                    dv(lambda e: e.tensor_scalar(out=t2[:], in0=t1[:], scalar1=float(np.pi), scalar2=-TWO_PI,
                                                 op0=ALU.is_gt, op1=ALU.mult))
                    dv(lambda e: e.tensor_tensor(out=t1[:], in0=t1[:], in1=t2[:], op=ALU.add))
                    dv(lambda e: e.tensor_scalar(out=t1[:], in0=t1[:], scalar1=float(np.pi), scalar2=-float(np.pi),
                                                 op0=ALU.min, op1=ALU.max))
                    ac(lambda e, which=which: e.activation(out=cs[which][:], in_=t1[:], func=AF.Sin))
                dv(lambda e: e.tensor_tensor(out=ab[0][:], in0=mag[:], in1=cs[0][:], op=ALU.mult))
                dv(lambda e: e.tensor_tensor(out=ab[1][:], in0=mag[:], in1=cs[1][:], op=ALU.mult))
                dv(lambda e: e.tensor_tensor(out=t1[:], in0=Are[:], in1=Are[:], op=ALU.mult))
                dv(lambda e: e.tensor_tensor(out=t2[:], in0=Aim[:], in1=Aim[:], op=ALU.mult))
                dv(lambda e: e.tensor_tensor(out=t1[:], in0=t1[:], in1=t2[:], op=ALU.add))
                dv(lambda e: e.reciprocal(out=t1[:], in_=t1[:]))
                dv(lambda e: e.tensor_scalar(out=mag[:], in0=ab[0][:], scalar1=-1.0, scalar2=None, op0=ALU.add))
                dv(lambda e: e.tensor_tensor(out=qq[0][:], in0=mag[:], in1=Are[:], op=ALU.mult))
                dv(lambda e: e.tensor_tensor(out=t2[:], in0=ab[1][:], in1=Aim[:], op=ALU.mult))
                dv(lambda e: e.tensor_tensor(out=qq[0][:], in0=qq[0][:], in1=t2[:], op=ALU.add))
                dv(lambda e: e.tensor_tensor(out=qq[0][:], in0=qq[0][:], in1=t1[:], op=ALU.mult))
                dv(lambda e: e.tensor_tensor(out=qq[1][:], in0=ab[1][:], in1=Are[:], op=ALU.mult))
                dv(lambda e: e.tensor_tensor(out=t2[:], in0=mag[:], in1=Aim[:], op=ALU.mult))
                dv(lambda e: e.tensor_tensor(out=qq[1][:], in0=qq[1][:], in1=t2[:], op=ALU.subtract))
                dv(lambda e: e.tensor_tensor(out=qq[1][:], in0=qq[1][:], in1=t1[:], op=ALU.mult))

                def bq(t):
                    return t[:, :].unsqueeze(2).broadcast_to([128, 64, 16])

                def v3(t):
                    return t[:, :].rearrange("g (p i) -> g p i", i=16)
                dv(lambda e: e.tensor_tensor(out=v3(Bb[0]), in0=v3(Bre), in1=bq(qq[0]), op=ALU.mult))
                dv(lambda e: e.tensor_tensor(out=v3(tb), in0=v3(Bim), in1=bq(qq[1]), op=ALU.mult))
                dv(lambda e: e.tensor_tensor(out=Bb[0][:], in0=Bb[0][:], in1=tb[:], op=ALU.subtract))
                dv(lambda e: e.tensor_tensor(out=v3(Bb[1]), in0=v3(Bim), in1=bq(qq[0]), op=ALU.mult))
                dv(lambda e: e.tensor_tensor(out=v3(tb), in0=v3(Bre), in1=bq(qq[1]), op=ALU.mult))
                dv(lambda e: e.tensor_tensor(out=Bb[1][:], in0=Bb[1][:], in1=tb[:], op=ALU.add))
                S.dma("sp", "prep_o", [lambda e, r=r: e.dma_start(out=scrA[j, r], in_=ab[r][:]) for r in range(2)] +
                      [lambda e, r=r: e.dma_start(out=scrB[j, r], in_=Bb[r][:]) for r in range(2)],
                      reads=["pw"], writes=["scr1"])
                pidx = LT("pidx", [128, 1], mybir.dt.int32); pi2 = LT("pi2", [128, 1], mybir.dt.int32)
                mpar = [LT("mpar0", [128, 1]), LT("mpar1", [128, 1])]
                mhalf = [LT("mh0", [128, 1]), LT("mh1", [128, 1])]
                nmhalf = [LT("nmh0", [128, 1]), LT("nmh1", [128, 1])]
                S.op("pool", lambda e: e.iota(pidx[:], pattern=[[0, 1]], base=0, channel_multiplier=1), writes=["pidx"])
                pf = LT("pf", [128, 1]); ptmp = LT("ptmp", [128, 1])
                S.op("dve", lambda e: e.tensor_copy(out=pf[:], in_=pidx[:]), reads=["pidx", "pw"], writes=["pw"])
                dv(lambda e: e.tensor_scalar(out=mhalf[1][:], in0=pf[:], scalar1=64.0, scalar2=None, op0=ALU.is_ge))
                dv(lambda e: e.memset(mpar[1][:], 0.0))
                for m_, (thr, sg) in enumerate(((16, 1.0), (32, -1.0), (48, 1.0), (64, -1.0), (80, 1.0), (96, -1.0), (112, 1.0))):
                    dv(lambda e, thr=thr, sg=sg: e.tensor_scalar(out=ptmp[:], in0=pf[:], scalar1=float(thr), scalar2=sg,
                                                                 op0=ALU.is_ge, op1=ALU.mult))
                    dv(lambda e: e.tensor_tensor(out=mpar[1][:], in0=mpar[1][:], in1=ptmp[:], op=ALU.add))
                for mm in (mpar, mhalf):
                    dv(lambda e, mm=mm: e.tensor_scalar(out=mm[0][:], in0=mm[1][:], scalar1=-1.0, scalar2=1.0,
                                                        op0=ALU.mult, op1=ALU.add))
                for a in range(2):
                    dv(lambda e, a=a: e.tensor_scalar(out=nmhalf[a][:], in0=mhalf[a][:], scalar1=-1.0, scalar2=None,
                                                      op0=ALU.mult))
                BbT = [LT("BbT0", [128, DC, 64]), LT("BbT1", [128, DC, 64])]
                Cp = [LT("Cp0", [128, 64, 16]), LT("Cp1", [128, 64, 16])]
                with nc.allow_non_contiguous_dma(reason="ssm weight relayout"):
                    S.dma("sp", "prep_in", [lambda e, r=r: e.dma_start(
                        out=apair[j][r][:], in_=scrA[j, r].rearrange("(q a) p -> (a p) q", a=2)) for r in range(2)],
                        reads=["scr1"], writes=[("apair", j)])
                    fl = []
                    for r in range(2):
                        vB = scrB[j, r].rearrange("(f a) (p i) -> a i f p", a=8, i=16)
                        for a in range(8):
                            for f in range(DC):
                                fl.append(lambda e, r=r, a=a, vB=vB, f=f: e.dma_start(
                                    out=BbT[r][16 * a:16 * a + 16, f, :], in_=vB[a][:, f, :]))
                        vC = (ssm_C_re, ssm_C_im)[r][j].rearrange("(q a) (j p) -> a p q j", a=2, p=64)
                        for a in range(2):
                            for q in range(64):
                                fl.append(lambda e, r=r, a=a, vC=vC, q=q: e.dma_start(
                                    out=Cp[r][64 * a:64 * a + 64, q, :], in_=vC[a][:, q, :]))
                    S.dma("sp", "prep_in", fl, reads=["scr1"], writes=["pin2"])
                Bblk = [LT("Bblk0", [128, DC, 2, 64], BF16), LT("Bblk1", [128, DC, 2, 64], BF16)]
                Cblk = [LT("Cblk0", [128, 64, 2, 16], BF16), LT("Cblk1", [128, 64, 2, 16], BF16)]
                for r in range(2):
                    for a in range(2):
                        S.op("dve", lambda e, r=r, a=a: e.tensor_scalar(
                            out=Bblk[r][:, :, a, :], in0=BbT[r][:, :, :], scalar1=mpar[a][:, 0:1], scalar2=None, op0=ALU.mult),
                            reads=["pin2", "pw"], writes=["pblk"])
                        mm = mhalf if r == 0 else nmhalf
                        S.op("dve", lambda e, r=r, a=a, mm=mm: e.tensor_scalar(
                            out=Cblk[r][:, :, a, :], in0=Cp[r][:, :, :], scalar1=mm[a][:, 0:1], scalar2=None, op0=ALU.mult),
                            reads=["pin2", "pw"], writes=["pblk"])
                S.dma("sp", "prep_o", [lambda e, r=r: e.dma_start(
                    out=scrBb[j, r], in_=Bblk[r][:, :, :, :].rearrange("p f a q -> p (f a q)")) for r in range(2)] +
                    [lambda e, r=r: e.dma_start(
                        out=scrCb[j, r], in_=Cblk[r][:, :, :, :].rearrange("p f a q -> p (f a q)")) for r in range(2)],
                    reads=["pblk"], writes=[("scrblk", j)])
                for r in range(2):
                    S.op("dve", lambda e, r=r: e.memset(carry[j][r][:], 0.0), writes=[("carry", j)])
                S.barrier()

        TS = 32
        GC1 = 2.0 * float(np.sqrt(2.0 / np.pi))

        def ssm_layer(p, li, j):
            blks = blocks_of(p)
            last = (p == npass - 1)
            with ExitStack() as ls:
                def LT(name, shape, dt=F32):
                    uid[0] += 1
                    return ls.enter_context(nc.sbuf_tensor("%s_%d" % (name, uid[0]), list(shape), dt))
                Bblk = [LT("Bblk0", [128, DC, 2, 64], BF16), LT("Bblk1", [128, DC, 2, 64], BF16)]
                Cblk = [LT("Cblk0", [128, 64, 2, 16], BF16), LT("Cblk1", [128, 64, 2, 16], BF16)]
                H = [LT("H0", [128, 64, 40]), LT("H1", [128, 64, 40])]
                Hb = [LT("Hb0", [128, 64, 32], BF16), LT("Hb1", [128, 64, 32], BF16)]
                tm = [LT("tm%d" % i, [128, 512]) for i in range(4)]
                h0s = [LT("h0s%d" % r, [128, 64, NSEQ]) for r in range(2)] if (last and ssm_stage >= 3) else None
                S.dma("sp", "ssm_w", [lambda e, r=r: e.dma_start(
                    out=Bblk[r][:, :, :, :].rearrange("p f a q -> p (f a q)"), in_=scrBb[j, r]) for r in range(2)] +
                    [lambda e, r=r: e.dma_start(
                        out=Cblk[r][:, :, :, :].rearrange("p f a q -> p (f a q)"), in_=scrCb[j, r]) for r in range(2)],
                    reads=[("scrblk", j)], writes=["ssmw"])
                if h0s is not None:
                    for r in range(2):
                        for pc in range(4):
                            xi = pc % 2
                            S.dma("sp", ("xin", xi), lambda e, r=r, pc=pc, xi=xi: e.dma_start(
                                out=xin[xi][0:NSEQ, :], in_=(sre, sim)[r][j][:, pc * 2048:(pc + 1) * 2048]),
                                writes=[("xin", xi)])
                            pk, pt = nextps()
                            S.op("pe", [lambda e, k=k, xi=xi, pt=pt: e.transpose(
                                pt[:, k * NSEQ:(k + 1) * NSEQ], xin[xi][0:NSEQ, k * 128:(k + 1) * 128],
                                ident[0:NSEQ, 0:NSEQ]) for k in range(16)],
                                reads=[("xin", xi), "ident"], writes=[pk])
                            S.op("act", lambda e, r=r, pc=pc, pt=pt: e.activation(
                                out=h0s[r][:, 16 * pc:16 * pc + 16, :],
                                in_=pt[:, 0:16 * NSEQ].rearrange("p (q s) -> p q s", s=NSEQ), func=AF.Copy),
                                reads=[pk], writes=["h0s"])
                rmsnorm(p, g_mix, li * DC)
                are, aim = apair[j]

                def run_chunk(col0, ncols, nseq, tlen, init_fn, final_fn):
                    HW = nseq * (1 + tlen)
                    Hv = [H[r][:, :, 0:HW].rearrange("p q (s t) -> p q s t", t=1 + tlen) for r in range(2)]
                    init_fn(Hv)
                    for r in range(2):
                        for q in range(4):
                            pk, pt = nextps()
                            fns = []
                            for fc in range(DC):
                                fns.append(lambda e, r=r, fc=fc, q=q, pt=pt: e.matmul(
                                    pt[:, fc * ncols:(fc + 1) * ncols],
                                    Bblk[r][32 * q:32 * q + 32, fc, :, :].rearrange("p a q -> p (a q)"),
                                    hn[32 * q:32 * q + 32, fc, col0:col0 + ncols], start=True, stop=True,
                                    tile_position=(32 * q, 0)))
                            S.op("pe", fns, reads=["ssmw"] + HN_KEYS, writes=[pk])
                            S.op("act", lambda e, r=r, q=q, pt=pt: e.activation(
                                out=Hv[r][:, q::4, :, 1:1 + tlen],
                                in_=pt[:, 0:DC * ncols].rearrange("p (f s t) -> p f s t", s=nseq, t=tlen), func=AF.Copy),
                                reads=[pk], writes=["H"])
                    n = 64 * nseq

                    def tv(i):
                        return tm[i][:, 0:n].rearrange("p (q s) -> p q s", s=nseq)

                    def bc(a):
                        return a[:, :].unsqueeze(2).broadcast_to([128, 64, nseq])
                    chain = []
                    for t in range(tlen):
                        hr0, hi0 = Hv[0][:, :, :, t], Hv[1][:, :, :, t]
                        hr1, hi1 = Hv[0][:, :, :, t + 1], Hv[1][:, :, :, t + 1]
                        chain += [
                            lambda e, hr0=hr0: e.tensor_tensor(out=tv(0), in0=hr0, in1=bc(are), op=ALU.mult),
                            lambda e, hi0=hi0: e.tensor_tensor(out=tv(1), in0=hi0, in1=bc(aim), op=ALU.mult),
                            lambda e, hi0=hi0: e.tensor_tensor(out=tv(2), in0=hi0, in1=bc(are), op=ALU.mult),
                            lambda e, hr0=hr0: e.tensor_tensor(out=tv(3), in0=hr0, in1=bc(aim), op=ALU.mult),
                            lambda e: e.tensor_tensor(out=tv(0), in0=tv(0), in1=tv(1), op=ALU.subtract),
                            lambda e: e.tensor_tensor(out=tv(2), in0=tv(2), in1=tv(3), op=ALU.add),
                            lambda e, hr1=hr1: e.tensor_tensor(out=hr1, in0=hr1, in1=tv(0), op=ALU.add),
                            lambda e, hi1=hi1: e.tensor_tensor(out=hi1, in0=hi1, in1=tv(2), op=ALU.add),
                        ]
                    S.op("dve", chain, reads=["H", "tmS", ("apair", j)], writes=["H", "tmS"])
                    final_fn(Hv)
                    if ssm_stage < 2:
                        return
                    for r in range(2):
                        S.op("act", lambda e, r=r: e.activation(
                            out=Hb[r][:, :, 0:ncols].rearrange("p q (s t) -> p q s t", t=tlen),
                            in_=Hv[r][:, :, :, 1:1 + tlen], func=AF.Copy), reads=["H"], writes=["Hb"])
                    pk, pt = nextps()
                    fns = []
                    for fc in range(DC):
                        for q in range(4):
                            pr = fc * 4 + q
                            for r in range(2):
                                fns.append(lambda e, q=q, pr=pr, r=r, fc=fc, pt=pt: e.matmul(
                                    pt[32 * q:32 * q + 32, fc * ncols:(fc + 1) * ncols],
                                    Cblk[r][:, pr, :, :].rearrange("p a q -> p (a q)"), Hb[r][:, pr, 0:ncols],
                                    start=(r == 0), stop=(r == 1), tile_position=(0, 32 * q)))
                    S.op("pe", fns, reads=["ssmw", "Hb"], writes=[pk])
                    W_ = DC * ncols

                    def wv(i):
                        return tm[i][:, 0:W_].rearrange("p (f t) -> p f t", t=ncols)
                    y, z = wv(0), wv(1)
                    Dv = sDv[:, j * DC:(j + 1) * DC].unsqueeze(2).broadcast_to([128, DC, ncols])
                    S.op("dve", [
                        lambda e: e.tensor_tensor(out=y, in0=hn[:, :, col0:col0 + ncols], in1=Dv, op=ALU.mult),
                        lambda e, pt=pt: e.tensor_tensor(out=y, in0=y, in1=pt[:, 0:W_].rearrange("p (f t) -> p f t", t=ncols),
                                                         op=ALU.add),
                        lambda e: e.tensor_tensor(out=z, in0=y, in1=y, op=ALU.mult),
                        lambda e: e.tensor_scalar(out=z, in0=z, scalar1=0.044715, scalar2=1.0, op0=ALU.mult, op1=ALU.add),
                        lambda e: e.tensor_tensor(out=z, in0=z, in1=y, op=ALU.mult),
                    ], reads=[pk, "tmS"] + HN_KEYS + VEC, writes=["tmS"])
                    S.op("act", lambda e: e.activation(out=z, in_=z, func=AF.Sigmoid, scale=GC1), reads=["tmS"], writes=["tmS"])
                    S.op("dve", lambda e: e.tensor_tensor(out=cb[:, :, col0:col0 + ncols], in0=z, in1=y, op=ALU.mult),
                         reads=["tmS"], writes=[("cb", c) for c in range(DC)])

                nsub = NTA // TS
                for sc_ in range(nsub):
                    def init_p(Hv):
                        for r in range(2):
                            S.op("act", lambda e, r=r: e.activation(out=Hv[r][:, :, 0, 0], in_=carry[j][r][:, :], func=AF.Copy),
                                 reads=[("carry", j), "H", "Hb"], writes=["H"])

                    def fin_p(Hv, sc_=sc_):
                        for r in range(2):
                            S.op("act", lambda e, r=r: e.activation(out=carry[j][r][:, :], in_=Hv[r][:, :, 0, TS], func=AF.Copy),
                                 reads=["H"], writes=[("carry", j)])
                        if last and sc_ == nsub - 1 and ssm_stage >= 3:
                            with nc.allow_non_contiguous_dma(reason="ssm state out"):
                                S.dma("sp", "st_o", [lambda e, r=r: e.dma_start(
                                    out=(rep, imp)[r][j].rearrange("(q a) p -> (a p) q", a=2), in_=carry[j][r][:, :])
                                    for r in range(2)], reads=[("carry", j)])
                    run_chunk(sc_ * TS, TS, 1, TS, init_p, fin_p)
                if last and ssm_stage >= 3:
                    for hb in range(2):
                        def init_s(Hv, hb=hb):
                            for r in range(2):
                                S.op("act", lambda e, r=r: e.activation(out=Hv[r][:, :, :, 0], in_=h0s[r][:, :, 8 * hb:8 * hb + 8],
                                                                        func=AF.Copy), reads=["h0s", "H", "Hb"], writes=["H"])

                        def fin_s(Hv, hb=hb):
                            for r in range(2):
                                S.op("act", lambda e, r=r: e.activation(out=h0s[r][:, :, 8 * hb:8 * hb + 8], in_=Hv[r][:, :, :, 4],
                                                                        func=AF.Copy), reads=["H", "h0s"], writes=[("h0o", hb), "h0s"])
                        run_chunk(NTA + 32 * hb, 32, 8, 4, init_s, fin_s)
                    for r in range(2):
                        for pc in range(4):
                            xi = pc % 2
                            for g4 in range(4):
                                pk, pt = nextps()
                                S.op("pe", [lambda e, k=k, r=r, pc=pc, g4=g4, pt=pt: e.transpose(
                                    pt[0:NSEQ, k * 128:(k + 1) * 128], h0s[r][:, 16 * pc + 4 * g4 + k, :], ident[:, :])
                                    for k in range(4)], reads=["h0s", ("h0o", 0), ("h0o", 1), "ident"], writes=[pk])
                                S.op("act", lambda e, g4=g4, xi=xi, pt=pt: e.activation(
                                    out=xin[xi][0:NSEQ, g4 * 512:(g4 + 1) * 512], in_=pt[0:NSEQ, :], func=AF.Copy),
                                    reads=[pk], writes=[("xin", xi)])
                            S.dma("sp", ("xin", xi), lambda e, r=r, pc=pc, xi=xi: e.dma_start(
                                out=(res_, ims_)[r][j][:, pc * 2048:(pc + 1) * 2048], in_=xin[xi][0:NSEQ, :]),
                                reads=[("xin", xi)])
                if ssm_stage >= 4:
                    W = ssm_w_glu[j]
                    gemm(p, [(W, 0, 0), (W, 0, D)], DC, 8, lambda kc, b0, bn: cb[:, kc, b0:b0 + bn],
                         [("cb", c) for c in range(DC)], glu_residual_epi)
                S.barrier()

        def final_out(p):
            blks = blocks_of(p)
            rmsnorm(p, g_fin, 0, dst_fn=True)
            dsts = [(yp, p * NTA + r * 128, 128, r * 128) for r in range(4)]
            if p == npass - 1:
                dsts.append((ys, 0, NTB, NTA))
            for si, (dst, row0, n, col0) in enumerate(dsts):
                xi = si % 2
                for c4 in range(4):
                    tk, tt = nexttmp()
                    for cc in range(4):
                        c = 4 * c4 + cc
                        S.op("dve", lambda e, c=c, cc=cc, tt=tt, n=n, col0=col0: e.scalar_tensor_tensor(
                            out=tt[:, cc * 128:cc * 128 + n], in0=xT[:, c, col0:col0 + n], scalar=g_fin[:, c:c + 1],
                            in1=rs[:, col0:col0 + n], op0=ALU.mult, op1=ALU.mult),
                            reads=[("xT", c), "rs"] + VEC, writes=[tk])
                    pk, pt = nextps()
                    S.op("pe", [lambda e, cc=cc, n=n, pt=pt, tt=tt: e.transpose(
                        pt[0:n, cc * 128:(cc + 1) * 128], tt[:, cc * 128:cc * 128 + n], ident[:, :])
                        for cc in range(4)], reads=[tk, "ident"], writes=[pk])
                    S.op("act", lambda e, c4=c4, n=n, xi=xi, pt=pt: e.activation(
                        out=xin[xi][0:n, c4 * 512:(c4 + 1) * 512], in_=pt[0:n, :], func=AF.Copy),
                        reads=[pk], writes=[("xin", xi)])
                S.dma("sp", ("xin", xi), lambda e, dst=dst, row0=row0, n=n, xi=xi: e.dma_start(
                    out=dst[row0:row0 + n, :], in_=xin[xi][0:n, :]), reads=[("xin", xi)])

        for li in layers:
            if li % 2 == 1:
                ssm_prep(li // 2)
        for p in range(npass):
            load_x(p)
            for li in layers:
                if li % 2 == 0:
                    conv_layer(p, li, li // 2)
                elif ssm_stage >= 1:
                    ssm_layer(p, li, li // 2)
                ffn_layer(p, li)
                assert not pre_loaded, "prefetched weight tiles must be consumed by the very next gemm"
            final_out(p)
        S.final_wait("sp")
    return nc


_NC_CACHE = {}


def kernel(**inputs):
    layers = tuple(int(c) for c in os.environ.get("K_LAYERS", "0123"))
    stage = int(os.environ.get("K_SSM_STAGE", "4"))
    key = (layers, stage)
    if key not in _NC_CACHE:
        _NC_CACHE[key] = build_nc(layers=layers, ssm_stage=stage)
    nc = _NC_CACHE[key]
    f = lambda a: np.ascontiguousarray(np.asarray(a, dtype=np.float32))
    inp = {k: f(v) for k, v in inputs.items()}
    in_maps = []
    for c in range(NCORES):
        b = c % 4
        sl = slice(NSEQ * c, NSEQ * c + NSEQ)
        m = {
            "xp": inp["x_prompt"][b],
            "xs": f(inp["x_sample"][sl].reshape(NTB, D)),
            "sconv": f(inp["state_conv"][:, sl]),
            "sre": f(inp["state_ssm_re"][:, sl].reshape(2, NSEQ, 8192)),
            "sim": f(inp["state_ssm_im"][:, sl].reshape(2, NSEQ, 8192)),
            "sffn": f(inp["state_ffn"][:, sl].reshape(4, NSEQ * 2, FF)),
            "norm_final": inp["norm_final"].reshape(1, D),
            "ssm_B_re": inp["ssm_B_re"].reshape(2, 128, 1024),
            "ssm_B_im": inp["ssm_B_im"].reshape(2, 128, 1024),
            "ssm_C_re": inp["ssm_C_re"].reshape(2, 128, 1024),
            "ssm_C_im": inp["ssm_C_im"].reshape(2, 128, 1024),
        }
        for k in ("norm_mix", "norm_ffn", "conv_w_in", "conv_dw", "conv_dw_b", "conv_ln_g", "conv_ln_b",
                  "conv_w_out", "ssm_A_re", "ssm_A_im", "ssm_log_dt", "ssm_D", "ssm_w_glu", "ffn_w_gate",
                  "ffn_w_up", "ffn_conv", "ffn_w_down"):
            m[k] = inp[k]
        in_maps.append(m)
    res = run_bass_kernel_spmd(nc, in_maps, core_ids=list(range(NCORES)))
    R = res.results
    y_prompt = np.stack([R[b]["yp"] for b in range(4)])
    y_sample = np.concatenate([R[c]["ys"].reshape(NSEQ, 4, D) for c in range(NCORES)], axis=0)
    conv_p = np.stack([R[b]["convp"] for b in range(4)], axis=1)
    conv_s = np.concatenate([R[c]["convs"] for c in range(NCORES)], axis=1)
    re_p = np.stack([R[b]["rep"] for b in range(4)], axis=1)
    im_p = np.stack([R[b]["imp"] for b in range(4)], axis=1)
    re_s = np.concatenate([R[c]["res"].reshape(2, NSEQ, 128, 64) for c in range(NCORES)], axis=1)
    im_s = np.concatenate([R[c]["ims"].reshape(2, NSEQ, 128, 64) for c in range(NCORES)], axis=1)
    ffn_p = np.stack([R[b]["ffnp"] for b in range(4)], axis=1)
    ffn_s = np.concatenate([R[c]["ffns"].reshape(4, NSEQ, 2, FF) for c in range(NCORES)], axis=1)
    return (y_prompt, y_sample, conv_p, conv_s, re_p, im_p, re_s, im_s, ffn_p, ffn_s)
```

```python
import os
import numpy as np
from contextlib import ExitStack
import concourse.bass as bass
import concourse.mybir as mybir
from concourse.bass_utils import run_bass_kernel_spmd

F32 = mybir.dt.float32
BF16 = mybir.dt.bfloat16
AF = mybir.ActivationFunctionType
ALU = mybir.AluOpType
AX = mybir.AxisListType

D = 2048
DC = 16
FF = 5632
FCH = 44
NPASS = 4
NTA = 512
NSEQ = 16
NTB = 64
NT = NTA + NTB
DEPTH = 4
EPS = 1e-6
NCORES = 8


class Sched:
    def __init__(self, nc, stack, n_dma_sems=24):
        self.nc = nc
        self.eng = {"pe": nc.tensor, "act": nc.scalar, "dve": nc.vector,
                    "pool": nc.gpsimd, "sp": nc.sync}
        self.sem = {}
        self.cnt = {}
        for e in self.eng:
            self.sem[e] = stack.enter_context(nc.semaphore("s_" + e))
            self.cnt[e] = 0
        self.dma_free = []
        for i in range(n_dma_sems):
            k = "dma%d" % i
            self.sem[k] = stack.enter_context(nc.semaphore("s_" + k))
            self.cnt[k] = 0
            self.dma_free.append(k)
        self.slot_sem = {}
        self.waited = {e: {} for e in self.eng}
        self.last_w = {}
        self.reads = {}

    def _deps(self, reads, writes):
        deps = {}

        def add(d):
            if d is None:
                return
            k, n = d
            if deps.get(k, 0) < n:
                deps[k] = n
        for b in reads:
            add(self.last_w.get(b))
        for b in writes:
            add(self.last_w.get(b))
            for d in self.reads.get(b, ()):
                add(d)
        return deps

    def _emit_waits(self, e, deps):
        for k, n in deps.items():
            if k == e and e == "pe":
                continue
            if self.waited[e].get(k, 0) >= n:
                continue
            self.eng[e].wait_ge(self.sem[k], n)
            self.waited[e][k] = n

    def _record(self, key, n, reads, writes):
        for b in writes:
            self.last_w[b] = (key, n)
            self.reads[b] = []
        for b in reads:
            lst = self.reads.setdefault(b, [])
            lst.append((key, n))
            if len(lst) > 8:
                m = {}
                for k2, n2 in lst:
                    m[k2] = max(m.get(k2, 0), n2)
                self.reads[b] = list(m.items())

    def op(self, e, fns, reads=(), writes=()):
        if callable(fns):
            fns = [fns]
        deps = self._deps(reads, writes)
        self._emit_waits(e, deps)
        eng = self.eng[e]
        ins = None
        for f in fns:
            ins = f(eng)
        self.cnt[e] += 1
        ins.then_inc(self.sem[e], 1)
        self._record(e, self.cnt[e], reads, writes)

    def dma(self, e, slot, fns, reads=(), writes=()):
        if callable(fns):
            fns = [fns]
        if slot not in self.slot_sem:
            self.slot_sem[slot] = self.dma_free.pop(0)
        k = self.slot_sem[slot]
        deps = self._deps(reads, writes)
        self._emit_waits(e, deps)
        eng = self.eng[e]
        for f in fns:
            ins = f(eng)
            self.cnt[k] += 16
            ins.then_inc(self.sem[k], 16)
        self._record(k, self.cnt[k], reads, writes)

    def barrier(self):
        for e in self.eng:
            deps = {k: self.cnt[k] for k in self.sem if self.cnt[k] > 0}
            self._emit_waits(e, deps)

    def final_wait(self, e="sp"):
        deps = {k: self.cnt[k] for k in self.sem if self.cnt[k] > 0}
        self._emit_waits(e, deps)


def build_nc(layers=(0, 1, 2, 3), npass=NPASS, ssm_stage=4):
    nc = bass.Bass("TRN2", target_bir_lowering=False)

    def din(name, shape):
        return nc.dram_tensor(name, list(shape), F32, kind="ExternalInput").ap()

    def dout(name, shape):
        return nc.dram_tensor(name, list(shape), F32, kind="ExternalOutput").ap()

    xp = din("xp", [2048, D])
    xs = din("xs", [NTB, D])
    sconv = din("sconv", [2, NSEQ, 30, D])
    sre = din("sre", [2, NSEQ, 8192])
    sim = din("sim", [2, NSEQ, 8192])
    sffn = din("sffn", [4, NSEQ * 2, FF])
    norm_mix = din("norm_mix", [4, D])
    norm_ffn = din("norm_ffn", [4, D])
    norm_final = din("norm_final", [1, D])
    conv_w_in = din("conv_w_in", [2, D, 2 * D])
    conv_dw = din("conv_dw", [2, 31, D])
    conv_dw_b = din("conv_dw_b", [2, D])
    conv_ln_g = din("conv_ln_g", [2, D])
    conv_ln_b = din("conv_ln_b", [2, D])
    conv_w_out = din("conv_w_out", [2, D, D])
    ssm_A_re = din("ssm_A_re", [2, 128, 64])
    ssm_A_im = din("ssm_A_im", [2, 128, 64])
    ssm_log_dt = din("ssm_log_dt", [2, 128])
    ssm_B_re = din("ssm_B_re", [2, 128, 1024])
    ssm_B_im = din("ssm_B_im", [2, 128, 1024])
    ssm_C_re = din("ssm_C_re", [2, 128, 1024])
    ssm_C_im = din("ssm_C_im", [2, 128, 1024])
    ssm_D = din("ssm_D", [2, D])
    ssm_w_glu = din("ssm_w_glu", [2, D, 2 * D])
    ffn_w_gate = din("ffn_w_gate", [4, D, FF])
    ffn_w_up = din("ffn_w_up", [4, D, FF])
    ffn_conv = din("ffn_conv", [4, 3, FF])
    ffn_w_down = din("ffn_w_down", [4, FF, D])

    yp = dout("yp", [2048, D])
    ys = dout("ys", [NTB, D])
    convp = dout("convp", [2, 30, D])
    convs = dout("convs", [2, NSEQ, 30, D])
    rep = dout("rep", [2, 128, 64])
    imp = dout("imp", [2, 128, 64])
    res_ = dout("res", [2, NSEQ, 8192])
    ims_ = dout("ims", [2, NSEQ, 8192])
    ffnp = dout("ffnp", [4, 2, FF])
    ffns = dout("ffns", [4, NSEQ * 2, FF])

    scrA = nc.dram_tensor("scrA", [2, 2, 128, 64], F32, kind="Internal").ap()
    scrB = nc.dram_tensor("scrB", [2, 2, 128, 1024], F32, kind="Internal").ap()
    scrBb = nc.dram_tensor("scrBb", [2, 2, 128, DC * 128], BF16, kind="Internal").ap()
    scrCb = nc.dram_tensor("scrCb", [2, 2, 128, 64 * 32], BF16, kind="Internal").ap()

    with ExitStack() as st:
        S = Sched(nc, st, n_dma_sems=40)

        def T(name, shape, dt=F32):
            return st.enter_context(nc.sbuf_tensor(name, list(shape), dt))

        xT = T("xT", [128, DC, NT])
        hn = T("hn", [128, DC, NT], BF16)
        cb = T("cb", [128, DC, NT], BF16)
        rs = T("rs", [128, NT])
        rs2 = T("rs2", [128, NT])
        tmpA = [T("tmpA%d" % i, [128, NT]) for i in range(3)]
        xin = [T("xin%d" % i, [128, D]) for i in range(2)]
        NWS = 4
        wsl = [T("wsl%d" % i, [128, 16, 256], BF16) for i in range(NWS)]
        ident = T("ident", [128, 128])
        identb = T("identb", [128, 128], BF16)
        ones = T("ones", [128, 128])
        onesb = T("onesb", [128, 128], BF16)
        epst = T("epst", [128, 1])
        g_mix = T("g_mix", [128, 64])
        g_ffn = T("g_ffn", [128, 64])
        g_fin = T("g_fin", [128, 16])
        cdwb = T("cdwb", [128, 32])
        clng = T("clng", [128, 32])
        clnb = T("clnb", [128, 32])
        sDv = T("sDv", [128, 32])
        fcw = T("fcw", [128, 4 * 3 * FCH])
        dwT = T("dwT", [128, 2 * 31 * DC])
        halo_c = [T("halo_c%d" % j, [128, DC, 30], BF16) for j in range(2)]
        halo_f = [T("halo_f%d" % i, [128, FCH, 2]) for i in range(4)]
        stg = T("stg", [128, 128])
        apair = [[T("apair%d%d" % (j, r), [128, 64]) for r in range(2)] for j in range(2)]
        carry = [[T("carry%d%d" % (j, r), [128, 64]) for r in range(2)] for j in range(2)]
        ps = [st.enter_context(nc.psum_tensor("ps%d" % i, [128, 512], F32)) for i in range(8)]
        psn = [0]
        uid = [0]

        def nextps():
            i = psn[0] % 8
            psn[0] += 1
            return ("ps", i), ps[i]

        wsn = [0]

        def nextws():
            i = wsn[0] % NWS
            wsn[0] += 1
            return ("ws", i), wsl[i]

        tan = [0]

        def nexttmp():
            i = tan[0] % 3
            tan[0] += 1
            return ("tmpA", i), tmpA[i]

        S.op("pool", lambda e: e.memset(ones[:], 1.0), writes=["ones"])
        S.op("pool", lambda e: e.memset(ident[:], 1.0), writes=["ident"])
        S.op("pool", lambda e: e.affine_select(out=ident[:], in_=ident[:], pattern=[[1, 128]],
                                               compare_op=ALU.is_equal, fill=0.0, base=0,
                                               channel_multiplier=-1),
             reads=["ident"], writes=["ident"])
        S.op("dve", lambda e: e.tensor_copy(out=identb[:], in_=ident[:]), reads=["ident"], writes=["identb"])
        S.op("dve", lambda e: e.memset(epst[:], EPS), writes=["epst"])
        S.op("dve", lambda e: e.memset(onesb[:], 1.0), writes=["onesb"])
        for j in range(2):
            S.op("dve", lambda e, j=j: e.memset(halo_c[j][:], 0.0), writes=[("halo_c", j)])
        for i in range(4):
            S.op("dve", lambda e, i=i: e.memset(halo_f[i][:], 0.0), writes=[("halo_f", i)])

        def load_vecT(src_rows_ap, nrows, dst, dst_col0):
            r0 = 0
            while r0 < nrows:
                n = min(128, nrows - r0)
                S.dma("sp", "stg", lambda e, r0=r0, n=n: e.dma_start(out=stg[0:n, :], in_=src_rows_ap[r0:r0 + n, :]),
                      writes=["stg"])
                pk, pt = nextps()
                S.op("pe", lambda e, n=n, pt=pt: e.transpose(pt[:, 0:n], stg[0:n, :], ident[0:n, 0:n]),
                     reads=["stg", "ident"], writes=[pk])
                S.op("act", lambda e, n=n, pt=pt, r0=r0: e.activation(out=dst[:, dst_col0 + r0:dst_col0 + r0 + n],
                                                                   in_=pt[:, 0:n], func=AF.Copy),
                     reads=[pk], writes=[("vec", id(dst))])
                r0 += n

        load_vecT(norm_mix.rearrange("l (c p) -> (l c) p", p=128), 64, g_mix, 0)
        load_vecT(norm_ffn.rearrange("l (c p) -> (l c) p", p=128), 64, g_ffn, 0)
        load_vecT(norm_final.rearrange("l (c p) -> (l c) p", p=128), 16, g_fin, 0)
        load_vecT(conv_dw_b.rearrange("l (c p) -> (l c) p", p=128), 32, cdwb, 0)
        load_vecT(conv_ln_g.rearrange("l (c p) -> (l c) p", p=128), 32, clng, 0)
        load_vecT(conv_ln_b.rearrange("l (c p) -> (l c) p", p=128), 32, clnb, 0)
        load_vecT(ssm_D.rearrange("l (c p) -> (l c) p", p=128), 32, sDv, 0)
        load_vecT(ffn_conv.rearrange("l k (c p) -> (l k c) p", p=128), 4 * 3 * FCH, fcw, 0)
        load_vecT(conv_dw.rearrange("l k (c p) -> (l k c) p", p=128), 2 * 31 * DC, dwT, 0)
        VEC = [("vec", id(t)) for t in (g_mix, g_ffn, g_fin, cdwb, clng, clnb, sDv, fcw, dwT)]

        def blocks_of(p):
            return [(0, NTA)] + ([(NTA, NTB)] if p == npass - 1 else [])

        def load_x(p):
            srcs = [(xp, p * NTA + r * 128, 128, r * 128) for r in range(4)]
            if p == npass - 1:
                srcs.append((xs, 0, NTB, NTA))
            for si, (src, row0, n, col0) in enumerate(srcs):
                xi = si % 2
                S.dma("sp", ("xin", xi), lambda e, src=src, row0=row0, n=n, xi=xi:
                      e.dma_start(out=xin[xi][0:n, :], in_=src[row0:row0 + n, :]), writes=[("xin", xi)])
                for c4 in range(4):
                    pk, pt = nextps()
                    S.op("pe", [lambda e, c=c, n=n, xi=xi, pt=pt, c4=c4: e.transpose(
                        pt[:, (c - 4 * c4) * 128:(c - 4 * c4) * 128 + n], xin[xi][0:n, c * 128:(c + 1) * 128],
                        ident[0:n, 0:n]) for c in range(4 * c4, 4 * c4 + 4)],
                        reads=[("xin", xi), "ident"], writes=[pk])
                    S.op("act", lambda e, c4=c4, n=n, col0=col0, pt=pt: e.activation(
                        out=xT[:, 4 * c4:4 * c4 + 4, col0:col0 + n],
                        in_=pt[:, :].rearrange("p (c t) -> p c t", t=128)[:, :, 0:n], func=AF.Copy),
                        reads=[pk], writes=[("xT", c) for c in range(4 * c4, 4 * c4 + 4)])

        def rmsnorm(p, gain, gcol0, dst_fn=None, dst_key="hn"):
            blks = blocks_of(p)
            ncol = blks[-1][0] + blks[-1][1]
            pks = [nextps() for _ in blks]
            for c in range(DC):
                tk, tt = nexttmp()
                S.op("act", lambda e, c=c, tt=tt: e.activation(out=tt[:, 0:ncol], in_=xT[:, c, 0:ncol], func=AF.Square),
                     reads=[("xT", c)], writes=[tk])
                for (pk, pt), (b0, bn) in zip(pks, blks):
                    S.op("pe", lambda e, c=c, pt=pt, b0=b0, bn=bn, tt=tt: e.matmul(
                        pt[:, 0:bn], ones[:], tt[:, b0:b0 + bn], start=(c == 0), stop=(c == DC - 1)),
                        reads=[tk, "ones"], writes=[pk])
            for (pk, pt), (b0, bn) in zip(pks, blks):
                S.op("act", lambda e, pt=pt, b0=b0, bn=bn: e.activation(
                    out=rs[:, b0:b0 + bn], in_=pt[:, 0:bn], func=AF.Sqrt, scale=1.0 / D, bias=epst[:, 0:1]),
                    reads=[pk, "epst"], writes=["rs"])
            S.op("dve", lambda e: e.reciprocal(out=rs[:, 0:ncol], in_=rs[:, 0:ncol]), reads=["rs"], writes=["rs"])
            if dst_fn is None:
                for c in range(DC):
                    S.op("dve", lambda e, c=c: e.scalar_tensor_tensor(
                        out=hn[:, c, 0:ncol], in0=xT[:, c, 0:ncol], scalar=gain[:, gcol0 + c:gcol0 + c + 1],
                        in1=rs[:, 0:ncol], op0=ALU.mult, op1=ALU.mult),
                        reads=[("xT", c), "rs"] + VEC, writes=[(dst_key, c)])

        pre_loaded = {}

        def _load_group(srcs, kc_n, g):
            lst = []
            for (W, row0, col0) in srcs:
                wk, wt = nextws()
                S.dma("pool", wk, lambda e, W=W, row0=row0, col0=col0, wt=wt, g=g: e.dma_start(
                    out=wt[:, 0:kc_n, :],
                    in_=W[row0:row0 + kc_n * 128, col0 + 256 * g:col0 + 256 * g + 256].rearrange(
                        "(k p) m -> p k m", p=128)), writes=[wk])
                lst.append((wk, wt))
            return lst

        def gemm(p, srcs, kc_n, ngroups, rhs_fn, rhs_keys, epi, tag=None, nxt=None):
            blks = blocks_of(p)
            loaded = {}
            if tag is not None and tag in pre_loaded:
                loaded[0] = pre_loaded.pop(tag)
            else:
                loaded[0] = _load_group(srcs, kc_n, 0)
            for g in range(ngroups):
                if g + 1 < ngroups:
                    loaded[g + 1] = _load_group(srcs, kc_n, g + 1)
                elif nxt is not None and len(srcs) + len(nxt[1]) <= NWS:
                    pre_loaded[nxt[0]] = _load_group(nxt[1], nxt[2], 0)
                for ci in range(2):
                    for bi, (b0, bn) in enumerate(blks):
                        pls = []
                        for (wk, wt) in loaded[g]:
                            pk, pt = nextps()
                            S.op("pe", [lambda e, kc=kc, wt=wt, pt=pt, b0=b0, bn=bn, ci=ci: e.matmul(
                                pt[:, 0:bn], wt[:, kc, ci * 128:(ci + 1) * 128], rhs_fn(kc, b0, bn),
                                start=(kc == 0), stop=(kc == kc_n - 1)) for kc in range(kc_n)],
                                reads=[wk] + rhs_keys, writes=[pk])
                            pls.append((pk, pt))
                        epi(g, ci, bi, b0, bn, pls)
                del loaded[g]

        def glu_residual_epi(g, ci, bi, b0, bn, pls):
            mc = 2 * g + ci
            (pkv, ptv), (pkg, ptg) = pls
            tk, tt = nexttmp()
            S.op("act", lambda e: e.activation(out=tt[:, 0:bn], in_=ptg[:, 0:bn], func=AF.Sigmoid),
                 reads=[pkg], writes=[tk])
            S.op("dve", lambda e: e.tensor_tensor(out=tt[:, 0:bn], in0=ptv[:, 0:bn], in1=tt[:, 0:bn], op=ALU.mult),
                 reads=[pkv, tk], writes=[tk])
            S.op("dve", lambda e: e.tensor_tensor(out=xT[:, mc, b0:b0 + bn], in0=xT[:, mc, b0:b0 + bn],
                                                  in1=tt[:, 0:bn], op=ALU.add),
                 reads=[tk, ("xT", mc)], writes=[("xT", mc)])

        def residual_epi(g, ci, bi, b0, bn, pls):
            mc = 2 * g + ci
            (pk, pt), = pls
            S.op("dve", lambda e: e.tensor_tensor(out=xT[:, mc, b0:b0 + bn], in0=xT[:, mc, b0:b0 + bn],
                                                  in1=pt[:, 0:bn], op=ALU.add),
                 reads=[pk, ("xT", mc)], writes=[("xT", mc)])

        HN_KEYS = [("hn", c) for c in range(DC)]

        def conv_layer(p, li, j):
            blks = blocks_of(p)
            last = (p == npass - 1)
            ncol = blks[-1][0] + blks[-1][1]
            with ExitStack() as ls:
                def LT(name, shape, dt=F32):
                    uid[0] += 1
                    return ls.enter_context(nc.sbuf_tensor("%s_%d" % (name, uid[0]), list(shape), dt))
                uxp = LT("uxp", [128, DC, 30 + NTA], BF16)
                uxs = LT("uxs", [128, DC, NSEQ, 34], BF16)
                dg = [LT("dg%d" % i, [128, 31, 128], BF16) for i in range(1)]
                ulast = LT("ulast", [128, DC, 94])
                rmsnorm(p, g_mix, li * DC)
                for c in range(DC):
                    S.op("act", lambda e, c=c: e.activation(out=uxp[:, c, 0:30], in_=halo_c[j][:, c, :], func=AF.Copy),
                         reads=[("halo_c", j)], writes=[("uxp", c)])
                if last:
                    for r4 in range(4):
                        xi = r4 % 2
                        S.dma("sp", ("xin", xi), lambda e, r4=r4, xi=xi: e.dma_start(
                            out=xin[xi][0:120, :],
                            in_=sconv[j, 4 * r4:4 * r4 + 4].rearrange("s r d -> (s r) d")), writes=[("xin", xi)])
                        for c4 in range(4):
                            pk, pt = nextps()
                            S.op("pe", [lambda e, c=c, xi=xi, pt=pt, c4=c4: e.transpose(
                                pt[:, (c - 4 * c4) * 128:(c - 4 * c4) * 128 + 120], xin[xi][0:120, c * 128:(c + 1) * 128],
                                ident[0:120, 0:120]) for c in range(4 * c4, 4 * c4 + 4)],
                                reads=[("xin", xi), "ident"], writes=[pk])
                            for cc in range(4):
                                c = 4 * c4 + cc
                                S.op("act", lambda e, c=c, cc=cc, r4=r4, pt=pt: e.activation(
                                    out=uxs[:, c, 4 * r4:4 * r4 + 4, 0:30],
                                    in_=pt[:, cc * 128:cc * 128 + 120].rearrange("p (s r) -> p s r", r=30),
                                    func=AF.Copy), reads=[pk], writes=[("uxs", c)])
                    S.dma("sp", "cs_copy", lambda e: e.dma_start(out=convs[j, :, 0:26, :], in_=sconv[j, :, 4:30, :]))

                def glu_u_epi(g, ci, bi, b0, bn, pls):
                    mc = 2 * g + ci
                    (pkv, ptv), (pkg, ptg) = pls
                    tk, tt = nexttmp()
                    S.op("act", lambda e: e.activation(out=tt[:, 0:bn], in_=ptg[:, 0:bn], func=AF.Sigmoid),
                         reads=[pkg], writes=[tk])
                    if bi == 0:
                        S.op("dve", lambda e: e.tensor_tensor(out=uxp[:, mc, 30:30 + NTA], in0=ptv[:, 0:NTA],
                                                              in1=tt[:, 0:NTA], op=ALU.mult),
                             reads=[pkv, tk], writes=[("uxp", mc)])
                        S.op("act", lambda e: e.activation(out=halo_c[j][:, mc, :], in_=uxp[:, mc, NTA:NTA + 30],
                                                           func=AF.Copy),
                             reads=[("uxp", mc)], writes=[("halo_c", j)])
                        if last:
                            S.op("dve", lambda e: e.tensor_tensor(out=ulast[:, mc, 0:30], in0=ptv[:, NTA - 30:NTA],
                                                                  in1=tt[:, NTA - 30:NTA], op=ALU.mult),
                                 reads=[pkv, tk], writes=[("ulast", mc)])
                    else:
                        S.op("dve", lambda e: e.tensor_tensor(
                            out=uxs[:, mc, :, 30:34], in0=ptv[:, 0:NTB].rearrange("p (s t) -> p s t", t=4),
                            in1=tt[:, 0:NTB].rearrange("p (s t) -> p s t", t=4), op=ALU.mult),
                            reads=[pkv, tk], writes=[("uxs", mc)])
                        S.op("dve", lambda e: e.tensor_tensor(out=ulast[:, mc, 30:94], in0=ptv[:, 0:NTB],
                                                              in1=tt[:, 0:NTB], op=ALU.mult),
                             reads=[pkv, tk], writes=[("ulast", mc)])

                W = conv_w_in[j]
                gemm(p, [(W, 0, 0), (W, 0, D)], DC, 8, lambda kc, b0, bn: hn[:, kc, b0:b0 + bn], HN_KEYS, glu_u_epi)

                if last:
                    for c4 in range(4):
                        pk, pt = nextps()
                        S.op("pe", [lambda e, c=c, pt=pt, c4=c4: e.transpose(
                            pt[0:94, (c - 4 * c4) * 128:(c - 4 * c4 + 1) * 128], ulast[:, c, :], ident[:, :])
                            for c in range(4 * c4, 4 * c4 + 4)],
                            reads=[("ulast", c) for c in range(4 * c4, 4 * c4 + 4)] + ["ident"], writes=[pk])
                        S.op("act", lambda e, c4=c4, pt=pt: e.activation(out=xin[0][0:94, c4 * 512:(c4 + 1) * 512],
                                                                      in_=pt[0:94, :], func=AF.Copy),
                             reads=[pk], writes=[("xin", 0)])
                    S.dma("sp", ("xin", 0), [lambda e: e.dma_start(out=convp[j], in_=xin[0][0:30, :])] +
                          [lambda e, s=s: e.dma_start(out=convs[j, s, 26:30, :], in_=xin[0][30 + 4 * s:34 + 4 * s, :])
                           for s in range(NSEQ)], reads=[("xin", 0)])

                for c in range(DC):
                    di = 0
                    S.op("dve", lambda e, c=c, di=di: e.tensor_tensor(
                        out=dg[di][:, :, :], in0=identb[:, :].unsqueeze(1).broadcast_to([128, 31, 128]),
                        in1=dwT[:, (j * 31) * DC + c:(j * 31 + 31) * DC:DC].unsqueeze(2).broadcast_to([128, 31, 128]),
                        op=ALU.mult), reads=["identb"] + VEC, writes=[("dg", di)])
                    for bi, (b0, bn) in enumerate(blks):
                        pk, pt = nextps()
                        if bi == 0:
                            S.op("pe", [lambda e, k=k, c=c, di=di, pt=pt: e.matmul(
                                pt[:, 0:NTA], dg[di][:, k, :], uxp[:, c, k:k + NTA], start=(k == 0), stop=(k == 30))
                                for k in range(31)], reads=[("dg", di), ("uxp", c)], writes=[pk])
                        else:
                            S.op("pe", [lambda e, k=k, c=c, di=di, pt=pt: e.matmul(
                                pt[:, 0:NTB].rearrange("p (s t) -> p s t", t=4), dg[di][:, k, :], uxs[:, c, :, k:k + 4],
                                start=(k == 0), stop=(k == 30)) for k in range(31)],
                                reads=[("dg", di), ("uxs", c)], writes=[pk])
                        S.op("act", lambda e, c=c, pt=pt, b0=b0, bn=bn: e.activation(
                            out=cb[:, c, b0:b0 + bn], in_=pt[:, 0:bn], func=AF.Identity,
                            bias=cdwb[:, j * DC + c:j * DC + c + 1]), reads=[pk] + VEC, writes=[("cb", c)])
                pk1 = [nextps() for _ in blks]
                pk2 = [nextps() for _ in blks]
                for c in range(DC):
                    tk, tt = nexttmp()
                    S.op("act", lambda e, c=c, tt=tt: e.activation(out=tt[:, 0:ncol], in_=cb[:, c, 0:ncol], func=AF.Square),
                         reads=[("cb", c)], writes=[tk])
                    for (pka, pta), (pkb, ptb), (b0, bn) in zip(pk1, pk2, blks):
                        S.op("pe", lambda e, c=c, pta=pta, b0=b0, bn=bn: e.matmul(
                            pta[:, 0:bn], onesb[:], cb[:, c, b0:b0 + bn], start=(c == 0), stop=(c == DC - 1)),
                            reads=[("cb", c), "onesb"], writes=[pka])
                        S.op("pe", lambda e, c=c, ptb=ptb, b0=b0, bn=bn, tt=tt: e.matmul(
                            ptb[:, 0:bn], ones[:], tt[:, b0:b0 + bn], start=(c == 0), stop=(c == DC - 1)),
                            reads=[tk, "ones"], writes=[pkb])
                for (pka, pta), (pkb, ptb), (b0, bn) in zip(pk1, pk2, blks):
                    S.op("act", lambda e, pta=pta, b0=b0, bn=bn: e.activation(
                        out=rs2[:, b0:b0 + bn], in_=pta[:, 0:bn], func=AF.Copy, scale=1.0 / D), reads=[pka], writes=["rs2"])
                    tk, tt = nexttmp()
                    S.op("dve", lambda e, tt=tt, b0=b0, bn=bn: e.tensor_tensor(
                        out=tt[:, 0:bn], in0=rs2[:, b0:b0 + bn], in1=rs2[:, b0:b0 + bn], op=ALU.mult),
                        reads=["rs2"], writes=[tk])
                    S.op("dve", lambda e, tt=tt, ptb=ptb, b0=b0, bn=bn: e.scalar_tensor_tensor(
                        out=tt[:, 0:bn], in0=ptb[:, 0:bn], scalar=1.0 / D, in1=tt[:, 0:bn], op0=ALU.mult,
                        op1=ALU.subtract), reads=[pkb, tk], writes=[tk])
                    S.op("act", lambda e, tt=tt, b0=b0, bn=bn: e.activation(
                        out=rs[:, b0:b0 + bn], in_=tt[:, 0:bn], func=AF.Sqrt, bias=epst[:, 0:1]),
                        reads=[tk, "epst"], writes=["rs"])
                S.op("dve", lambda e: e.reciprocal(out=rs[:, 0:ncol], in_=rs[:, 0:ncol]), reads=["rs"], writes=["rs"])
                for c in range(DC):
                    tk, tt = nexttmp()
                    S.op("dve", lambda e, c=c, tt=tt: e.tensor_tensor(out=tt[:, 0:ncol], in0=cb[:, c, 0:ncol],
                                                               in1=rs2[:, 0:ncol], op=ALU.subtract),
                         reads=[("cb", c), "rs2"], writes=[tk])
                    S.op("dve", lambda e, c=c, tt=tt: e.tensor_tensor(out=tt[:, 0:ncol], in0=tt[:, 0:ncol],
                                                               in1=rs[:, 0:ncol], op=ALU.mult),
                         reads=[tk, "rs"], writes=[tk])
                    S.op("act", lambda e, c=c, tt=tt: e.activation(
                        out=hn[:, c, 0:ncol], in_=tt[:, 0:ncol], func=AF.Silu,
                        scale=clng[:, j * DC + c:j * DC + c + 1], bias=clnb[:, j * DC + c:j * DC + c + 1]),
                        reads=[tk] + VEC, writes=[("hn", c)])
                gemm(p, [(conv_w_out[j], 0, 0)], DC, 8, lambda kc, b0, bn: hn[:, kc, b0:b0 + bn], HN_KEYS, residual_epi)
                S.barrier()

        def ffn_layer(p, li):
            blks = blocks_of(p)
            last = (p == npass - 1)
            ncol = blks[-1][0] + blks[-1][1]
            QS = [(0, 12), (12, 12), (24, 12), (36, 8)]
            with ExitStack() as ls:
                def LT(name, shape, dt=F32):
                    uid[0] += 1
                    return ls.enter_context(nc.sbuf_tensor("%s_%d" % (name, uid[0]), list(shape), dt))
                aT = LT("aT", [128, 12, NT], BF16)
                gxp = [LT("gxp%d" % i, [128, 2 + NTA]) for i in range(3)]
                gxs = [LT("gxs%d" % i, [128, NSEQ, 6]) for i in range(3)]
                gcs = LT("gcs", [128, FCH, NSEQ, 2]) if last else None
                glast = LT("glast", [128, FCH, 34]) if last else None
                rmsnorm(p, g_ffn, li * DC)
                if last:
                    for h in range(3):
                        c0 = h * 2048
                        cn_ = min(2048, FF - c0)
                        xi = h % 2
                        S.dma("sp", ("xin", xi), lambda e, c0=c0, cn_=cn_, xi=xi: e.dma_start(
                            out=xin[xi][0:32, 0:cn_], in_=sffn[li, :, c0:c0 + cn_]), writes=[("xin", xi)])
                        for c4 in range(cn_ // 512):
                            pk, pt = nextps()
                            S.op("pe", [lambda e, cc=cc, xi=xi, pt=pt, c4=c4: e.transpose(
                                pt[:, cc * 128:cc * 128 + 32], xin[xi][0:32, (4 * c4 + cc) * 128:(4 * c4 + cc + 1) * 128],
                                ident[0:32, 0:32]) for cc in range(4)],
                                reads=[("xin", xi), "ident"], writes=[pk])
                            ch0 = h * 16 + 4 * c4
                            S.op("act", lambda e, ch0=ch0, pt=pt: e.activation(
                                out=gcs[:, ch0:ch0 + 4, :, :].rearrange("p c s r -> p c (s r)"),
                                in_=pt[:, :].rearrange("p (c t) -> p c t", t=128)[:, :, 0:32], func=AF.Copy),
                                reads=[pk], writes=["gcs"])
                gi = [0]

                def ffn_epi_factory(q0):
                    def epi(g, ci, bi, b0, bn, pls):
                        fl = 2 * g + ci
                        f = q0 + fl
                        (pkg, ptg), (pku, ptu) = pls
                        w0 = fcw[:, (li * 3 + 0) * FCH + f:(li * 3 + 0) * FCH + f + 1]
                        w1 = fcw[:, (li * 3 + 1) * FCH + f:(li * 3 + 1) * FCH + f + 1]
                        w2 = fcw[:, (li * 3 + 2) * FCH + f:(li * 3 + 2) * FCH + f + 1]
                        i3 = gi[0] % 3
                        gi[0] += 1
                        tk, tt = nexttmp()
                        if bi == 0:
                            gx = gxp[i3]
                            gk = ("gxp", i3)
                            S.op("act", lambda e: e.activation(out=gx[:, 0:2], in_=halo_f[li][:, f, :], func=AF.Copy),
                                 reads=[("halo_f", li)], writes=[gk])
                            S.op("act", lambda e: e.activation(out=gx[:, 2:2 + NTA], in_=ptg[:, 0:NTA], func=AF.Copy),
                                 reads=[pkg, gk], writes=[gk])
                            S.op("act", lambda e: e.activation(out=halo_f[li][:, f, :], in_=gx[:, NTA:NTA + 2], func=AF.Copy),
                                 reads=[gk], writes=[("halo_f", li)])
                            if last:
                                S.op("act", lambda e: e.activation(out=glast[:, f, 0:2], in_=gx[:, NTA:NTA + 2],
                                                                   func=AF.Copy), reads=[gk], writes=["glast"])
                            a0, a1, a2 = gx[:, 0:NTA], gx[:, 1:1 + NTA], gx[:, 2:2 + NTA]
                            to = tt[:, 0:NTA]
                            pu = ptu[:, 0:NTA]
                            ao = aT[:, fl, 0:NTA]
                        else:
                            gx = gxs[i3]
                            gk = ("gxs", i3)
                            S.op("act", lambda e: e.activation(out=gx[:, :, 0:2], in_=gcs[:, f, :, :], func=AF.Copy),
                                 reads=["gcs"], writes=[gk])
                            S.op("act", lambda e: e.activation(
                                out=gx[:, :, 2:6], in_=ptg[:, 0:NTB].rearrange("p (s t) -> p s t", t=4), func=AF.Copy),
                                reads=[pkg, gk], writes=[gk])
                            S.op("act", lambda e: e.activation(
                                out=glast[:, f, 2:34].rearrange("p (s r) -> p s r", r=2), in_=gx[:, :, 4:6], func=AF.Copy),
                                reads=[gk], writes=["glast"])
                            a0, a1, a2 = gx[:, :, 0:4], gx[:, :, 1:5], gx[:, :, 2:6]
                            to = tt[:, 0:NTB].rearrange("p (s t) -> p s t", t=4)
                            pu = ptu[:, 0:NTB].rearrange("p (s t) -> p s t", t=4)
                            ao = aT[:, fl, NTA:NT].rearrange("p (s t) -> p s t", t=4)
                        S.op("dve", lambda e: e.tensor_scalar(out=to, in0=a0, scalar1=w0, scalar2=None, op0=ALU.mult),
                             reads=[gk] + VEC, writes=[tk])
                        S.op("dve", lambda e: e.scalar_tensor_tensor(out=to, in0=a1, scalar=w1, in1=to, op0=ALU.mult,
                                                                     op1=ALU.add), reads=[gk, tk], writes=[tk])
                        S.op("dve", lambda e: e.scalar_tensor_tensor(out=to, in0=a2, scalar=w2, in1=to, op0=ALU.mult,
                                                                     op1=ALU.add), reads=[gk, tk], writes=[tk])
                        S.op("act", lambda e: e.activation(out=to, in_=to, func=AF.Silu), reads=[tk], writes=[tk])
                        S.op("dve", lambda e: e.tensor_tensor(out=ao, in0=pu, in1=to, op=ALU.mult),
                             reads=[tk, pku], writes=[("aT", fl)])
                    return epi

                AT_KEYS = [("aT", f) for f in range(12)]
                def gu_srcs(q0):
                    return [(ffn_w_gate[li], 0, q0 * 128), (ffn_w_up[li], 0, q0 * 128)]

                def dn_srcs(q0):
                    return [(ffn_w_down[li], q0 * 128, 0)]
                for qi, (q0, qn) in enumerate(QS):
                    gemm(p, gu_srcs(q0), DC, qn // 2, lambda kc, b0, bn: hn[:, kc, b0:b0 + bn], HN_KEYS, ffn_epi_factory(q0),
                         tag=("gu", p, li, qi), nxt=(("dn", p, li, qi), dn_srcs(q0), qn))
                    nq = QS[qi + 1] if qi + 1 < len(QS) else None
                    gemm(p, dn_srcs(q0), qn, 8, lambda kc, b0, bn: aT[:, kc, b0:b0 + bn], AT_KEYS, residual_epi,
                         tag=("dn", p, li, qi),
                         nxt=((("gu", p, li, qi + 1), gu_srcs(nq[0]), DC) if nq is not None else None))
                if last:
                    for h in range(3):
                        c0 = h * 2048
                        cn_ = min(2048, FF - c0)
                        xi = h % 2
                        for c4 in range(cn_ // 512):
                            pk, pt = nextps()
                            S.op("pe", [lambda e, cc=cc, pt=pt, c4=c4, h=h: e.transpose(
                                pt[0:34, cc * 128:(cc + 1) * 128], glast[:, h * 16 + 4 * c4 + cc, :], ident[:, :])
                                for cc in range(4)], reads=["glast", "ident"], writes=[pk])
                            S.op("act", lambda e, c4=c4, pt=pt, xi=xi: e.activation(
                                out=xin[xi][0:34, c4 * 512:(c4 + 1) * 512], in_=pt[0:34, :], func=AF.Copy),
                                reads=[pk], writes=[("xin", xi)])
                        S.dma("sp", ("xin", xi), [
                            lambda e, c0=c0, cn_=cn_, xi=xi: e.dma_start(out=ffnp[li][:, c0:c0 + cn_], in_=xin[xi][0:2, 0:cn_]),
                            lambda e, c0=c0, cn_=cn_, xi=xi: e.dma_start(out=ffns[li][:, c0:c0 + cn_], in_=xin[xi][2:34, 0:cn_])],
                            reads=[("xin", xi)])
                S.barrier()


        TWO_PI = 2.0 * np.pi

        def ssm_prep(j):
            with ExitStack() as ls:
                def LT(name, shape, dt=F32):
                    uid[0] += 1
                    return ls.enter_context(nc.sbuf_tensor("%s_%d" % (name, uid[0]), list(shape), dt))
                k = [0]

                def key(n):
                    return ("pp", n)
                Are = LT("Are", [128, 64]); Aim = LT("Aim", [128, 64]); ldt = LT("ldt", [128, 1])
                Bre = LT("Bre", [128, 1024]); Bim = LT("Bim", [128, 1024])
                S.dma("sp", "prep_in", [
                    lambda e: e.dma_start(out=Are[:], in_=ssm_A_re[j]),
                    lambda e: e.dma_start(out=Aim[:], in_=ssm_A_im[j]),
                    lambda e: e.dma_start(out=ldt[:], in_=ssm_log_dt[j].rearrange("(g o) -> g o", o=1)),
                    lambda e: e.dma_start(out=Bre[:], in_=ssm_B_re[j]),
                    lambda e: e.dma_start(out=Bim[:], in_=ssm_B_im[j])], writes=["pin"])
                dtt = LT("dtt", [128, 1]); lr = LT("lr", [128, 64]); li = LT("li", [128, 64])
                mag = LT("mag", [128, 64]); t1 = LT("t1", [128, 64]); t2 = LT("t2", [128, 64])
                ki = LT("ki", [128, 64], mybir.dt.int32)
                cs = [LT("cs0", [128, 64]), LT("cs1", [128, 64])]
                ab = [LT("ab0", [128, 64]), LT("ab1", [128, 64])]
                qq = [LT("q0", [128, 64]), LT("q1", [128, 64])]
                Bb = [LT("Bb0", [128, 1024]), LT("Bb1", [128, 1024])]
                tb = LT("tb", [128, 1024])
                PK = ["pin", "pw"]

                def dv(fn):
                    S.op("dve", fn, reads=PK, writes=["pw"])

                def ac(fn):
                    S.op("act", fn, reads=PK, writes=["pw"])
                ac(lambda e: e.activation(out=dtt[:], in_=ldt[:], func=AF.Exp))
                dv(lambda e: e.tensor_scalar(out=lr[:], in0=Are[:], scalar1=dtt[:, 0:1], scalar2=None, op0=ALU.mult))
                dv(lambda e: e.tensor_scalar(out=li[:], in0=Aim[:], scalar1=dtt[:, 0:1], scalar2=None, op0=ALU.mult))
                ac(lambda e: e.activation(out=mag[:], in_=lr[:], func=AF.Exp))
                for which, shift in ((0, np.pi / 2.0), (1, 0.0)):
                    dv(lambda e, shift=shift: e.tensor_scalar(out=t1[:], in0=li[:], scalar1=shift, scalar2=None, op0=ALU.add))
                    dv(lambda e: e.tensor_scalar(out=t2[:], in0=t1[:], scalar1=1.0 / TWO_PI, scalar2=None, op0=ALU.mult))
                    dv(lambda e: e.tensor_copy(out=ki[:], in_=t2[:]))
                    dv(lambda e: e.tensor_copy(out=t2[:], in_=ki[:]))
                    dv(lambda e: e.scalar_tensor_tensor(out=t1[:], in0=t2[:], scalar=-TWO_PI, in1=t1[:], op0=ALU.mult,
                                                        op1=ALU.add))
                    dv(lambda e: e.tensor_scalar(out=t2[:], in0=t1[:], scalar1=float(np.pi), scalar2=-TWO_PI,
                                                 op0=ALU.is_gt, op1=ALU.mult))
                    dv(lambda e: e.tensor_tensor(out=t1[:], in0=t1[:], in1=t2[:], op=ALU.add))
                    dv(lambda e: e.tensor_scalar(out=t1[:], in0=t1[:], scalar1=float(np.pi), scalar2=-float(np.pi),
                                                 op0=ALU.min, op1=ALU.max))
                    ac(lambda e, which=which: e.activation(out=cs[which][:], in_=t1[:], func=AF.Sin))
                dv(lambda e: e.tensor_tensor(out=ab[0][:], in0=mag[:], in1=cs[0][:], op=ALU.mult))
                dv(lambda e: e.tensor_tensor(out=ab[1][:], in0=mag[:], in1=cs[1][:], op=ALU.mult))
                dv(lambda e: e.tensor_tensor(out=t1[:], in0=Are[:], in1=Are[:], op=ALU.mult))
                dv(lambda e: e.tensor_tensor(out=t2[:], in0=Aim[:], in1=Aim[:], op=ALU.mult))
                dv(lambda e: e.tensor_tensor(out=t1[:], in0=t1[:], in1=t2[:], op=ALU.add))
                dv(lambda e: e.reciprocal(out=t1[:], in_=t1[:]))
                dv(lambda e: e.tensor_scalar(out=mag[:], in0=ab[0][:], scalar1=-1.0, scalar2=None, op0=ALU.add))
                dv(lambda e: e.tensor_tensor(out=qq[0][:], in0=mag[:], in1=Are[:], op=ALU.mult))
                dv(lambda e: e.tensor_tensor(out=t2[:], in0=ab[1][:], in1=Aim[:], op=ALU.mult))
                dv(lambda e: e.tensor_tensor(out=qq[0][:], in0=qq[0][:], in1=t2[:], op=ALU.add))
                dv(lambda e: e.tensor_tensor(out=qq[0][:], in0=qq[0][:], in1=t1[:], op=ALU.mult))
                dv(lambda e: e.tensor_tensor(out=qq[1][:], in0=ab[1][:], in1=Are[:], op=ALU.mult))
                dv(lambda e: e.tensor_tensor(out=t2[:], in0=mag[:], in1=Aim[:], op=ALU.mult))
                dv(lambda e: e.tensor_tensor(out=qq[1][:], in0=qq[1][:], in1=t2[:], op=ALU.subtract))
                dv(lambda e: e.tensor_tensor(out=qq[1][:], in0=qq[1][:], in1=t1[:], op=ALU.mult))

                def bq(t):
                    return t[:, :].unsqueeze(2).broadcast_to([128, 64, 16])

                def v3(t):
                    return t[:, :].rearrange("g (p i) -> g p i", i=16)
                dv(lambda e: e.tensor_tensor(out=v3(Bb[0]), in0=v3(Bre), in1=bq(qq[0]), op=ALU.mult))
                dv(lambda e: e.tensor_tensor(out=v3(tb), in0=v3(Bim), in1=bq(qq[1]), op=ALU.mult))
                dv(lambda e: e.tensor_tensor(out=Bb[0][:], in0=Bb[0][:], in1=tb[:], op=ALU.subtract))
                dv(lambda e: e.tensor_tensor(out=v3(Bb[1]), in0=v3(Bim), in1=bq(qq[0]), op=ALU.mult))
                dv(lambda e: e.tensor_tensor(out=v3(tb), in0=v3(Bre), in1=bq(qq[1]), op=ALU.mult))
                dv(lambda e: e.tensor_tensor(out=Bb[1][:], in0=Bb[1][:], in1=tb[:], op=ALU.add))
                S.dma("sp", "prep_o", [lambda e, r=r: e.dma_start(out=scrA[j, r], in_=ab[r][:]) for r in range(2)] +
                      [lambda e, r=r: e.dma_start(out=scrB[j, r], in_=Bb[r][:]) for r in range(2)],
                      reads=["pw"], writes=["scr1"])
                pidx = LT("pidx", [128, 1], mybir.dt.int32); pi2 = LT("pi2", [128, 1], mybir.dt.int32)
                mpar = [LT("mpar0", [128, 1]), LT("mpar1", [128, 1])]
                mhalf = [LT("mh0", [128, 1]), LT("mh1", [128, 1])]
                nmhalf = [LT("nmh0", [128, 1]), LT("nmh1", [128, 1])]
                S.op("pool", lambda e: e.iota(pidx[:], pattern=[[0, 1]], base=0, channel_multiplier=1), writes=["pidx"])
                pf = LT("pf", [128, 1]); ptmp = LT("ptmp", [128, 1])
                S.op("dve", lambda e: e.tensor_copy(out=pf[:], in_=pidx[:]), reads=["pidx", "pw"], writes=["pw"])
                dv(lambda e: e.tensor_scalar(out=mhalf[1][:], in0=pf[:], scalar1=64.0, scalar2=None, op0=ALU.is_ge))
                dv(lambda e: e.memset(mpar[1][:], 0.0))
                for m_, (thr, sg) in enumerate(((16, 1.0), (32, -1.0), (48, 1.0), (64, -1.0), (80, 1.0), (96, -1.0), (112, 1.0))):
                    dv(lambda e, thr=thr, sg=sg: e.tensor_scalar(out=ptmp[:], in0=pf[:], scalar1=float(thr), scalar2=sg,
                                                                 op0=ALU.is_ge, op1=ALU.mult))
                    dv(lambda e: e.tensor_tensor(out=mpar[1][:], in0=mpar[1][:], in1=ptmp[:], op=ALU.add))
                for mm in (mpar, mhalf):
                    dv(lambda e, mm=mm: e.tensor_scalar(out=mm[0][:], in0=mm[1][:], scalar1=-1.0, scalar2=1.0,
                                                        op0=ALU.mult, op1=ALU.add))
                for a in range(2):
                    dv(lambda e, a=a: e.tensor_scalar(out=nmhalf[a][:], in0=mhalf[a][:], scalar1=-1.0, scalar2=None,
                                                      op0=ALU.mult))
                BbT = [LT("BbT0", [128, DC, 64]), LT("BbT1", [128, DC, 64])]
                Cp = [LT("Cp0", [128, 64, 16]), LT("Cp1", [128, 64, 16])]
                with nc.allow_non_contiguous_dma(reason="ssm weight relayout"):
                    S.dma("sp", "prep_in", [lambda e, r=r: e.dma_start(
                        out=apair[j][r][:], in_=scrA[j, r].rearrange("(q a) p -> (a p) q", a=2)) for r in range(2)],
                        reads=["scr1"], writes=[("apair", j)])
                    fl = []
                    for r in range(2):
                        vB = scrB[j, r].rearrange("(f a) (p i) -> a i f p", a=8, i=16)
                        for a in range(8):
                            for f in range(DC):
                                fl.append(lambda e, r=r, a=a, vB=vB, f=f: e.dma_start(
                                    out=BbT[r][16 * a:16 * a + 16, f, :], in_=vB[a][:, f, :]))
                        vC = (ssm_C_re, ssm_C_im)[r][j].rearrange("(q a) (j p) -> a p q j", a=2, p=64)
                        for a in range(2):
                            for q in range(64):
                                fl.append(lambda e, r=r, a=a, vC=vC, q=q: e.dma_start(
                                    out=Cp[r][64 * a:64 * a + 64, q, :], in_=vC[a][:, q, :]))
                    S.dma("sp", "prep_in", fl, reads=["scr1"], writes=["pin2"])
                Bblk = [LT("Bblk0", [128, DC, 2, 64], BF16), LT("Bblk1", [128, DC, 2, 64], BF16)]
                Cblk = [LT("Cblk0", [128, 64, 2, 16], BF16), LT("Cblk1", [128, 64, 2, 16], BF16)]
                for r in range(2):
                    for a in range(2):
                        S.op("dve", lambda e, r=r, a=a: e.tensor_scalar(
                            out=Bblk[r][:, :, a, :], in0=BbT[r][:, :, :], scalar1=mpar[a][:, 0:1], scalar2=None, op0=ALU.mult),
                            reads=["pin2", "pw"], writes=["pblk"])
                        mm = mhalf if r == 0 else nmhalf
                        S.op("dve", lambda e, r=r, a=a, mm=mm: e.tensor_scalar(
                            out=Cblk[r][:, :, a, :], in0=Cp[r][:, :, :], scalar1=mm[a][:, 0:1], scalar2=None, op0=ALU.mult),
                            reads=["pin2", "pw"], writes=["pblk"])
                S.dma("sp", "prep_o", [lambda e, r=r: e.dma_start(
                    out=scrBb[j, r], in_=Bblk[r][:, :, :, :].rearrange("p f a q -> p (f a q)")) for r in range(2)] +
                    [lambda e, r=r: e.dma_start(
                        out=scrCb[j, r], in_=Cblk[r][:, :, :, :].rearrange("p f a q -> p (f a q)")) for r in range(2)],
                    reads=["pblk"], writes=[("scrblk", j)])
                for r in range(2):
                    S.op("dve", lambda e, r=r: e.memset(carry[j][r][:], 0.0), writes=[("carry", j)])
                S.barrier()

        TS = 32
        GC1 = 2.0 * float(np.sqrt(2.0 / np.pi))

        def ssm_layer(p, li, j):
            blks = blocks_of(p)
            last = (p == npass - 1)
            with ExitStack() as ls:
                def LT(name, shape, dt=F32):
                    uid[0] += 1
                    return ls.enter_context(nc.sbuf_tensor("%s_%d" % (name, uid[0]), list(shape), dt))
                Bblk = [LT("Bblk0", [128, DC, 2, 64], BF16), LT("Bblk1", [128, DC, 2, 64], BF16)]
                Cblk = [LT("Cblk0", [128, 64, 2, 16], BF16), LT("Cblk1", [128, 64, 2, 16], BF16)]
                H = [LT("H0", [128, 64, 40]), LT("H1", [128, 64, 40])]
                Hb = [LT("Hb0", [128, 64, 32], BF16), LT("Hb1", [128, 64, 32], BF16)]
                tm = [LT("tm%d" % i, [128, 512]) for i in range(4)]
                h0s = [LT("h0s%d" % r, [128, 64, NSEQ]) for r in range(2)] if (last and ssm_stage >= 3) else None
                S.dma("sp", "ssm_w", [lambda e, r=r: e.dma_start(
                    out=Bblk[r][:, :, :, :].rearrange("p f a q -> p (f a q)"), in_=scrBb[j, r]) for r in range(2)] +
                    [lambda e, r=r: e.dma_start(
                        out=Cblk[r][:, :, :, :].rearrange("p f a q -> p (f a q)"), in_=scrCb[j, r]) for r in range(2)],
                    reads=[("scrblk", j)], writes=["ssmw"])
                if h0s is not None:
                    for r in range(2):
                        for pc in range(4):
                            xi = pc % 2
                            S.dma("sp", ("xin", xi), lambda e, r=r, pc=pc, xi=xi: e.dma_start(
                                out=xin[xi][0:NSEQ, :], in_=(sre, sim)[r][j][:, pc * 2048:(pc + 1) * 2048]),
                                writes=[("xin", xi)])
                            pk, pt = nextps()
                            S.op("pe", [lambda e, k=k, xi=xi, pt=pt: e.transpose(
                                pt[:, k * NSEQ:(k + 1) * NSEQ], xin[xi][0:NSEQ, k * 128:(k + 1) * 128],
                                ident[0:NSEQ, 0:NSEQ]) for k in range(16)],
                                reads=[("xin", xi), "ident"], writes=[pk])
                            S.op("act", lambda e, r=r, pc=pc, pt=pt: e.activation(
                                out=h0s[r][:, 16 * pc:16 * pc + 16, :],
                                in_=pt[:, 0:16 * NSEQ].rearrange("p (q s) -> p q s", s=NSEQ), func=AF.Copy),
                                reads=[pk], writes=["h0s"])
                rmsnorm(p, g_mix, li * DC)
                are, aim = apair[j]

                def run_chunk(col0, ncols, nseq, tlen, init_fn, final_fn):
                    HW = nseq * (1 + tlen)
                    Hv = [H[r][:, :, 0:HW].rearrange("p q (s t) -> p q s t", t=1 + tlen) for r in range(2)]
                    init_fn(Hv)
                    for r in range(2):
                        for q in range(4):
                            pk, pt = nextps()
                            fns = []
                            for fc in range(DC):
                                fns.append(lambda e, r=r, fc=fc, q=q, pt=pt: e.matmul(
                                    pt[:, fc * ncols:(fc + 1) * ncols],
                                    Bblk[r][32 * q:32 * q + 32, fc, :, :].rearrange("p a q -> p (a q)"),
                                    hn[32 * q:32 * q + 32, fc, col0:col0 + ncols], start=True, stop=True,
                                    tile_position=(32 * q, 0)))
                            S.op("pe", fns, reads=["ssmw"] + HN_KEYS, writes=[pk])
                            S.op("act", lambda e, r=r, q=q, pt=pt: e.activation(
                                out=Hv[r][:, q::4, :, 1:1 + tlen],
                                in_=pt[:, 0:DC * ncols].rearrange("p (f s t) -> p f s t", s=nseq, t=tlen), func=AF.Copy),
                                reads=[pk], writes=["H"])
                    n = 64 * nseq

                    def tv(i):
                        return tm[i][:, 0:n].rearrange("p (q s) -> p q s", s=nseq)

                    def bc(a):
                        return a[:, :].unsqueeze(2).broadcast_to([128, 64, nseq])
                    chain = []
                    for t in range(tlen):
                        hr0, hi0 = Hv[0][:, :, :, t], Hv[1][:, :, :, t]
                        hr1, hi1 = Hv[0][:, :, :, t + 1], Hv[1][:, :, :, t + 1]
                        chain += [
                            lambda e, hr0=hr0: e.tensor_tensor(out=tv(0), in0=hr0, in1=bc(are), op=ALU.mult),
                            lambda e, hi0=hi0: e.tensor_tensor(out=tv(1), in0=hi0, in1=bc(aim), op=ALU.mult),
                            lambda e, hi0=hi0: e.tensor_tensor(out=tv(2), in0=hi0, in1=bc(are), op=ALU.mult),
                            lambda e, hr0=hr0: e.tensor_tensor(out=tv(3), in0=hr0, in1=bc(aim), op=ALU.mult),
                            lambda e: e.tensor_tensor(out=tv(0), in0=tv(0), in1=tv(1), op=ALU.subtract),
                            lambda e: e.tensor_tensor(out=tv(2), in0=tv(2), in1=tv(3), op=ALU.add),
                            lambda e, hr1=hr1: e.tensor_tensor(out=hr1, in0=hr1, in1=tv(0), op=ALU.add),
                            lambda e, hi1=hi1: e.tensor_tensor(out=hi1, in0=hi1, in1=tv(2), op=ALU.add),
                        ]
                    S.op("dve", chain, reads=["H", "tmS", ("apair", j)], writes=["H", "tmS"])
                    final_fn(Hv)
                    if ssm_stage < 2:
                        return
                    for r in range(2):
                        S.op("act", lambda e, r=r: e.activation(
                            out=Hb[r][:, :, 0:ncols].rearrange("p q (s t) -> p q s t", t=tlen),
                            in_=Hv[r][:, :, :, 1:1 + tlen], func=AF.Copy), reads=["H"], writes=["Hb"])
                    pk, pt = nextps()
                    fns = []
                    for fc in range(DC):
                        for q in range(4):
                            pr = fc * 4 + q
                            for r in range(2):
                                fns.append(lambda e, q=q, pr=pr, r=r, fc=fc, pt=pt: e.matmul(
                                    pt[32 * q:32 * q + 32, fc * ncols:(fc + 1) * ncols],
                                    Cblk[r][:, pr, :, :].rearrange("p a q -> p (a q)"), Hb[r][:, pr, 0:ncols],
                                    start=(r == 0), stop=(r == 1), tile_position=(0, 32 * q)))
                    S.op("pe", fns, reads=["ssmw", "Hb"], writes=[pk])
                    W_ = DC * ncols

                    def wv(i):
                        return tm[i][:, 0:W_].rearrange("p (f t) -> p f t", t=ncols)
                    y, z = wv(0), wv(1)
                    Dv = sDv[:, j * DC:(j + 1) * DC].unsqueeze(2).broadcast_to([128, DC, ncols])
                    S.op("dve", [
                        lambda e: e.tensor_tensor(out=y, in0=hn[:, :, col0:col0 + ncols], in1=Dv, op=ALU.mult),
                        lambda e, pt=pt: e.tensor_tensor(out=y, in0=y, in1=pt[:, 0:W_].rearrange("p (f t) -> p f t", t=ncols),
                                                         op=ALU.add),
                        lambda e: e.tensor_tensor(out=z, in0=y, in1=y, op=ALU.mult),
                        lambda e: e.tensor_scalar(out=z, in0=z, scalar1=0.044715, scalar2=1.0, op0=ALU.mult, op1=ALU.add),
                        lambda e: e.tensor_tensor(out=z, in0=z, in1=y, op=ALU.mult),
                    ], reads=[pk, "tmS"] + HN_KEYS + VEC, writes=["tmS"])
                    S.op("act", lambda e: e.activation(out=z, in_=z, func=AF.Sigmoid, scale=GC1), reads=["tmS"], writes=["tmS"])
                    S.op("dve", lambda e: e.tensor_tensor(out=cb[:, :, col0:col0 + ncols], in0=z, in1=y, op=ALU.mult),
                         reads=["tmS"], writes=[("cb", c) for c in range(DC)])

                TSP = 16
                NSUBP = NTA // TSP
                HN2 = lambda b: ("Hp", b)
                HB2 = lambda b: ("Hbp", b)

                def Hcol(r, b, c0, c1=None):
                    return H[r][:, :, 20 * b + c0] if c1 is None else H[r][:, :, 20 * b + c0:20 * b + c1]

                def p_bproj_evac(k):
                    b = k % 2
                    col0 = k * TSP
                    for q in range(4):
                        pk, pt = nextps()
                        fns = []
                        for r in range(2):
                            for fc in range(DC):
                                fns.append(lambda e, r=r, fc=fc, q=q, pt=pt: e.matmul(
                                    pt[:, r * 256 + fc * TSP:r * 256 + (fc + 1) * TSP],
                                    Bblk[r][32 * q:32 * q + 32, fc, :, :].rearrange("p a q -> p (a q)"),
                                    hn[32 * q:32 * q + 32, fc, col0:col0 + TSP], start=True, stop=True,
                                    tile_position=(32 * q, 0)))
                        S.op("pe", fns, reads=["ssmw"] + HN_KEYS, writes=[pk])
                        for r in range(2):
                            S.op("act", lambda e, r=r, q=q, pt=pt, b=b: e.activation(
                                out=H[r][:, q::4, 20 * b + 1:20 * b + 1 + TSP],
                                in_=pt[:, r * 256:(r + 1) * 256].rearrange("p (f t) -> p f t", t=TSP), func=AF.Copy),
                                reads=[pk], writes=[HN2(b)])

                def p_rec(k):
                    b = k % 2
                    t4 = [tm[2][:, 0:64], tm[2][:, 64:128], tm[3][:, 0:64], tm[3][:, 64:128]]
                    chain = []
                    for t in range(TSP):
                        if t == 0:
                            hr0, hi0 = carry[j][0][:, :], carry[j][1][:, :]
                        else:
                            hr0, hi0 = Hcol(0, b, t), Hcol(1, b, t)
                        hr1, hi1 = Hcol(0, b, t + 1), Hcol(1, b, t + 1)
                        chain += [
                            lambda e, hr0=hr0: e.tensor_tensor(out=t4[0], in0=hr0, in1=are[:, :], op=ALU.mult),
                            lambda e, hi0=hi0: e.tensor_tensor(out=t4[1], in0=hi0, in1=aim[:, :], op=ALU.mult),
                            lambda e, hi0=hi0: e.tensor_tensor(out=t4[2], in0=hi0, in1=are[:, :], op=ALU.mult),
                            lambda e, hr0=hr0: e.tensor_tensor(out=t4[3], in0=hr0, in1=aim[:, :], op=ALU.mult),
                            lambda e: e.tensor_tensor(out=t4[0], in0=t4[0], in1=t4[1], op=ALU.subtract),
                            lambda e: e.tensor_tensor(out=t4[2], in0=t4[2], in1=t4[3], op=ALU.add),
                            lambda e, hr1=hr1: e.tensor_tensor(out=hr1, in0=hr1, in1=t4[0], op=ALU.add),
                            lambda e, hi1=hi1: e.tensor_tensor(out=hi1, in0=hi1, in1=t4[2], op=ALU.add),
                        ]
                    chain += [lambda e, r=r, b=b: e.tensor_copy(out=carry[j][r][:, :], in_=Hcol(r, b, TSP)) for r in range(2)]
                    S.op("dve", chain, reads=[HN2(b), "tmR", ("carry", j), ("apair", j)], writes=[HN2(b), "tmR", ("carry", j)])

                def p_hb_cproj(k):
                    b = k % 2
                    for r in range(2):
                        S.op("act", lambda e, r=r, b=b: e.activation(out=Hb[r][:, :, TSP * b:TSP * b + TSP],
                                                                   in_=Hcol(r, b, 1, 1 + TSP), func=AF.Copy),
                             reads=[HN2(b)], writes=[HB2(b)])
                    pk, pt = nextps()
                    fns = []
                    for fc in range(DC):
                        for q in range(4):
                            pr = fc * 4 + q
                            for r in range(2):
                                fns.append(lambda e, q=q, pr=pr, r=r, fc=fc, pt=pt, b=b: e.matmul(
                                    pt[32 * q:32 * q + 32, fc * TSP:(fc + 1) * TSP],
                                    Cblk[r][:, pr, :, :].rearrange("p a q -> p (a q)"), Hb[r][:, pr, TSP * b:TSP * b + TSP],
                                    start=(r == 0), stop=(r == 1), tile_position=(0, 32 * q)))
                    S.op("pe", fns, reads=["ssmw", HB2(b)], writes=[pk])
                    return pk, pt

                def p_epi(k, pk, pt):
                    col0 = k * TSP
                    W_ = DC * TSP
                    y = tm[0][:, 0:W_].rearrange("p (f t) -> p f t", t=TSP)
                    z = tm[1][:, 0:W_].rearrange("p (f t) -> p f t", t=TSP)
                    Dv = sDv[:, j * DC:(j + 1) * DC].unsqueeze(2).broadcast_to([128, DC, TSP])
                    S.op("dve", [
                        lambda e: e.tensor_tensor(out=y, in0=hn[:, :, col0:col0 + TSP], in1=Dv, op=ALU.mult),
                        lambda e: e.tensor_tensor(out=y, in0=y, in1=pt[:, 0:W_].rearrange("p (f t) -> p f t", t=TSP), op=ALU.add),
                        lambda e: e.tensor_tensor(out=z, in0=y, in1=y, op=ALU.mult),
                        lambda e: e.tensor_scalar(out=z, in0=z, scalar1=0.044715, scalar2=1.0, op0=ALU.mult, op1=ALU.add),
                        lambda e: e.tensor_tensor(out=z, in0=z, in1=y, op=ALU.mult),
                    ], reads=[pk, "tmE"] + HN_KEYS + VEC, writes=["tmE"])
                    S.op("act", lambda e: e.activation(out=z, in_=z, func=AF.Sigmoid, scale=GC1), reads=["tmE"], writes=["tmE"])
                    S.op("dve", lambda e: e.tensor_tensor(out=cb[:, :, col0:col0 + TSP], in0=z, in1=y, op=ALU.mult),
                         reads=["tmE"], writes=[("cb", c) for c in range(DC)])

                pend = None
                p_bproj_evac(0)
                for k in range(NSUBP):
                    if k + 1 < NSUBP:
                        p_bproj_evac(k + 1)
                    p_rec(k)
                    if pend is not None:
                        p_epi(*pend)
                    pkc, ptc = p_hb_cproj(k)
                    pend = (k, pkc, ptc)
                p_epi(*pend)
                if last and ssm_stage >= 3:
                    with nc.allow_non_contiguous_dma(reason="ssm state out"):
                        S.dma("sp", "st_o", [lambda e, r=r: e.dma_start(
                            out=(rep, imp)[r][j].rearrange("(q a) p -> (a p) q", a=2), in_=carry[j][r][:, :])
                            for r in range(2)], reads=[("carry", j)])
                    S.barrier()
                if last and ssm_stage >= 3:
                    for hb in range(2):
                        def init_s(Hv, hb=hb):
                            for r in range(2):
                                S.op("act", lambda e, r=r: e.activation(out=Hv[r][:, :, :, 0], in_=h0s[r][:, :, 8 * hb:8 * hb + 8],
                                                                        func=AF.Copy), reads=["h0s", "H", "Hb"], writes=["H"])

                        def fin_s(Hv, hb=hb):
                            for r in range(2):
                                S.op("act", lambda e, r=r: e.activation(out=h0s[r][:, :, 8 * hb:8 * hb + 8], in_=Hv[r][:, :, :, 4],
                                                                        func=AF.Copy), reads=["H", "h0s"], writes=[("h0o", hb), "h0s"])
                        run_chunk(NTA + 32 * hb, 32, 8, 4, init_s, fin_s)
                    for r in range(2):
                        for pc in range(4):
                            xi = pc % 2
                            for g4 in range(4):
                                pk, pt = nextps()
                                S.op("pe", [lambda e, k=k, r=r, pc=pc, g4=g4, pt=pt: e.transpose(
                                    pt[0:NSEQ, k * 128:(k + 1) * 128], h0s[r][:, 16 * pc + 4 * g4 + k, :], ident[:, :])
                                    for k in range(4)], reads=["h0s", ("h0o", 0), ("h0o", 1), "ident"], writes=[pk])
                                S.op("act", lambda e, g4=g4, xi=xi, pt=pt: e.activation(
                                    out=xin[xi][0:NSEQ, g4 * 512:(g4 + 1) * 512], in_=pt[0:NSEQ, :], func=AF.Copy),
                                    reads=[pk], writes=[("xin", xi)])
                            S.dma("sp", ("xin", xi), lambda e, r=r, pc=pc, xi=xi: e.dma_start(
                                out=(res_, ims_)[r][j][:, pc * 2048:(pc + 1) * 2048], in_=xin[xi][0:NSEQ, :]),
                                reads=[("xin", xi)])
                if ssm_stage >= 4:
                    W = ssm_w_glu[j]
                    gemm(p, [(W, 0, 0), (W, 0, D)], DC, 8, lambda kc, b0, bn: cb[:, kc, b0:b0 + bn],
                         [("cb", c) for c in range(DC)], glu_residual_epi)
                S.barrier()

        def final_out(p):
            blks = blocks_of(p)
            rmsnorm(p, g_fin, 0, dst_fn=True)
            dsts = [(yp, p * NTA + r * 128, 128, r * 128) for r in range(4)]
            if p == npass - 1:
                dsts.append((ys, 0, NTB, NTA))
            for si, (dst, row0, n, col0) in enumerate(dsts):
                xi = si % 2
                for c4 in range(4):
                    tk, tt = nexttmp()
                    for cc in range(4):
                        c = 4 * c4 + cc
                        S.op("dve", lambda e, c=c, cc=cc, tt=tt, n=n, col0=col0: e.scalar_tensor_tensor(
                            out=tt[:, cc * 128:cc * 128 + n], in0=xT[:, c, col0:col0 + n], scalar=g_fin[:, c:c + 1],
                            in1=rs[:, col0:col0 + n], op0=ALU.mult, op1=ALU.mult),
                            reads=[("xT", c), "rs"] + VEC, writes=[tk])
                    pk, pt = nextps()
                    S.op("pe", [lambda e, cc=cc, n=n, pt=pt, tt=tt: e.transpose(
                        pt[0:n, cc * 128:(cc + 1) * 128], tt[:, cc * 128:cc * 128 + n], ident[:, :])
                        for cc in range(4)], reads=[tk, "ident"], writes=[pk])
                    S.op("act", lambda e, c4=c4, n=n, xi=xi, pt=pt: e.activation(
                        out=xin[xi][0:n, c4 * 512:(c4 + 1) * 512], in_=pt[0:n, :], func=AF.Copy),
                        reads=[pk], writes=[("xin", xi)])
                S.dma("sp", ("xin", xi), lambda e, dst=dst, row0=row0, n=n, xi=xi: e.dma_start(
                    out=dst[row0:row0 + n, :], in_=xin[xi][0:n, :]), reads=[("xin", xi)])

        for li in layers:
            if li % 2 == 1:
                ssm_prep(li // 2)
        for p in range(npass):
            load_x(p)
            for li in layers:
                if li % 2 == 0:
                    conv_layer(p, li, li // 2)
                elif ssm_stage >= 1:
                    ssm_layer(p, li, li // 2)
                ffn_layer(p, li)
                assert not pre_loaded, "prefetched weight tiles must be consumed by the very next gemm"
            final_out(p)
        S.final_wait("sp")
    return nc


_NC_CACHE = {}


def kernel(**inputs):
    layers = tuple(int(c) for c in os.environ.get("K_LAYERS", "0123"))
    stage = int(os.environ.get("K_SSM_STAGE", "4"))
    key = (layers, stage)
    if key not in _NC_CACHE:
        _NC_CACHE[key] = build_nc(layers=layers, ssm_stage=stage)
    nc = _NC_CACHE[key]
    f = lambda a: np.ascontiguousarray(np.asarray(a, dtype=np.float32))
    inp = {k: f(v) for k, v in inputs.items()}
    in_maps = []
    for c in range(NCORES):
        b = c % 4
        sl = slice(NSEQ * c, NSEQ * c + NSEQ)
        m = {
            "xp": inp["x_prompt"][b],
            "xs": f(inp["x_sample"][sl].reshape(NTB, D)),
            "sconv": f(inp["state_conv"][:, sl]),
            "sre": f(inp["state_ssm_re"][:, sl].reshape(2, NSEQ, 8192)),
            "sim": f(inp["state_ssm_im"][:, sl].reshape(2, NSEQ, 8192)),
            "sffn": f(inp["state_ffn"][:, sl].reshape(4, NSEQ * 2, FF)),
            "norm_final": inp["norm_final"].reshape(1, D),
            "ssm_B_re": inp["ssm_B_re"].reshape(2, 128, 1024),
            "ssm_B_im": inp["ssm_B_im"].reshape(2, 128, 1024),
            "ssm_C_re": inp["ssm_C_re"].reshape(2, 128, 1024),
            "ssm_C_im": inp["ssm_C_im"].reshape(2, 128, 1024),
        }
        for k in ("norm_mix", "norm_ffn", "conv_w_in", "conv_dw", "conv_dw_b", "conv_ln_g", "conv_ln_b",
                  "conv_w_out", "ssm_A_re", "ssm_A_im", "ssm_log_dt", "ssm_D", "ssm_w_glu", "ffn_w_gate",
                  "ffn_w_up", "ffn_conv", "ffn_w_down"):
            m[k] = inp[k]
        in_maps.append(m)
    res = run_bass_kernel_spmd(nc, in_maps, core_ids=list(range(NCORES)))
    R = res.results
    y_prompt = np.stack([R[b]["yp"] for b in range(4)])
    y_sample = np.concatenate([R[c]["ys"].reshape(NSEQ, 4, D) for c in range(NCORES)], axis=0)
    conv_p = np.stack([R[b]["convp"] for b in range(4)], axis=1)
    conv_s = np.concatenate([R[c]["convs"] for c in range(NCORES)], axis=1)
    re_p = np.stack([R[b]["rep"] for b in range(4)], axis=1)
    im_p = np.stack([R[b]["imp"] for b in range(4)], axis=1)
    re_s = np.concatenate([R[c]["res"].reshape(2, NSEQ, 128, 64) for c in range(NCORES)], axis=1)
    im_s = np.concatenate([R[c]["ims"].reshape(2, NSEQ, 128, 64) for c in range(NCORES)], axis=1)
    ffn_p = np.stack([R[b]["ffnp"] for b in range(4)], axis=1)
    ffn_s = np.concatenate([R[c]["ffns"].reshape(4, NSEQ, 2, FF) for c in range(NCORES)], axis=1)
    return (y_prompt, y_sample, conv_p, conv_s, re_p, im_p, re_s, im_s, ffn_p, ffn_s)
```

```python
import os
import numpy as np
from contextlib import ExitStack
import concourse.bass as bass
import concourse.mybir as mybir
from concourse.bass_utils import run_bass_kernel_spmd

F32 = mybir.dt.float32
BF16 = mybir.dt.bfloat16
AF = mybir.ActivationFunctionType
ALU = mybir.AluOpType
AX = mybir.AxisListType

D = 2048
DC = 16
FF = 5632
FCH = 44
NPASS = 4
NTA = 512
NSEQ = 16
NTB = 64
NT = NTA + NTB
DEPTH = 4
EPS = 1e-6
NCORES = 8


class Sched:
    def __init__(self, nc, stack, n_dma_sems=24):
        self.nc = nc
        self.eng = {"pe": nc.tensor, "act": nc.scalar, "dve": nc.vector,
                    "pool": nc.gpsimd, "sp": nc.sync}
        self.sem = {}
        self.cnt = {}
        for e in self.eng:
            self.sem[e] = stack.enter_context(nc.semaphore("s_" + e))
            self.cnt[e] = 0
        self.dma_free = []
        for i in range(n_dma_sems):
            k = "dma%d" % i
            self.sem[k] = stack.enter_context(nc.semaphore("s_" + k))
            self.cnt[k] = 0
            self.dma_free.append(k)
        self.slot_sem = {}
        self.waited = {e: {} for e in self.eng}
        self.last_w = {}
        self.reads = {}

    def _deps(self, reads, writes):
        deps = {}

        def add(d):
            if d is None:
                return
            k, n = d
            if deps.get(k, 0) < n:
                deps[k] = n
        for b in reads:
            add(self.last_w.get(b))
        for b in writes:
            add(self.last_w.get(b))
            for d in self.reads.get(b, ()):
                add(d)
        return deps

    def _emit_waits(self, e, deps):
        for k, n in deps.items():
            if k == e and e == "pe":
                continue
            if self.waited[e].get(k, 0) >= n:
                continue
            self.eng[e].wait_ge(self.sem[k], n)
            self.waited[e][k] = n

    def _record(self, key, n, reads, writes):
        for b in writes:
            self.last_w[b] = (key, n)
            self.reads[b] = []
        for b in reads:
            lst = self.reads.setdefault(b, [])
            lst.append((key, n))
            if len(lst) > 8:
                m = {}
                for k2, n2 in lst:
                    m[k2] = max(m.get(k2, 0), n2)
                self.reads[b] = list(m.items())

    def op(self, e, fns, reads=(), writes=()):
        if callable(fns):
            fns = [fns]
        deps = self._deps(reads, writes)
        self._emit_waits(e, deps)
        eng = self.eng[e]
        ins = None
        for f in fns:
            ins = f(eng)
        self.cnt[e] += 1
        ins.then_inc(self.sem[e], 1)
        self._record(e, self.cnt[e], reads, writes)

    def dma(self, e, slot, fns, reads=(), writes=()):
        if callable(fns):
            fns = [fns]
        if slot not in self.slot_sem:
            self.slot_sem[slot] = self.dma_free.pop(0)
        k = self.slot_sem[slot]
        deps = self._deps(reads, writes)
        self._emit_waits(e, deps)
        eng = self.eng[e]
        for f in fns:
            ins = f(eng)
            self.cnt[k] += 16
            ins.then_inc(self.sem[k], 16)
        self._record(k, self.cnt[k], reads, writes)

    def barrier(self):
        for e in self.eng:
            deps = {k: self.cnt[k] for k in self.sem if self.cnt[k] > 0}
            self._emit_waits(e, deps)

    def final_wait(self, e="sp"):
        deps = {k: self.cnt[k] for k in self.sem if self.cnt[k] > 0}
        self._emit_waits(e, deps)


def build_nc(layers=(0, 1, 2, 3), npass=NPASS, ssm_stage=4):
    nc = bass.Bass("TRN2", target_bir_lowering=False)

    def din(name, shape):
        return nc.dram_tensor(name, list(shape), F32, kind="ExternalInput").ap()

    def dout(name, shape):
        return nc.dram_tensor(name, list(shape), F32, kind="ExternalOutput").ap()

    xp = din("xp", [2048, D])
    xs = din("xs", [NTB, D])
    sconv = din("sconv", [2, NSEQ, 30, D])
    sre = din("sre", [2, NSEQ, 8192])
    sim = din("sim", [2, NSEQ, 8192])
    sffn = din("sffn", [4, NSEQ * 2, FF])
    norm_mix = din("norm_mix", [4, D])
    norm_ffn = din("norm_ffn", [4, D])
    norm_final = din("norm_final", [1, D])
    conv_w_in = din("conv_w_in", [2, D, 2 * D])
    conv_dw = din("conv_dw", [2, 31, D])
    conv_dw_b = din("conv_dw_b", [2, D])
    conv_ln_g = din("conv_ln_g", [2, D])
    conv_ln_b = din("conv_ln_b", [2, D])
    conv_w_out = din("conv_w_out", [2, D, D])
    ssm_A_re = din("ssm_A_re", [2, 128, 64])
    ssm_A_im = din("ssm_A_im", [2, 128, 64])
    ssm_log_dt = din("ssm_log_dt", [2, 128])
    ssm_B_re = din("ssm_B_re", [2, 128, 1024])
    ssm_B_im = din("ssm_B_im", [2, 128, 1024])
    ssm_C_re = din("ssm_C_re", [2, 128, 1024])
    ssm_C_im = din("ssm_C_im", [2, 128, 1024])
    ssm_D = din("ssm_D", [2, D])
    ssm_w_glu = din("ssm_w_glu", [2, D, 2 * D])
    ffn_w_gate = din("ffn_w_gate", [4, D, FF])
    ffn_w_up = din("ffn_w_up", [4, D, FF])
    ffn_conv = din("ffn_conv", [4, 3, FF])
    ffn_w_down = din("ffn_w_down", [4, FF, D])

    yp = dout("yp", [2048, D])
    ys = dout("ys", [NTB, D])
    convp = dout("convp", [2, 30, D])
    convs = dout("convs", [2, NSEQ, 30, D])
    rep = dout("rep", [2, 128, 64])
    imp = dout("imp", [2, 128, 64])
    res_ = dout("res", [2, NSEQ, 8192])
    ims_ = dout("ims", [2, NSEQ, 8192])
    ffnp = dout("ffnp", [4, 2, FF])
    ffns = dout("ffns", [4, NSEQ * 2, FF])

    scrA = nc.dram_tensor("scrA", [2, 2, 128, 64], F32, kind="Internal").ap()
    scrB = nc.dram_tensor("scrB", [2, 2, 128, 1024], F32, kind="Internal").ap()
    scrBb = nc.dram_tensor("scrBb", [2, 2, 128, DC * 128], BF16, kind="Internal").ap()
    scrCb = nc.dram_tensor("scrCb", [2, 2, 128, 64 * 32], BF16, kind="Internal").ap()

    with ExitStack() as st:
        S = Sched(nc, st, n_dma_sems=40)

        def T(name, shape, dt=F32):
            return st.enter_context(nc.sbuf_tensor(name, list(shape), dt))

        xT = T("xT", [128, DC, NT])
        hn = T("hn", [128, DC, NT], BF16)
        cb = T("cb", [128, DC, NT], BF16)
        rs = T("rs", [128, NT])
        rs2 = T("rs2", [128, NT])
        tmpA = [T("tmpA%d" % i, [128, NT]) for i in range(3)]
        xin = [T("xin%d" % i, [128, D]) for i in range(2)]
        NWS = 4
        wsl = [T("wsl%d" % i, [128, 16, 256], BF16) for i in range(NWS)]
        ident = T("ident", [128, 128])
        identb = T("identb", [128, 128], BF16)
        ones = T("ones", [128, 128])
        onesb = T("onesb", [128, 128], BF16)
        epst = T("epst", [128, 1])
        g_mix = T("g_mix", [128, 64])
        g_ffn = T("g_ffn", [128, 64])
        g_fin = T("g_fin", [128, 16])
        cdwb = T("cdwb", [128, 32])
        clng = T("clng", [128, 32])
        clnb = T("clnb", [128, 32])
        sDv = T("sDv", [128, 32])
        fcw = T("fcw", [128, 4 * 3 * FCH])
        dwT = T("dwT", [128, 2 * 31 * DC])
        halo_c = [T("halo_c%d" % j, [128, DC, 30], BF16) for j in range(2)]
        halo_f = [T("halo_f%d" % i, [128, FCH, 2]) for i in range(4)]
        stg = T("stg", [128, 128])
        apair = [[T("apair%d%d" % (j, r), [128, 64]) for r in range(2)] for j in range(2)]
        carry = [[T("carry%d%d" % (j, r), [128, 64]) for r in range(2)] for j in range(2)]
        ps = [st.enter_context(nc.psum_tensor("ps%d" % i, [128, 512], F32)) for i in range(8)]
        psn = [0]
        uid = [0]

        def nextps():
            i = psn[0] % 8
            psn[0] += 1
            return ("ps", i), ps[i]

        wsn = [0]

        def nextws():
            i = wsn[0] % NWS
            wsn[0] += 1
            return ("ws", i), wsl[i]

        tan = [0]

        def nexttmp():
            i = tan[0] % 3
            tan[0] += 1
            return ("tmpA", i), tmpA[i]

        S.op("pool", lambda e: e.memset(ones[:], 1.0), writes=["ones"])
        S.op("pool", lambda e: e.memset(ident[:], 1.0), writes=["ident"])
        S.op("pool", lambda e: e.affine_select(out=ident[:], in_=ident[:], pattern=[[1, 128]],
                                               compare_op=ALU.is_equal, fill=0.0, base=0,
                                               channel_multiplier=-1),
             reads=["ident"], writes=["ident"])
        S.op("dve", lambda e: e.tensor_copy(out=identb[:], in_=ident[:]), reads=["ident"], writes=["identb"])
        S.op("dve", lambda e: e.memset(epst[:], EPS), writes=["epst"])
        S.op("dve", lambda e: e.memset(onesb[:], 1.0), writes=["onesb"])
        for j in range(2):
            S.op("dve", lambda e, j=j: e.memset(halo_c[j][:], 0.0), writes=[("halo_c", j)])
        for i in range(4):
            S.op("dve", lambda e, i=i: e.memset(halo_f[i][:], 0.0), writes=[("halo_f", i)])

        def load_vecT(src_rows_ap, nrows, dst, dst_col0):
            r0 = 0
            while r0 < nrows:
                n = min(128, nrows - r0)
                S.dma("sp", "stg", lambda e, r0=r0, n=n: e.dma_start(out=stg[0:n, :], in_=src_rows_ap[r0:r0 + n, :]),
                      writes=["stg"])
                pk, pt = nextps()
                S.op("pe", lambda e, n=n, pt=pt: e.transpose(pt[:, 0:n], stg[0:n, :], ident[0:n, 0:n]),
                     reads=["stg", "ident"], writes=[pk])
                S.op("act", lambda e, n=n, pt=pt, r0=r0: e.activation(out=dst[:, dst_col0 + r0:dst_col0 + r0 + n],
                                                                   in_=pt[:, 0:n], func=AF.Copy),
                     reads=[pk], writes=[("vec", id(dst))])
                r0 += n

        load_vecT(norm_mix.rearrange("l (c p) -> (l c) p", p=128), 64, g_mix, 0)
        load_vecT(norm_ffn.rearrange("l (c p) -> (l c) p", p=128), 64, g_ffn, 0)
        load_vecT(norm_final.rearrange("l (c p) -> (l c) p", p=128), 16, g_fin, 0)
        load_vecT(conv_dw_b.rearrange("l (c p) -> (l c) p", p=128), 32, cdwb, 0)
        load_vecT(conv_ln_g.rearrange("l (c p) -> (l c) p", p=128), 32, clng, 0)
        load_vecT(conv_ln_b.rearrange("l (c p) -> (l c) p", p=128), 32, clnb, 0)
        load_vecT(ssm_D.rearrange("l (c p) -> (l c) p", p=128), 32, sDv, 0)
        load_vecT(ffn_conv.rearrange("l k (c p) -> (l k c) p", p=128), 4 * 3 * FCH, fcw, 0)
        load_vecT(conv_dw.rearrange("l k (c p) -> (l k c) p", p=128), 2 * 31 * DC, dwT, 0)
        VEC = [("vec", id(t)) for t in (g_mix, g_ffn, g_fin, cdwb, clng, clnb, sDv, fcw, dwT)]

        def blocks_of(p):
            return [(0, NTA)] + ([(NTA, NTB)] if p == npass - 1 else [])

        def load_x(p):
            srcs = [(xp, p * NTA + r * 128, 128, r * 128) for r in range(4)]
            if p == npass - 1:
                srcs.append((xs, 0, NTB, NTA))
            for si, (src, row0, n, col0) in enumerate(srcs):
                xi = si % 2
                S.dma("sp", ("xin", xi), lambda e, src=src, row0=row0, n=n, xi=xi:
                      e.dma_start(out=xin[xi][0:n, :], in_=src[row0:row0 + n, :]), writes=[("xin", xi)])
                for c4 in range(4):
                    pk, pt = nextps()
                    S.op("pe", [lambda e, c=c, n=n, xi=xi, pt=pt, c4=c4: e.transpose(
                        pt[:, (c - 4 * c4) * 128:(c - 4 * c4) * 128 + n], xin[xi][0:n, c * 128:(c + 1) * 128],
                        ident[0:n, 0:n]) for c in range(4 * c4, 4 * c4 + 4)],
                        reads=[("xin", xi), "ident"], writes=[pk])
                    S.op("act", lambda e, c4=c4, n=n, col0=col0, pt=pt: e.activation(
                        out=xT[:, 4 * c4:4 * c4 + 4, col0:col0 + n],
                        in_=pt[:, :].rearrange("p (c t) -> p c t", t=128)[:, :, 0:n], func=AF.Copy),
                        reads=[pk], writes=[("xT", c) for c in range(4 * c4, 4 * c4 + 4)])

        def rmsnorm(p, gain, gcol0, dst_fn=None, dst_key="hn"):
            blks = blocks_of(p)
            ncol = blks[-1][0] + blks[-1][1]
            pks = [nextps() for _ in blks]
            for c in range(DC):
                tk, tt = nexttmp()
                S.op("act", lambda e, c=c, tt=tt: e.activation(out=tt[:, 0:ncol], in_=xT[:, c, 0:ncol], func=AF.Square),
                     reads=[("xT", c)], writes=[tk])
                for (pk, pt), (b0, bn) in zip(pks, blks):
                    S.op("pe", lambda e, c=c, pt=pt, b0=b0, bn=bn, tt=tt: e.matmul(
                        pt[:, 0:bn], ones[:], tt[:, b0:b0 + bn], start=(c == 0), stop=(c == DC - 1)),
                        reads=[tk, "ones"], writes=[pk])
            for (pk, pt), (b0, bn) in zip(pks, blks):
                S.op("act", lambda e, pt=pt, b0=b0, bn=bn: e.activation(
                    out=rs[:, b0:b0 + bn], in_=pt[:, 0:bn], func=AF.Sqrt, scale=1.0 / D, bias=epst[:, 0:1]),
                    reads=[pk, "epst"], writes=["rs"])
            S.op("dve", lambda e: e.reciprocal(out=rs[:, 0:ncol], in_=rs[:, 0:ncol]), reads=["rs"], writes=["rs"])
            if dst_fn is None:
                for c in range(DC):
                    S.op("dve", lambda e, c=c: e.scalar_tensor_tensor(
                        out=hn[:, c, 0:ncol], in0=xT[:, c, 0:ncol], scalar=gain[:, gcol0 + c:gcol0 + c + 1],
                        in1=rs[:, 0:ncol], op0=ALU.mult, op1=ALU.mult),
                        reads=[("xT", c), "rs"] + VEC, writes=[(dst_key, c)])

        pre_loaded = {}

        def _load_group(srcs, kc_n, g):
            lst = []
            for (W, row0, col0) in srcs:
                wk, wt = nextws()
                S.dma("pool", wk, lambda e, W=W, row0=row0, col0=col0, wt=wt, g=g: e.dma_start(
                    out=wt[:, 0:kc_n, :],
                    in_=W[row0:row0 + kc_n * 128, col0 + 256 * g:col0 + 256 * g + 256].rearrange(
                        "(k p) m -> p k m", p=128)), writes=[wk])
                lst.append((wk, wt))
            return lst

        def gemm(p, srcs, kc_n, ngroups, rhs_fn, rhs_keys, epi, tag=None, nxt=None):
            blks = blocks_of(p)
            loaded = {}
            if tag is not None and tag in pre_loaded:
                loaded[0] = pre_loaded.pop(tag)
            else:
                loaded[0] = _load_group(srcs, kc_n, 0)
            for g in range(ngroups):
                if g + 1 < ngroups:
                    loaded[g + 1] = _load_group(srcs, kc_n, g + 1)
                elif nxt is not None and len(srcs) + len(nxt[1]) <= NWS:
                    pre_loaded[nxt[0]] = _load_group(nxt[1], nxt[2], 0)
                for ci in range(2):
                    for bi, (b0, bn) in enumerate(blks):
                        pls = []
                        for (wk, wt) in loaded[g]:
                            pk, pt = nextps()
                            S.op("pe", [lambda e, kc=kc, wt=wt, pt=pt, b0=b0, bn=bn, ci=ci: e.matmul(
                                pt[:, 0:bn], wt[:, kc, ci * 128:(ci + 1) * 128], rhs_fn(kc, b0, bn),
                                start=(kc == 0), stop=(kc == kc_n - 1)) for kc in range(kc_n)],
                                reads=[wk] + rhs_keys, writes=[pk])
                            pls.append((pk, pt))
                        epi(g, ci, bi, b0, bn, pls)
                del loaded[g]

        def glu_residual_epi(g, ci, bi, b0, bn, pls):
            mc = 2 * g + ci
            (pkv, ptv), (pkg, ptg) = pls
            tk, tt = nexttmp()
            S.op("act", lambda e: e.activation(out=tt[:, 0:bn], in_=ptg[:, 0:bn], func=AF.Sigmoid),
                 reads=[pkg], writes=[tk])
            S.op("dve", lambda e: e.tensor_tensor(out=tt[:, 0:bn], in0=ptv[:, 0:bn], in1=tt[:, 0:bn], op=ALU.mult),
                 reads=[pkv, tk], writes=[tk])
            S.op("dve", lambda e: e.tensor_tensor(out=xT[:, mc, b0:b0 + bn], in0=xT[:, mc, b0:b0 + bn],
                                                  in1=tt[:, 0:bn], op=ALU.add),
                 reads=[tk, ("xT", mc)], writes=[("xT", mc)])

        def residual_epi(g, ci, bi, b0, bn, pls):
            mc = 2 * g + ci
            (pk, pt), = pls
            S.op("dve", lambda e: e.tensor_tensor(out=xT[:, mc, b0:b0 + bn], in0=xT[:, mc, b0:b0 + bn],
                                                  in1=pt[:, 0:bn], op=ALU.add),
                 reads=[pk, ("xT", mc)], writes=[("xT", mc)])

        HN_KEYS = [("hn", c) for c in range(DC)]

        def conv_layer(p, li, j):
            blks = blocks_of(p)
            last = (p == npass - 1)
            ncol = blks[-1][0] + blks[-1][1]
            with ExitStack() as ls:
                def LT(name, shape, dt=F32):
                    uid[0] += 1
                    return ls.enter_context(nc.sbuf_tensor("%s_%d" % (name, uid[0]), list(shape), dt))
                uxp = LT("uxp", [128, DC, 30 + NTA], BF16)
                uxs = LT("uxs", [128, DC, NSEQ, 34], BF16)
                dg = [LT("dg%d" % i, [128, 31, 128], BF16) for i in range(1)]
                ulast = LT("ulast", [128, DC, 94])
                rmsnorm(p, g_mix, li * DC)
                for c in range(DC):
                    S.op("act", lambda e, c=c: e.activation(out=uxp[:, c, 0:30], in_=halo_c[j][:, c, :], func=AF.Copy),
                         reads=[("halo_c", j)], writes=[("uxp", c)])
                if last:
                    for r4 in range(4):
                        xi = r4 % 2
                        S.dma("sp", ("xin", xi), lambda e, r4=r4, xi=xi: e.dma_start(
                            out=xin[xi][0:120, :],
                            in_=sconv[j, 4 * r4:4 * r4 + 4].rearrange("s r d -> (s r) d")), writes=[("xin", xi)])
                        for c4 in range(4):
                            pk, pt = nextps()
                            S.op("pe", [lambda e, c=c, xi=xi, pt=pt, c4=c4: e.transpose(
                                pt[:, (c - 4 * c4) * 128:(c - 4 * c4) * 128 + 120], xin[xi][0:120, c * 128:(c + 1) * 128],
                                ident[0:120, 0:120]) for c in range(4 * c4, 4 * c4 + 4)],
                                reads=[("xin", xi), "ident"], writes=[pk])
                            for cc in range(4):
                                c = 4 * c4 + cc
                                S.op("act", lambda e, c=c, cc=cc, r4=r4, pt=pt: e.activation(
                                    out=uxs[:, c, 4 * r4:4 * r4 + 4, 0:30],
                                    in_=pt[:, cc * 128:cc * 128 + 120].rearrange("p (s r) -> p s r", r=30),
                                    func=AF.Copy), reads=[pk], writes=[("uxs", c)])
                    S.dma("sp", "cs_copy", lambda e: e.dma_start(out=convs[j, :, 0:26, :], in_=sconv[j, :, 4:30, :]))

                def glu_u_epi(g, ci, bi, b0, bn, pls):
                    mc = 2 * g + ci
                    (pkv, ptv), (pkg, ptg) = pls
                    tk, tt = nexttmp()
                    S.op("act", lambda e: e.activation(out=tt[:, 0:bn], in_=ptg[:, 0:bn], func=AF.Sigmoid),
                         reads=[pkg], writes=[tk])
                    if bi == 0:
                        S.op("dve", lambda e: e.tensor_tensor(out=uxp[:, mc, 30:30 + NTA], in0=ptv[:, 0:NTA],
                                                              in1=tt[:, 0:NTA], op=ALU.mult),
                             reads=[pkv, tk], writes=[("uxp", mc)])
                        S.op("act", lambda e: e.activation(out=halo_c[j][:, mc, :], in_=uxp[:, mc, NTA:NTA + 30],
                                                           func=AF.Copy),
                             reads=[("uxp", mc)], writes=[("halo_c", j)])
                        if last:
                            S.op("dve", lambda e: e.tensor_tensor(out=ulast[:, mc, 0:30], in0=ptv[:, NTA - 30:NTA],
                                                                  in1=tt[:, NTA - 30:NTA], op=ALU.mult),
                                 reads=[pkv, tk], writes=[("ulast", mc)])
                    else:
                        S.op("dve", lambda e: e.tensor_tensor(
                            out=uxs[:, mc, :, 30:34], in0=ptv[:, 0:NTB].rearrange("p (s t) -> p s t", t=4),
                            in1=tt[:, 0:NTB].rearrange("p (s t) -> p s t", t=4), op=ALU.mult),
                            reads=[pkv, tk], writes=[("uxs", mc)])
                        S.op("dve", lambda e: e.tensor_tensor(out=ulast[:, mc, 30:94], in0=ptv[:, 0:NTB],
                                                              in1=tt[:, 0:NTB], op=ALU.mult),
                             reads=[pkv, tk], writes=[("ulast", mc)])

                W = conv_w_in[j]
                gemm(p, [(W, 0, 0), (W, 0, D)], DC, 8, lambda kc, b0, bn: hn[:, kc, b0:b0 + bn], HN_KEYS, glu_u_epi)

                if last:
                    for c4 in range(4):
                        pk, pt = nextps()
                        S.op("pe", [lambda e, c=c, pt=pt, c4=c4: e.transpose(
                            pt[0:94, (c - 4 * c4) * 128:(c - 4 * c4 + 1) * 128], ulast[:, c, :], ident[:, :])
                            for c in range(4 * c4, 4 * c4 + 4)],
                            reads=[("ulast", c) for c in range(4 * c4, 4 * c4 + 4)] + ["ident"], writes=[pk])
                        S.op("act", lambda e, c4=c4, pt=pt: e.activation(out=xin[0][0:94, c4 * 512:(c4 + 1) * 512],
                                                                      in_=pt[0:94, :], func=AF.Copy),
                             reads=[pk], writes=[("xin", 0)])
                    S.dma("sp", ("xin", 0), [lambda e: e.dma_start(out=convp[j], in_=xin[0][0:30, :])] +
                          [lambda e, s=s: e.dma_start(out=convs[j, s, 26:30, :], in_=xin[0][30 + 4 * s:34 + 4 * s, :])
                           for s in range(NSEQ)], reads=[("xin", 0)])

                for c in range(DC):
                    di = 0
                    S.op("dve", lambda e, c=c, di=di: e.tensor_tensor(
                        out=dg[di][:, :, :], in0=identb[:, :].unsqueeze(1).broadcast_to([128, 31, 128]),
                        in1=dwT[:, (j * 31) * DC + c:(j * 31 + 31) * DC:DC].unsqueeze(2).broadcast_to([128, 31, 128]),
                        op=ALU.mult), reads=["identb"] + VEC, writes=[("dg", di)])
                    for bi, (b0, bn) in enumerate(blks):
                        pk, pt = nextps()
                        if bi == 0:
                            S.op("pe", [lambda e, k=k, c=c, di=di, pt=pt: e.matmul(
                                pt[:, 0:NTA], dg[di][:, k, :], uxp[:, c, k:k + NTA], start=(k == 0), stop=(k == 30))
                                for k in range(31)], reads=[("dg", di), ("uxp", c)], writes=[pk])
                        else:
                            S.op("pe", [lambda e, k=k, c=c, di=di, pt=pt: e.matmul(
                                pt[:, 0:NTB].rearrange("p (s t) -> p s t", t=4), dg[di][:, k, :], uxs[:, c, :, k:k + 4],
                                start=(k == 0), stop=(k == 30)) for k in range(31)],
                                reads=[("dg", di), ("uxs", c)], writes=[pk])
                        S.op("act", lambda e, c=c, pt=pt, b0=b0, bn=bn: e.activation(
                            out=cb[:, c, b0:b0 + bn], in_=pt[:, 0:bn], func=AF.Identity,
                            bias=cdwb[:, j * DC + c:j * DC + c + 1]), reads=[pk] + VEC, writes=[("cb", c)])
                pk1 = [nextps() for _ in blks]
                pk2 = [nextps() for _ in blks]
                for c in range(DC):
                    tk, tt = nexttmp()
                    S.op("act", lambda e, c=c, tt=tt: e.activation(out=tt[:, 0:ncol], in_=cb[:, c, 0:ncol], func=AF.Square),
                         reads=[("cb", c)], writes=[tk])
                    for (pka, pta), (pkb, ptb), (b0, bn) in zip(pk1, pk2, blks):
                        S.op("pe", lambda e, c=c, pta=pta, b0=b0, bn=bn: e.matmul(
                            pta[:, 0:bn], onesb[:], cb[:, c, b0:b0 + bn], start=(c == 0), stop=(c == DC - 1)),
                            reads=[("cb", c), "onesb"], writes=[pka])
                        S.op("pe", lambda e, c=c, ptb=ptb, b0=b0, bn=bn, tt=tt: e.matmul(
                            ptb[:, 0:bn], ones[:], tt[:, b0:b0 + bn], start=(c == 0), stop=(c == DC - 1)),
                            reads=[tk, "ones"], writes=[pkb])
                for (pka, pta), (pkb, ptb), (b0, bn) in zip(pk1, pk2, blks):
                    S.op("act", lambda e, pta=pta, b0=b0, bn=bn: e.activation(
                        out=rs2[:, b0:b0 + bn], in_=pta[:, 0:bn], func=AF.Copy, scale=1.0 / D), reads=[pka], writes=["rs2"])
                    tk, tt = nexttmp()
                    S.op("dve", lambda e, tt=tt, b0=b0, bn=bn: e.tensor_tensor(
                        out=tt[:, 0:bn], in0=rs2[:, b0:b0 + bn], in1=rs2[:, b0:b0 + bn], op=ALU.mult),
                        reads=["rs2"], writes=[tk])
                    S.op("dve", lambda e, tt=tt, ptb=ptb, b0=b0, bn=bn: e.scalar_tensor_tensor(
                        out=tt[:, 0:bn], in0=ptb[:, 0:bn], scalar=1.0 / D, in1=tt[:, 0:bn], op0=ALU.mult,
                        op1=ALU.subtract), reads=[pkb, tk], writes=[tk])
                    S.op("act", lambda e, tt=tt, b0=b0, bn=bn: e.activation(
                        out=rs[:, b0:b0 + bn], in_=tt[:, 0:bn], func=AF.Sqrt, bias=epst[:, 0:1]),
                        reads=[tk, "epst"], writes=["rs"])
                S.op("dve", lambda e: e.reciprocal(out=rs[:, 0:ncol], in_=rs[:, 0:ncol]), reads=["rs"], writes=["rs"])
                for c in range(DC):
                    tk, tt = nexttmp()
                    S.op("dve", lambda e, c=c, tt=tt: e.tensor_tensor(out=tt[:, 0:ncol], in0=cb[:, c, 0:ncol],
                                                               in1=rs2[:, 0:ncol], op=ALU.subtract),
                         reads=[("cb", c), "rs2"], writes=[tk])
                    S.op("dve", lambda e, c=c, tt=tt: e.tensor_tensor(out=tt[:, 0:ncol], in0=tt[:, 0:ncol],
                                                               in1=rs[:, 0:ncol], op=ALU.mult),
                         reads=[tk, "rs"], writes=[tk])
                    S.op("act", lambda e, c=c, tt=tt: e.activation(
                        out=hn[:, c, 0:ncol], in_=tt[:, 0:ncol], func=AF.Silu,
                        scale=clng[:, j * DC + c:j * DC + c + 1], bias=clnb[:, j * DC + c:j * DC + c + 1]),
                        reads=[tk] + VEC, writes=[("hn", c)])
                gemm(p, [(conv_w_out[j], 0, 0)], DC, 8, lambda kc, b0, bn: hn[:, kc, b0:b0 + bn], HN_KEYS, residual_epi)
                S.barrier()

        def ffn_layer(p, li):
            blks = blocks_of(p)
            last = (p == npass - 1)
            ncol = blks[-1][0] + blks[-1][1]
            QS = [(0, 12), (12, 12), (24, 12), (36, 8)]
            with ExitStack() as ls:
                def LT(name, shape, dt=F32):
                    uid[0] += 1
                    return ls.enter_context(nc.sbuf_tensor("%s_%d" % (name, uid[0]), list(shape), dt))
                aT = LT("aT", [128, 12, NT], BF16)
                gxp = [LT("gxp%d" % i, [128, 2 + NTA]) for i in range(3)]
                gxs = [LT("gxs%d" % i, [128, NSEQ, 6]) for i in range(3)]
                gcs = LT("gcs", [128, FCH, NSEQ, 2]) if last else None
                glast = LT("glast", [128, FCH, 34]) if last else None
                rmsnorm(p, g_ffn, li * DC)
                if last:
                    for h in range(3):
                        c0 = h * 2048
                        cn_ = min(2048, FF - c0)
                        xi = h % 2
                        S.dma("sp", ("xin", xi), lambda e, c0=c0, cn_=cn_, xi=xi: e.dma_start(
                            out=xin[xi][0:32, 0:cn_], in_=sffn[li, :, c0:c0 + cn_]), writes=[("xin", xi)])
                        for c4 in range(cn_ // 512):
                            pk, pt = nextps()
                            S.op("pe", [lambda e, cc=cc, xi=xi, pt=pt, c4=c4: e.transpose(
                                pt[:, cc * 128:cc * 128 + 32], xin[xi][0:32, (4 * c4 + cc) * 128:(4 * c4 + cc + 1) * 128],
                                ident[0:32, 0:32]) for cc in range(4)],
                                reads=[("xin", xi), "ident"], writes=[pk])
                            ch0 = h * 16 + 4 * c4
                            S.op("act", lambda e, ch0=ch0, pt=pt: e.activation(
                                out=gcs[:, ch0:ch0 + 4, :, :].rearrange("p c s r -> p c (s r)"),
                                in_=pt[:, :].rearrange("p (c t) -> p c t", t=128)[:, :, 0:32], func=AF.Copy),
                                reads=[pk], writes=["gcs"])
                gi = [0]

                def ffn_epi_factory(q0):
                    def epi(g, ci, bi, b0, bn, pls):
                        fl = 2 * g + ci
                        f = q0 + fl
                        (pkg, ptg), (pku, ptu) = pls
                        w0 = fcw[:, (li * 3 + 0) * FCH + f:(li * 3 + 0) * FCH + f + 1]
                        w1 = fcw[:, (li * 3 + 1) * FCH + f:(li * 3 + 1) * FCH + f + 1]
                        w2 = fcw[:, (li * 3 + 2) * FCH + f:(li * 3 + 2) * FCH + f + 1]
                        i3 = gi[0] % 3
                        gi[0] += 1
                        tk, tt = nexttmp()
                        if bi == 0:
                            gx = gxp[i3]
                            gk = ("gxp", i3)
                            S.op("act", lambda e: e.activation(out=gx[:, 0:2], in_=halo_f[li][:, f, :], func=AF.Copy),
                                 reads=[("halo_f", li)], writes=[gk])
                            S.op("act", lambda e: e.activation(out=gx[:, 2:2 + NTA], in_=ptg[:, 0:NTA], func=AF.Copy),
                                 reads=[pkg, gk], writes=[gk])
                            S.op("act", lambda e: e.activation(out=halo_f[li][:, f, :], in_=gx[:, NTA:NTA + 2], func=AF.Copy),
                                 reads=[gk], writes=[("halo_f", li)])
                            if last:
                                S.op("act", lambda e: e.activation(out=glast[:, f, 0:2], in_=gx[:, NTA:NTA + 2],
                                                                   func=AF.Copy), reads=[gk], writes=["glast"])
                            a0, a1, a2 = gx[:, 0:NTA], gx[:, 1:1 + NTA], gx[:, 2:2 + NTA]
                            to = tt[:, 0:NTA]
                            pu = ptu[:, 0:NTA]
                            ao = aT[:, fl, 0:NTA]
                        else:
                            gx = gxs[i3]
                            gk = ("gxs", i3)
                            S.op("act", lambda e: e.activation(out=gx[:, :, 0:2], in_=gcs[:, f, :, :], func=AF.Copy),
                                 reads=["gcs"], writes=[gk])
                            S.op("act", lambda e: e.activation(
                                out=gx[:, :, 2:6], in_=ptg[:, 0:NTB].rearrange("p (s t) -> p s t", t=4), func=AF.Copy),
                                reads=[pkg, gk], writes=[gk])
                            S.op("act", lambda e: e.activation(
                                out=glast[:, f, 2:34].rearrange("p (s r) -> p s r", r=2), in_=gx[:, :, 4:6], func=AF.Copy),
                                reads=[gk], writes=["glast"])
                            a0, a1, a2 = gx[:, :, 0:4], gx[:, :, 1:5], gx[:, :, 2:6]
                            to = tt[:, 0:NTB].rearrange("p (s t) -> p s t", t=4)
                            pu = ptu[:, 0:NTB].rearrange("p (s t) -> p s t", t=4)
                            ao = aT[:, fl, NTA:NT].rearrange("p (s t) -> p s t", t=4)
                        S.op("dve", lambda e: e.tensor_scalar(out=to, in0=a0, scalar1=w0, scalar2=None, op0=ALU.mult),
                             reads=[gk] + VEC, writes=[tk])
                        S.op("dve", lambda e: e.scalar_tensor_tensor(out=to, in0=a1, scalar=w1, in1=to, op0=ALU.mult,
                                                                     op1=ALU.add), reads=[gk, tk], writes=[tk])
                        S.op("dve", lambda e: e.scalar_tensor_tensor(out=to, in0=a2, scalar=w2, in1=to, op0=ALU.mult,
                                                                     op1=ALU.add), reads=[gk, tk], writes=[tk])
                        S.op("act", lambda e: e.activation(out=to, in_=to, func=AF.Silu), reads=[tk], writes=[tk])
                        S.op("dve", lambda e: e.tensor_tensor(out=ao, in0=pu, in1=to, op=ALU.mult),
                             reads=[tk, pku], writes=[("aT", fl)])
                    return epi

                AT_KEYS = [("aT", f) for f in range(12)]
                def gu_srcs(q0):
                    return [(ffn_w_gate[li], 0, q0 * 128), (ffn_w_up[li], 0, q0 * 128)]

                def dn_srcs(q0):
                    return [(ffn_w_down[li], q0 * 128, 0)]
                for qi, (q0, qn) in enumerate(QS):
                    gemm(p, gu_srcs(q0), DC, qn // 2, lambda kc, b0, bn: hn[:, kc, b0:b0 + bn], HN_KEYS, ffn_epi_factory(q0),
                         tag=("gu", p, li, qi), nxt=(("dn", p, li, qi), dn_srcs(q0), qn))
                    nq = QS[qi + 1] if qi + 1 < len(QS) else None
                    gemm(p, dn_srcs(q0), qn, 8, lambda kc, b0, bn: aT[:, kc, b0:b0 + bn], AT_KEYS, residual_epi,
                         tag=("dn", p, li, qi),
                         nxt=((("gu", p, li, qi + 1), gu_srcs(nq[0]), DC) if nq is not None else None))
                if last:
                    for h in range(3):
                        c0 = h * 2048
                        cn_ = min(2048, FF - c0)
                        xi = h % 2
                        for c4 in range(cn_ // 512):
                            pk, pt = nextps()
                            S.op("pe", [lambda e, cc=cc, pt=pt, c4=c4, h=h: e.transpose(
                                pt[0:34, cc * 128:(cc + 1) * 128], glast[:, h * 16 + 4 * c4 + cc, :], ident[:, :])
                                for cc in range(4)], reads=["glast", "ident"], writes=[pk])
                            S.op("act", lambda e, c4=c4, pt=pt, xi=xi: e.activation(
                                out=xin[xi][0:34, c4 * 512:(c4 + 1) * 512], in_=pt[0:34, :], func=AF.Copy),
                                reads=[pk], writes=[("xin", xi)])
                        S.dma("sp", ("xin", xi), [
                            lambda e, c0=c0, cn_=cn_, xi=xi: e.dma_start(out=ffnp[li][:, c0:c0 + cn_], in_=xin[xi][0:2, 0:cn_]),
                            lambda e, c0=c0, cn_=cn_, xi=xi: e.dma_start(out=ffns[li][:, c0:c0 + cn_], in_=xin[xi][2:34, 0:cn_])],
                            reads=[("xin", xi)])
                S.barrier()


        TWO_PI = 2.0 * np.pi

        def ssm_prep(j):
            with ExitStack() as ls:
                def LT(name, shape, dt=F32):
                    uid[0] += 1
                    return ls.enter_context(nc.sbuf_tensor("%s_%d" % (name, uid[0]), list(shape), dt))
                k = [0]

                def key(n):
                    return ("pp", n)
                Are = LT("Are", [128, 64]); Aim = LT("Aim", [128, 64]); ldt = LT("ldt", [128, 1])
                Bre = LT("Bre", [128, 1024]); Bim = LT("Bim", [128, 1024])
                S.dma("sp", "prep_in", [
                    lambda e: e.dma_start(out=Are[:], in_=ssm_A_re[j]),
                    lambda e: e.dma_start(out=Aim[:], in_=ssm_A_im[j]),
                    lambda e: e.dma_start(out=ldt[:], in_=ssm_log_dt[j].rearrange("(g o) -> g o", o=1)),
                    lambda e: e.dma_start(out=Bre[:], in_=ssm_B_re[j]),
                    lambda e: e.dma_start(out=Bim[:], in_=ssm_B_im[j])], writes=["pin"])
                dtt = LT("dtt", [128, 1]); lr = LT("lr", [128, 64]); li = LT("li", [128, 64])
                mag = LT("mag", [128, 64]); t1 = LT("t1", [128, 64]); t2 = LT("t2", [128, 64])
                ki = LT("ki", [128, 64], mybir.dt.int32)
                cs = [LT("cs0", [128, 64]), LT("cs1", [128, 64])]
                ab = [LT("ab0", [128, 64]), LT("ab1", [128, 64])]
                qq = [LT("q0", [128, 64]), LT("q1", [128, 64])]
                Bb = [LT("Bb0", [128, 1024]), LT("Bb1", [128, 1024])]
                tb = LT("tb", [128, 1024])
                PK = ["pin", "pw"]

                def dv(fn):
                    S.op("dve", fn, reads=PK, writes=["pw"])

                def ac(fn):
                    S.op("act", fn, reads=PK, writes=["pw"])
                ac(lambda e: e.activation(out=dtt[:], in_=ldt[:], func=AF.Exp))
                dv(lambda e: e.tensor_scalar(out=lr[:], in0=Are[:], scalar1=dtt[:, 0:1], scalar2=None, op0=ALU.mult))
                dv(lambda e: e.tensor_scalar(out=li[:], in0=Aim[:], scalar1=dtt[:, 0:1], scalar2=None, op0=ALU.mult))
                ac(lambda e: e.activation(out=mag[:], in_=lr[:], func=AF.Exp))
                for which, shift in ((0, np.pi / 2.0), (1, 0.0)):
                    dv(lambda e, shift=shift: e.tensor_scalar(out=t1[:], in0=li[:], scalar1=shift, scalar2=None, op0=ALU.add))
                    dv(lambda e: e.tensor_scalar(out=t2[:], in0=t1[:], scalar1=1.0 / TWO_PI, scalar2=None, op0=ALU.mult))
                    dv(lambda e: e.tensor_copy(out=ki[:], in_=t2[:]))
                    dv(lambda e: e.tensor_copy(out=t2[:], in_=ki[:]))
                    dv(lambda e: e.scalar_tensor_tensor(out=t1[:], in0=t2[:], scalar=-TWO_PI, in1=t1[:], op0=ALU.mult,
                                                        op1=ALU.add))
                    dv(lambda e: e.tensor_scalar(out=t2[:], in0=t1[:], scalar1=float(np.pi), scalar2=-TWO_PI,
                                                 op0=ALU.is_gt, op1=ALU.mult))
                    dv(lambda e: e.tensor_tensor(out=t1[:], in0=t1[:], in1=t2[:], op=ALU.add))
                    dv(lambda e: e.tensor_scalar(out=t1[:], in0=t1[:], scalar1=float(np.pi), scalar2=-float(np.pi),
                                                 op0=ALU.min, op1=ALU.max))
                    ac(lambda e, which=which: e.activation(out=cs[which][:], in_=t1[:], func=AF.Sin))
                dv(lambda e: e.tensor_tensor(out=ab[0][:], in0=mag[:], in1=cs[0][:], op=ALU.mult))
                dv(lambda e: e.tensor_tensor(out=ab[1][:], in0=mag[:], in1=cs[1][:], op=ALU.mult))
                dv(lambda e: e.tensor_tensor(out=t1[:], in0=Are[:], in1=Are[:], op=ALU.mult))
                dv(lambda e: e.tensor_tensor(out=t2[:], in0=Aim[:], in1=Aim[:], op=ALU.mult))
                dv(lambda e: e.tensor_tensor(out=t1[:], in0=t1[:], in1=t2[:], op=ALU.add))
                dv(lambda e: e.reciprocal(out=t1[:], in_=t1[:]))
                dv(lambda e: e.tensor_scalar(out=mag[:], in0=ab[0][:], scalar1=-1.0, scalar2=None, op0=ALU.add))
                dv(lambda e: e.tensor_tensor(out=qq[0][:], in0=mag[:], in1=Are[:], op=ALU.mult))
                dv(lambda e: e.tensor_tensor(out=t2[:], in0=ab[1][:], in1=Aim[:], op=ALU.mult))
                dv(lambda e: e.tensor_tensor(out=qq[0][:], in0=qq[0][:], in1=t2[:], op=ALU.add))
                dv(lambda e: e.tensor_tensor(out=qq[0][:], in0=qq[0][:], in1=t1[:], op=ALU.mult))
                dv(lambda e: e.tensor_tensor(out=qq[1][:], in0=ab[1][:], in1=Are[:], op=ALU.mult))
                dv(lambda e: e.tensor_tensor(out=t2[:], in0=mag[:], in1=Aim[:], op=ALU.mult))
                dv(lambda e: e.tensor_tensor(out=qq[1][:], in0=qq[1][:], in1=t2[:], op=ALU.subtract))
                dv(lambda e: e.tensor_tensor(out=qq[1][:], in0=qq[1][:], in1=t1[:], op=ALU.mult))

                def bq(t):
                    return t[:, :].unsqueeze(2).broadcast_to([128, 64, 16])

                def v3(t):
                    return t[:, :].rearrange("g (p i) -> g p i", i=16)
                dv(lambda e: e.tensor_tensor(out=v3(Bb[0]), in0=v3(Bre), in1=bq(qq[0]), op=ALU.mult))
                dv(lambda e: e.tensor_tensor(out=v3(tb), in0=v3(Bim), in1=bq(qq[1]), op=ALU.mult))
                dv(lambda e: e.tensor_tensor(out=Bb[0][:], in0=Bb[0][:], in1=tb[:], op=ALU.subtract))
                dv(lambda e: e.tensor_tensor(out=v3(Bb[1]), in0=v3(Bim), in1=bq(qq[0]), op=ALU.mult))
                dv(lambda e: e.tensor_tensor(out=v3(tb), in0=v3(Bre), in1=bq(qq[1]), op=ALU.mult))
                dv(lambda e: e.tensor_tensor(out=Bb[1][:], in0=Bb[1][:], in1=tb[:], op=ALU.add))
                S.dma("sp", "prep_o", [lambda e, r=r: e.dma_start(out=scrA[j, r], in_=ab[r][:]) for r in range(2)] +
                      [lambda e, r=r: e.dma_start(out=scrB[j, r], in_=Bb[r][:]) for r in range(2)],
                      reads=["pw"], writes=["scr1"])
                pidx = LT("pidx", [128, 1], mybir.dt.int32); pi2 = LT("pi2", [128, 1], mybir.dt.int32)
                mpar = [LT("mpar0", [128, 1]), LT("mpar1", [128, 1])]
                mhalf = [LT("mh0", [128, 1]), LT("mh1", [128, 1])]
                nmhalf = [LT("nmh0", [128, 1]), LT("nmh1", [128, 1])]
                S.op("pool", lambda e: e.iota(pidx[:], pattern=[[0, 1]], base=0, channel_multiplier=1), writes=["pidx"])
                pf = LT("pf", [128, 1]); ptmp = LT("ptmp", [128, 1])
                S.op("dve", lambda e: e.tensor_copy(out=pf[:], in_=pidx[:]), reads=["pidx", "pw"], writes=["pw"])
                dv(lambda e: e.tensor_scalar(out=mhalf[1][:], in0=pf[:], scalar1=64.0, scalar2=None, op0=ALU.is_ge))
                dv(lambda e: e.memset(mpar[1][:], 0.0))
                for m_, (thr, sg) in enumerate(((16, 1.0), (32, -1.0), (48, 1.0), (64, -1.0), (80, 1.0), (96, -1.0), (112, 1.0))):
                    dv(lambda e, thr=thr, sg=sg: e.tensor_scalar(out=ptmp[:], in0=pf[:], scalar1=float(thr), scalar2=sg,
                                                                 op0=ALU.is_ge, op1=ALU.mult))
                    dv(lambda e: e.tensor_tensor(out=mpar[1][:], in0=mpar[1][:], in1=ptmp[:], op=ALU.add))
                for mm in (mpar, mhalf):
                    dv(lambda e, mm=mm: e.tensor_scalar(out=mm[0][:], in0=mm[1][:], scalar1=-1.0, scalar2=1.0,
                                                        op0=ALU.mult, op1=ALU.add))
                for a in range(2):
                    dv(lambda e, a=a: e.tensor_scalar(out=nmhalf[a][:], in0=mhalf[a][:], scalar1=-1.0, scalar2=None,
                                                      op0=ALU.mult))
                BbT = [LT("BbT0", [128, DC, 64]), LT("BbT1", [128, DC, 64])]
                Cp = [LT("Cp0", [128, 64, 16]), LT("Cp1", [128, 64, 16])]
                with nc.allow_non_contiguous_dma(reason="ssm weight relayout"):
                    S.dma("sp", "prep_in", [lambda e, r=r: e.dma_start(
                        out=apair[j][r][:], in_=scrA[j, r].rearrange("(q a) p -> (a p) q", a=2)) for r in range(2)],
                        reads=["scr1"], writes=[("apair", j)])
                    fl = []
                    for r in range(2):
                        vB = scrB[j, r].rearrange("(f a) (p i) -> a i f p", a=8, i=16)
                        for a in range(8):
                            for f in range(DC):
                                fl.append(lambda e, r=r, a=a, vB=vB, f=f: e.dma_start(
                                    out=BbT[r][16 * a:16 * a + 16, f, :], in_=vB[a][:, f, :]))
                        vC = (ssm_C_re, ssm_C_im)[r][j].rearrange("(q a) (j p) -> a p q j", a=2, p=64)
                        for a in range(2):
                            for q in range(64):
                                fl.append(lambda e, r=r, a=a, vC=vC, q=q: e.dma_start(
                                    out=Cp[r][64 * a:64 * a + 64, q, :], in_=vC[a][:, q, :]))
                    S.dma("sp", "prep_in", fl, reads=["scr1"], writes=["pin2"])
                Bblk = [LT("Bblk0", [128, DC, 2, 64], BF16), LT("Bblk1", [128, DC, 2, 64], BF16)]
                Cblk = [LT("Cblk0", [128, 64, 2, 16], BF16), LT("Cblk1", [128, 64, 2, 16], BF16)]
                for r in range(2):
                    for a in range(2):
                        S.op("dve", lambda e, r=r, a=a: e.tensor_scalar(
                            out=Bblk[r][:, :, a, :], in0=BbT[r][:, :, :], scalar1=mpar[a][:, 0:1], scalar2=None, op0=ALU.mult),
                            reads=["pin2", "pw"], writes=["pblk"])
                        mm = mhalf if r == 0 else nmhalf
                        S.op("dve", lambda e, r=r, a=a, mm=mm: e.tensor_scalar(
                            out=Cblk[r][:, :, a, :], in0=Cp[r][:, :, :], scalar1=mm[a][:, 0:1], scalar2=None, op0=ALU.mult),
                            reads=["pin2", "pw"], writes=["pblk"])
                S.dma("sp", "prep_o", [lambda e, r=r: e.dma_start(
                    out=scrBb[j, r], in_=Bblk[r][:, :, :, :].rearrange("p f a q -> p (f a q)")) for r in range(2)] +
                    [lambda e, r=r: e.dma_start(
                        out=scrCb[j, r], in_=Cblk[r][:, :, :, :].rearrange("p f a q -> p (f a q)")) for r in range(2)],
                    reads=["pblk"], writes=[("scrblk", j)])
                for r in range(2):
                    S.op("dve", lambda e, r=r: e.memset(carry[j][r][:], 0.0), writes=[("carry", j)])
                S.barrier()

        TS = 32
        GC1 = 2.0 * float(np.sqrt(2.0 / np.pi))

        def ssm_layer(p, li, j):
            blks = blocks_of(p)
            last = (p == npass - 1)
            with ExitStack() as ls:
                def LT(name, shape, dt=F32):
                    uid[0] += 1
                    return ls.enter_context(nc.sbuf_tensor("%s_%d" % (name, uid[0]), list(shape), dt))
                Bblk = [LT("Bblk0", [128, DC, 2, 64], BF16), LT("Bblk1", [128, DC, 2, 64], BF16)]
                Cblk = [LT("Cblk0", [128, 64, 2, 16], BF16), LT("Cblk1", [128, 64, 2, 16], BF16)]
                H = [LT("H0", [128, 64, 40]), LT("H1", [128, 64, 40])]
                Hb = [LT("Hb0", [128, 64, 32], BF16), LT("Hb1", [128, 64, 32], BF16)]
                tm = [LT("tm%d" % i, [128, 512]) for i in range(4)]
                h0s = [LT("h0s%d" % r, [128, 64, NSEQ]) for r in range(2)] if (last and ssm_stage >= 3) else None
                S.dma("sp", "ssm_w", [lambda e, r=r: e.dma_start(
                    out=Bblk[r][:, :, :, :].rearrange("p f a q -> p (f a q)"), in_=scrBb[j, r]) for r in range(2)] +
                    [lambda e, r=r: e.dma_start(
                        out=Cblk[r][:, :, :, :].rearrange("p f a q -> p (f a q)"), in_=scrCb[j, r]) for r in range(2)],
                    reads=[("scrblk", j)], writes=["ssmw"])
                if h0s is not None:
                    for r in range(2):
                        for pc in range(4):
                            xi = pc % 2
                            S.dma("sp", ("xin", xi), lambda e, r=r, pc=pc, xi=xi: e.dma_start(
                                out=xin[xi][0:NSEQ, :], in_=(sre, sim)[r][j][:, pc * 2048:(pc + 1) * 2048]),
                                writes=[("xin", xi)])
                            pk, pt = nextps()
                            S.op("pe", [lambda e, k=k, xi=xi, pt=pt: e.transpose(
                                pt[:, k * NSEQ:(k + 1) * NSEQ], xin[xi][0:NSEQ, k * 128:(k + 1) * 128],
                                ident[0:NSEQ, 0:NSEQ]) for k in range(16)],
                                reads=[("xin", xi), "ident"], writes=[pk])
                            S.op("act", lambda e, r=r, pc=pc, pt=pt: e.activation(
                                out=h0s[r][:, 16 * pc:16 * pc + 16, :],
                                in_=pt[:, 0:16 * NSEQ].rearrange("p (q s) -> p q s", s=NSEQ), func=AF.Copy),
                                reads=[pk], writes=["h0s"])
                rmsnorm(p, g_mix, li * DC)
                are, aim = apair[j]

                def run_chunk(col0, ncols, nseq, tlen, init_fn, final_fn):
                    HW = nseq * (1 + tlen)
                    Hv = [H[r][:, :, 0:HW].rearrange("p q (s t) -> p q s t", t=1 + tlen) for r in range(2)]
                    init_fn(Hv)
                    for r in range(2):
                        for q in range(4):
                            pk, pt = nextps()
                            fns = []
                            for fc in range(DC):
                                fns.append(lambda e, r=r, fc=fc, q=q, pt=pt: e.matmul(
                                    pt[:, fc * ncols:(fc + 1) * ncols],
                                    Bblk[r][32 * q:32 * q + 32, fc, :, :].rearrange("p a q -> p (a q)"),
                                    hn[32 * q:32 * q + 32, fc, col0:col0 + ncols], start=True, stop=True,
                                    tile_position=(32 * q, 0)))
                            S.op("pe", fns, reads=["ssmw"] + HN_KEYS, writes=[pk])
                            S.op("act", lambda e, r=r, q=q, pt=pt: e.activation(
                                out=Hv[r][:, q::4, :, 1:1 + tlen],
                                in_=pt[:, 0:DC * ncols].rearrange("p (f s t) -> p f s t", s=nseq, t=tlen), func=AF.Copy),
                                reads=[pk], writes=["H"])
                    n = 64 * nseq

                    def tv(i):
                        return tm[i][:, 0:n].rearrange("p (q s) -> p q s", s=nseq)

                    def bc(a):
                        return a[:, :].unsqueeze(2).broadcast_to([128, 64, nseq])
                    chain = []
                    for t in range(tlen):
                        hr0, hi0 = Hv[0][:, :, :, t], Hv[1][:, :, :, t]
                        hr1, hi1 = Hv[0][:, :, :, t + 1], Hv[1][:, :, :, t + 1]
                        chain += [
                            lambda e, hr0=hr0: e.tensor_tensor(out=tv(0), in0=hr0, in1=bc(are), op=ALU.mult),
                            lambda e, hi0=hi0: e.tensor_tensor(out=tv(1), in0=hi0, in1=bc(aim), op=ALU.mult),
                            lambda e, hi0=hi0: e.tensor_tensor(out=tv(2), in0=hi0, in1=bc(are), op=ALU.mult),
                            lambda e, hr0=hr0: e.tensor_tensor(out=tv(3), in0=hr0, in1=bc(aim), op=ALU.mult),
                            lambda e: e.tensor_tensor(out=tv(0), in0=tv(0), in1=tv(1), op=ALU.subtract),
                            lambda e: e.tensor_tensor(out=tv(2), in0=tv(2), in1=tv(3), op=ALU.add),
                            lambda e, hr1=hr1: e.tensor_tensor(out=hr1, in0=hr1, in1=tv(0), op=ALU.add),
                            lambda e, hi1=hi1: e.tensor_tensor(out=hi1, in0=hi1, in1=tv(2), op=ALU.add),
                        ]
                    S.op("dve", chain, reads=["H", "tmS", ("apair", j)], writes=["H", "tmS"])
                    final_fn(Hv)
                    if ssm_stage < 2:
                        return
                    for r in range(2):
                        S.op("act", lambda e, r=r: e.activation(
                            out=Hb[r][:, :, 0:ncols].rearrange("p q (s t) -> p q s t", t=tlen),
                            in_=Hv[r][:, :, :, 1:1 + tlen], func=AF.Copy), reads=["H"], writes=["Hb"])
                    pk, pt = nextps()
                    fns = []
                    for fc in range(DC):
                        for q in range(4):
                            pr = fc * 4 + q
                            for r in range(2):
                                fns.append(lambda e, q=q, pr=pr, r=r, fc=fc, pt=pt: e.matmul(
                                    pt[32 * q:32 * q + 32, fc * ncols:(fc + 1) * ncols],
                                    Cblk[r][:, pr, :, :].rearrange("p a q -> p (a q)"), Hb[r][:, pr, 0:ncols],
                                    start=(r == 0), stop=(r == 1), tile_position=(0, 32 * q)))
                    S.op("pe", fns, reads=["ssmw", "Hb"], writes=[pk])
                    W_ = DC * ncols

                    def wv(i):
                        return tm[i][:, 0:W_].rearrange("p (f t) -> p f t", t=ncols)
                    y, z = wv(0), wv(1)
                    Dv = sDv[:, j * DC:(j + 1) * DC].unsqueeze(2).broadcast_to([128, DC, ncols])
                    S.op("dve", [
                        lambda e: e.tensor_tensor(out=y, in0=hn[:, :, col0:col0 + ncols], in1=Dv, op=ALU.mult),
                        lambda e, pt=pt: e.tensor_tensor(out=y, in0=y, in1=pt[:, 0:W_].rearrange("p (f t) -> p f t", t=ncols),
                                                         op=ALU.add),
                        lambda e: e.tensor_tensor(out=z, in0=y, in1=y, op=ALU.mult),
                        lambda e: e.tensor_scalar(out=z, in0=z, scalar1=0.044715, scalar2=1.0, op0=ALU.mult, op1=ALU.add),
                        lambda e: e.tensor_tensor(out=z, in0=z, in1=y, op=ALU.mult),
                    ], reads=[pk, "tmS"] + HN_KEYS + VEC, writes=["tmS"])
                    S.op("act", lambda e: e.activation(out=z, in_=z, func=AF.Sigmoid, scale=GC1), reads=["tmS"], writes=["tmS"])
                    S.op("dve", lambda e: e.tensor_tensor(out=cb[:, :, col0:col0 + ncols], in0=z, in1=y, op=ALU.mult),
                         reads=["tmS"], writes=[("cb", c) for c in range(DC)])

                TSP = 16
                NSUBP = NTA // TSP
                HN2 = lambda b: ("Hp", b)
                HB2 = lambda b: ("Hbp", b)

                def Hcol(r, b, c0, c1=None):
                    return H[r][:, :, 20 * b + c0] if c1 is None else H[r][:, :, 20 * b + c0:20 * b + c1]

                def p_bproj_evac(k):
                    b = k % 2
                    col0 = k * TSP
                    for q in range(4):
                        pk, pt = nextps()
                        fns = []
                        for r in range(2):
                            for fc in range(DC):
                                fns.append(lambda e, r=r, fc=fc, q=q, pt=pt: e.matmul(
                                    pt[:, r * 256 + fc * TSP:r * 256 + (fc + 1) * TSP],
                                    Bblk[r][32 * q:32 * q + 32, fc, :, :].rearrange("p a q -> p (a q)"),
                                    hn[32 * q:32 * q + 32, fc, col0:col0 + TSP], start=True, stop=True,
                                    tile_position=(32 * q, 0)))
                        S.op("pe", fns, reads=["ssmw"] + HN_KEYS, writes=[pk])
                        for r in range(2):
                            S.op("act", lambda e, r=r, q=q, pt=pt, b=b: e.activation(
                                out=H[r][:, q::4, 20 * b + 1:20 * b + 1 + TSP],
                                in_=pt[:, r * 256:(r + 1) * 256].rearrange("p (f t) -> p f t", t=TSP), func=AF.Copy),
                                reads=[pk], writes=[HN2(b)])

                def p_rec(k):
                    b = k % 2
                    t4 = [tm[2][:, 0:64], tm[2][:, 64:128], tm[3][:, 0:64], tm[3][:, 64:128]]
                    chain = []
                    for t in range(TSP):
                        if t == 0:
                            hr0, hi0 = carry[j][0][:, :], carry[j][1][:, :]
                        else:
                            hr0, hi0 = Hcol(0, b, t), Hcol(1, b, t)
                        hr1, hi1 = Hcol(0, b, t + 1), Hcol(1, b, t + 1)
                        chain += [
                            lambda e, hr0=hr0: e.tensor_tensor(out=t4[0], in0=hr0, in1=are[:, :], op=ALU.mult),
                            lambda e, hi0=hi0: e.tensor_tensor(out=t4[1], in0=hi0, in1=aim[:, :], op=ALU.mult),
                            lambda e, hi0=hi0: e.tensor_tensor(out=t4[2], in0=hi0, in1=are[:, :], op=ALU.mult),
                            lambda e, hr0=hr0: e.tensor_tensor(out=t4[3], in0=hr0, in1=aim[:, :], op=ALU.mult),
                            lambda e: e.tensor_tensor(out=t4[0], in0=t4[0], in1=t4[1], op=ALU.subtract),
                            lambda e: e.tensor_tensor(out=t4[2], in0=t4[2], in1=t4[3], op=ALU.add),
                            lambda e, hr1=hr1: e.tensor_tensor(out=hr1, in0=hr1, in1=t4[0], op=ALU.add),
                            lambda e, hi1=hi1: e.tensor_tensor(out=hi1, in0=hi1, in1=t4[2], op=ALU.add),
                        ]
                    chain += [lambda e, r=r, b=b: e.tensor_copy(out=carry[j][r][:, :], in_=Hcol(r, b, TSP)) for r in range(2)]
                    S.op("dve", chain, reads=[HN2(b), "tmR", ("carry", j), ("apair", j)], writes=[HN2(b), "tmR", ("carry", j)])

                def p_hb_cproj(k):
                    b = k % 2
                    for r in range(2):
                        S.op("act", lambda e, r=r, b=b: e.activation(out=Hb[r][:, :, TSP * b:TSP * b + TSP],
                                                                   in_=Hcol(r, b, 1, 1 + TSP), func=AF.Copy),
                             reads=[HN2(b)], writes=[HB2(b)])
                    pk, pt = nextps()
                    fns = []
                    for fc in range(DC):
                        for q in range(4):
                            pr = fc * 4 + q
                            for r in range(2):
                                fns.append(lambda e, q=q, pr=pr, r=r, fc=fc, pt=pt, b=b: e.matmul(
                                    pt[32 * q:32 * q + 32, fc * TSP:(fc + 1) * TSP],
                                    Cblk[r][:, pr, :, :].rearrange("p a q -> p (a q)"), Hb[r][:, pr, TSP * b:TSP * b + TSP],
                                    start=(r == 0), stop=(r == 1), tile_position=(0, 32 * q)))
                    S.op("pe", fns, reads=["ssmw", HB2(b)], writes=[pk])
                    return pk, pt

                def p_tmps(k):
                    W_ = DC * TSP
                    o = (k % 2) * W_
                    y = tm[0][:, o:o + W_].rearrange("p (f t) -> p f t", t=TSP)
                    z = tm[1][:, o:o + W_].rearrange("p (f t) -> p f t", t=TSP)
                    return y, z, ("tmE", k % 2)

                def p_epi1(k, pk, pt):
                    col0 = k * TSP
                    W_ = DC * TSP
                    y, z, tk = p_tmps(k)
                    Dv = sDv[:, j * DC:(j + 1) * DC].unsqueeze(2).broadcast_to([128, DC, TSP])
                    S.op("dve", [
                        lambda e: e.tensor_tensor(out=y, in0=hn[:, :, col0:col0 + TSP], in1=Dv, op=ALU.mult),
                        lambda e: e.tensor_tensor(out=y, in0=y, in1=pt[:, 0:W_].rearrange("p (f t) -> p f t", t=TSP), op=ALU.add),
                        lambda e: e.tensor_tensor(out=z, in0=y, in1=y, op=ALU.mult),
                        lambda e: e.tensor_scalar(out=z, in0=z, scalar1=0.044715, scalar2=1.0, op0=ALU.mult, op1=ALU.add),
                        lambda e: e.tensor_tensor(out=z, in0=z, in1=y, op=ALU.mult),
                    ], reads=[pk, tk] + HN_KEYS + VEC, writes=[tk])
                    S.op("act", lambda e: e.activation(out=z, in_=z, func=AF.Sigmoid, scale=GC1), reads=[tk], writes=[tk])

                def p_epi2(k):
                    col0 = k * TSP
                    y, z, tk = p_tmps(k)
                    S.op("dve", lambda e: e.tensor_tensor(out=cb[:, :, col0:col0 + TSP], in0=z, in1=y, op=ALU.mult),
                         reads=[tk], writes=[("cb", c) for c in range(DC)])

                pend1 = None
                pend2 = None
                p_bproj_evac(0)
                for k in range(NSUBP):
                    if k + 1 < NSUBP:
                        p_bproj_evac(k + 1)
                    p_rec(k)
                    if pend2 is not None:
                        p_epi2(pend2)
                        pend2 = None
                    if pend1 is not None:
                        p_epi1(*pend1)
                        pend2 = pend1[0]
                    pkc, ptc = p_hb_cproj(k)
                    pend1 = (k, pkc, ptc)
                if pend2 is not None:
                    p_epi2(pend2)
                p_epi1(*pend1)
                p_epi2(pend1[0])
                if last and ssm_stage >= 3:
                    with nc.allow_non_contiguous_dma(reason="ssm state out"):
                        S.dma("sp", "st_o", [lambda e, r=r: e.dma_start(
                            out=(rep, imp)[r][j].rearrange("(q a) p -> (a p) q", a=2), in_=carry[j][r][:, :])
                            for r in range(2)], reads=[("carry", j)])
                    S.barrier()
                if last and ssm_stage >= 3:
                    for hb in range(2):
                        def init_s(Hv, hb=hb):
                            for r in range(2):
                                S.op("act", lambda e, r=r: e.activation(out=Hv[r][:, :, :, 0], in_=h0s[r][:, :, 8 * hb:8 * hb + 8],
                                                                        func=AF.Copy), reads=["h0s", "H", "Hb"], writes=["H"])

                        def fin_s(Hv, hb=hb):
                            for r in range(2):
                                S.op("act", lambda e, r=r: e.activation(out=h0s[r][:, :, 8 * hb:8 * hb + 8], in_=Hv[r][:, :, :, 4],
                                                                        func=AF.Copy), reads=["H", "h0s"], writes=[("h0o", hb), "h0s"])
                        run_chunk(NTA + 32 * hb, 32, 8, 4, init_s, fin_s)
                    for r in range(2):
                        for pc in range(4):
                            xi = pc % 2
                            for g4 in range(4):
                                pk, pt = nextps()
                                S.op("pe", [lambda e, k=k, r=r, pc=pc, g4=g4, pt=pt: e.transpose(
                                    pt[0:NSEQ, k * 128:(k + 1) * 128], h0s[r][:, 16 * pc + 4 * g4 + k, :], ident[:, :])
                                    for k in range(4)], reads=["h0s", ("h0o", 0), ("h0o", 1), "ident"], writes=[pk])
                                S.op("act", lambda e, g4=g4, xi=xi, pt=pt: e.activation(
                                    out=xin[xi][0:NSEQ, g4 * 512:(g4 + 1) * 512], in_=pt[0:NSEQ, :], func=AF.Copy),
                                    reads=[pk], writes=[("xin", xi)])
                            S.dma("sp", ("xin", xi), lambda e, r=r, pc=pc, xi=xi: e.dma_start(
                                out=(res_, ims_)[r][j][:, pc * 2048:(pc + 1) * 2048], in_=xin[xi][0:NSEQ, :]),
                                reads=[("xin", xi)])
                if ssm_stage >= 4:
                    W = ssm_w_glu[j]
                    gemm(p, [(W, 0, 0), (W, 0, D)], DC, 8, lambda kc, b0, bn: cb[:, kc, b0:b0 + bn],
                         [("cb", c) for c in range(DC)], glu_residual_epi)
                S.barrier()

        def final_out(p):
            blks = blocks_of(p)
            rmsnorm(p, g_fin, 0, dst_fn=True)
            dsts = [(yp, p * NTA + r * 128, 128, r * 128) for r in range(4)]
            if p == npass - 1:
                dsts.append((ys, 0, NTB, NTA))
            for si, (dst, row0, n, col0) in enumerate(dsts):
                xi = si % 2
                for c4 in range(4):
                    tk, tt = nexttmp()
                    for cc in range(4):
                        c = 4 * c4 + cc
                        S.op("dve", lambda e, c=c, cc=cc, tt=tt, n=n, col0=col0: e.scalar_tensor_tensor(
                            out=tt[:, cc * 128:cc * 128 + n], in0=xT[:, c, col0:col0 + n], scalar=g_fin[:, c:c + 1],
                            in1=rs[:, col0:col0 + n], op0=ALU.mult, op1=ALU.mult),
                            reads=[("xT", c), "rs"] + VEC, writes=[tk])
                    pk, pt = nextps()
                    S.op("pe", [lambda e, cc=cc, n=n, pt=pt, tt=tt: e.transpose(
                        pt[0:n, cc * 128:(cc + 1) * 128], tt[:, cc * 128:cc * 128 + n], ident[:, :])
                        for cc in range(4)], reads=[tk, "ident"], writes=[pk])
                    S.op("act", lambda e, c4=c4, n=n, xi=xi, pt=pt: e.activation(
                        out=xin[xi][0:n, c4 * 512:(c4 + 1) * 512], in_=pt[0:n, :], func=AF.Copy),
                        reads=[pk], writes=[("xin", xi)])
                S.dma("sp", ("xin", xi), lambda e, dst=dst, row0=row0, n=n, xi=xi: e.dma_start(
                    out=dst[row0:row0 + n, :], in_=xin[xi][0:n, :]), reads=[("xin", xi)])

        for li in layers:
            if li % 2 == 1:
                ssm_prep(li // 2)
        for p in range(npass):
            load_x(p)
            for li in layers:
                if li % 2 == 0:
                    conv_layer(p, li, li // 2)
                elif ssm_stage >= 1:
                    ssm_layer(p, li, li // 2)
                ffn_layer(p, li)
                assert not pre_loaded, "prefetched weight tiles must be consumed by the very next gemm"
            final_out(p)
        S.final_wait("sp")
    return nc


_NC_CACHE = {}


def kernel(**inputs):
    layers = tuple(int(c) for c in os.environ.get("K_LAYERS", "0123"))
    stage = int(os.environ.get("K_SSM_STAGE", "4"))
    key = (layers, stage)
    if key not in _NC_CACHE:
        _NC_CACHE[key] = build_nc(layers=layers, ssm_stage=stage)
    nc = _NC_CACHE[key]
    f = lambda a: np.ascontiguousarray(np.asarray(a, dtype=np.float32))
    inp = {k: f(v) for k, v in inputs.items()}
    in_maps = []
    for c in range(NCORES):
        b = c % 4
        sl = slice(NSEQ * c, NSEQ * c + NSEQ)
        m = {
            "xp": inp["x_prompt"][b],
            "xs": f(inp["x_sample"][sl].reshape(NTB, D)),
            "sconv": f(inp["state_conv"][:, sl]),
            "sre": f(inp["state_ssm_re"][:, sl].reshape(2, NSEQ, 8192)),
            "sim": f(inp["state_ssm_im"][:, sl].reshape(2, NSEQ, 8192)),
            "sffn": f(inp["state_ffn"][:, sl].reshape(4, NSEQ * 2, FF)),
            "norm_final": inp["norm_final"].reshape(1, D),
            "ssm_B_re": inp["ssm_B_re"].reshape(2, 128, 1024),
            "ssm_B_im": inp["ssm_B_im"].reshape(2, 128, 1024),
            "ssm_C_re": inp["ssm_C_re"].reshape(2, 128, 1024),
            "ssm_C_im": inp["ssm_C_im"].reshape(2, 128, 1024),
        }
        for k in ("norm_mix", "norm_ffn", "conv_w_in", "conv_dw", "conv_dw_b", "conv_ln_g", "conv_ln_b",
                  "conv_w_out", "ssm_A_re", "ssm_A_im", "ssm_log_dt", "ssm_D", "ssm_w_glu", "ffn_w_gate",
                  "ffn_w_up", "ffn_conv", "ffn_w_down"):
            m[k] = inp[k]
        in_maps.append(m)
    res = run_bass_kernel_spmd(nc, in_maps, core_ids=list(range(NCORES)))
    R = res.results
    y_prompt = np.stack([R[b]["yp"] for b in range(4)])
    y_sample = np.concatenate([R[c]["ys"].reshape(NSEQ, 4, D) for c in range(NCORES)], axis=0)
    conv_p = np.stack([R[b]["convp"] for b in range(4)], axis=1)
    conv_s = np.concatenate([R[c]["convs"] for c in range(NCORES)], axis=1)
    re_p = np.stack([R[b]["rep"] for b in range(4)], axis=1)
    im_p = np.stack([R[b]["imp"] for b in range(4)], axis=1)
    re_s = np.concatenate([R[c]["res"].reshape(2, NSEQ, 128, 64) for c in range(NCORES)], axis=1)
    im_s = np.concatenate([R[c]["ims"].reshape(2, NSEQ, 128, 64) for c in range(NCORES)], axis=1)
    ffn_p = np.stack([R[b]["ffnp"] for b in range(4)], axis=1)
    ffn_s = np.concatenate([R[c]["ffns"].reshape(4, NSEQ, 2, FF) for c in range(NCORES)], axis=1)
    return (y_prompt, y_sample, conv_p, conv_s, re_p, im_p, re_s, im_s, ffn_p, ffn_s)
```
